# Optimizing a Trainium2 kernel written in Bass

```python
import jax, jax.numpy as jnp
from jax import lax
import numpy as np

D_MODEL = 1024
BATCH = 4
SEQ = 8192
DEPTH = 2
DEC_BATCH = 16
DEC_SEQ = 32
PAST_LEN = 4096

CHUNK = 64
N_A = DEPTH // 2
N_B = DEPTH - N_A
SGU_BLOCK = 128
SGU_GROUPS = 4
SGU_HALF = D_MODEL
SGU_GROUP_DIM = SGU_HALF // SGU_GROUPS
D_FF = 2816
CONV_W = 3
N_HEADS = 16
HEAD_DIM = D_MODEL // N_HEADS
Q_BLOCK = 128
EPS = 1e-6

kernel_name = 'yoco_gmlp_fox_streaming_encoder'


def rms_norm(x, g):
    xf = x.astype(jnp.float32)
    y = xf * lax.rsqrt(jnp.mean(xf * xf, axis=-1, keepdims=True) + EPS)
    return (y * g.astype(jnp.float32)).astype(x.dtype)


def layer_norm(x, g, b):
    xf = x.astype(jnp.float32)
    xc = xf - jnp.mean(xf, axis=-1, keepdims=True)
    y = xc * lax.rsqrt(jnp.mean(xc * xc, axis=-1, keepdims=True) + EPS)
    return (y * g.astype(jnp.float32) + b.astype(jnp.float32)).astype(x.dtype)


def sgu_mask(n):
    c = np.arange(n) // CHUNK
    return c[None, :] <= c[:, None]


def sgu_mixer(hn, w_in, ln_g, ln_b, w_s, b_s, w_out):
    bsz, s, _ = hn.shape
    blk = min(SGU_BLOCK, s)
    z = jax.nn.gelu(hn @ w_in)
    u, v = jnp.split(z, 2, axis=-1)
    v = layer_norm(v, ln_g, ln_b)
    mask = sgu_mask(SGU_BLOCK)[:blk, :blk]
    ws = jnp.where(mask[None], w_s[:, :blk, :blk], jnp.zeros((), w_s.dtype))
    vb = v.reshape(bsz, s // blk, blk, SGU_GROUPS, SGU_GROUP_DIM)
    mixed = jnp.einsum('gij,bnjgc->bnigc', ws, vb) + b_s[:, :blk].T[None, None, :, :, None]
    out = u * mixed.reshape(bsz, s, SGU_HALF)
    return out @ w_out, v


def conv_ffn(hn, conv_state, w_up, conv_w, conv_b, w_down):
    a = hn @ w_up
    s = a.shape[1]
    ext = jnp.concatenate([conv_state.astype(a.dtype), a], axis=1)
    c = sum(ext[:, k:k + s] * conv_w[k] for k in range(CONV_W)) + conv_b
    gate, val = jnp.split(c, 2, axis=-1)
    y = (jax.nn.silu(gate) * val) @ w_down
    return y, ext[:, -(CONV_W - 1):]


def shared_kv(h, kv_norm, w_k, w_v, k_norm_g, w_f, b_f):
    bsz, s, _ = h.shape
    hn = rms_norm(h, kv_norm)
    k = rms_norm((hn @ w_k).reshape(bsz, s, N_HEADS, HEAD_DIM), k_norm_g)
    v = (hn @ w_v).reshape(bsz, s, N_HEADS, HEAD_DIM)
    logf = jax.nn.log_sigmoid((hn @ w_f).astype(jnp.float32) + b_f.astype(jnp.float32))
    return k, v, logf


def fox_attention(q, k, v, cq, ck, q_off):
    bsz, sq, _, _ = q.shape
    sk = k.shape[1]
    blk = min(Q_BLOCK, sq)
    nb = sq // blk
    scale = HEAD_DIM ** -0.5
    kpos = jnp.arange(sk, dtype=jnp.int32)
    ck_t = ck.astype(jnp.float32).transpose(0, 2, 1)[:, :, None, :]

    def one_block(args):
        qb, cqb, qpos = args
        logits = jnp.einsum('bqhd,bkhd->bhqk', qb, k, preferred_element_type=jnp.float32) * scale
        logits = logits + cqb.astype(jnp.float32).transpose(0, 2, 1)[..., None] - ck_t
        mask = kpos[None, :] <= qpos[:, None]
        p = jax.nn.softmax(jnp.where(mask, logits, -jnp.inf), axis=-1)
        return jnp.einsum('bhqk,bkhd->bqhd', p.astype(v.dtype), v)

    qs = q.reshape(bsz, nb, blk, N_HEADS, HEAD_DIM).transpose(1, 0, 2, 3, 4)
    cqs = cq.reshape(bsz, nb, blk, N_HEADS).transpose(1, 0, 2, 3)
    qposs = (q_off + jnp.arange(sq, dtype=jnp.int32)).reshape(nb, blk)
    out = lax.map(one_block, (qs, cqs, qposs))
    return out.transpose(1, 0, 2, 3, 4).reshape(bsz, sq, N_HEADS * HEAD_DIM)


def run_group(x, conv_in, past, w):
    bsz, s, _ = x.shape
    q_off = 0 if past is None else past[0].shape[1]
    h = x
    sgu_rows, conv_rows = [], []
    shared = None
    for layer in range(DEPTH):
        hn = rms_norm(h, w['norm_mix'][layer])
        if layer < N_A:
            mix, v_rows = sgu_mixer(hn, w['a_w_in'][layer], w['a_ln_g'][layer], w['a_ln_b'][layer],
                                    w['a_w_s'][layer], w['a_b_s'][layer], w['a_w_out'][layer])
            sgu_rows.append(v_rows)
        else:
            if shared is None:
                k_new, v_new, lf_new = shared_kv(h, w['kv_norm'], w['w_k'], w['w_v'],
                                                 w['k_norm_g'], w['w_f'], w['b_f'])
                if past is None:
                    k_all, v_all, lf_all = k_new, v_new, lf_new
                else:
                    k_all = jnp.concatenate([past[0].astype(k_new.dtype), k_new], axis=1)
                    v_all = jnp.concatenate([past[1].astype(v_new.dtype), v_new], axis=1)
                    lf_all = jnp.concatenate([past[2].astype(jnp.float32), lf_new], axis=1)
                c_all = jnp.cumsum(lf_all.astype(jnp.float32), axis=1)
                shared = (k_all, v_all, c_all)
            j = layer - N_A
            q = rms_norm((hn @ w['b_w_q'][j]).reshape(bsz, s, N_HEADS, HEAD_DIM), w['q_norm_g'][j])
            att = fox_attention(q, shared[0], shared[1], shared[2][:, q_off:], shared[2], q_off)
            mix = att @ w['b_w_o'][j]
        h = h + mix
        y, conv_new = conv_ffn(rms_norm(h, w['norm_ffn'][layer]), conv_in[layer], w['f_w_up'][layer],
                               w['f_conv_w'][layer], w['f_conv_b'][layer], w['f_w_down'][layer])
        conv_rows.append(conv_new)
        h = h + y
    return h, jnp.stack(sgu_rows), jnp.stack(conv_rows), k_new, v_new, lf_new


def setup_inputs(seed: int = 0) -> dict:
    key = jax.random.key(seed)
    ks = jax.random.split(key, 32)

    def nrm(k, shape, scale=1.0):
        return jax.random.normal(k, shape, jnp.float32) * scale

    hd = N_HEADS * HEAD_DIM
    return {
        'x_prompt': nrm(ks[0], (BATCH, SEQ, D_MODEL)),
        'x_sample': nrm(ks[1], (DEC_BATCH, DEC_SEQ, D_MODEL)),
        'cache_k': nrm(ks[2], (DEC_BATCH, PAST_LEN, N_HEADS, HEAD_DIM)),
        'cache_v': nrm(ks[3], (DEC_BATCH, PAST_LEN, N_HEADS, HEAD_DIM)),
        'cache_logf': jax.nn.log_sigmoid(3.0 + nrm(ks[4], (DEC_BATCH, PAST_LEN, N_HEADS))),
        'cache_ffn_conv': nrm(ks[5], (DEPTH, DEC_BATCH, CONV_W - 1, 2 * D_FF)),
        'norm_mix': 1.0 + nrm(ks[6], (DEPTH, D_MODEL), 0.01),
        'norm_ffn': 1.0 + nrm(ks[7], (DEPTH, D_MODEL), 0.01),
        'a_w_in': nrm(ks[8], (N_A, D_MODEL, 2 * SGU_HALF), D_MODEL ** -0.5),
        'a_ln_g': 1.0 + nrm(ks[9], (N_A, SGU_HALF), 0.01),
        'a_ln_b': nrm(ks[10], (N_A, SGU_HALF), 0.01),
        'a_w_s': nrm(ks[11], (N_A, SGU_GROUPS, SGU_BLOCK, SGU_BLOCK), 0.5 * SGU_BLOCK ** -0.5),
        'a_b_s': 1.0 + nrm(ks[12], (N_A, SGU_GROUPS, SGU_BLOCK), 0.01),
        'a_w_out': nrm(ks[13], (N_A, SGU_HALF, D_MODEL), SGU_HALF ** -0.5),
        'f_w_up': nrm(ks[14], (DEPTH, D_MODEL, 2 * D_FF), D_MODEL ** -0.5),
        'f_conv_w': nrm(ks[15], (DEPTH, CONV_W, 2 * D_FF), CONV_W ** -0.5),
        'f_conv_b': nrm(ks[16], (DEPTH, 2 * D_FF), 0.01),
        'f_w_down': nrm(ks[17], (DEPTH, D_FF, D_MODEL), D_FF ** -0.5),
        'kv_norm': 1.0 + nrm(ks[18], (D_MODEL,), 0.01),
        'w_k': nrm(ks[19], (D_MODEL, hd), D_MODEL ** -0.5),
        'w_v': nrm(ks[20], (D_MODEL, hd), D_MODEL ** -0.5),
        'k_norm_g': 1.0 + nrm(ks[21], (HEAD_DIM,), 0.01),
        'w_f': nrm(ks[22], (D_MODEL, N_HEADS), D_MODEL ** -0.5),
        'b_f': jax.random.uniform(ks[23], (N_HEADS,), jnp.float32, 1.0, 5.0),
        'b_w_q': nrm(ks[24], (N_B, D_MODEL, hd), D_MODEL ** -0.5),
        'q_norm_g': 1.0 + nrm(ks[25], (N_B, HEAD_DIM), 0.01),
        'b_w_o': nrm(ks[26], (N_B, hd, D_MODEL), hd ** -0.5),
    }


def reference(x_prompt, x_sample, cache_k, cache_v, cache_logf, cache_ffn_conv,
              norm_mix, norm_ffn, a_w_in, a_ln_g, a_ln_b, a_w_s, a_b_s, a_w_out,
              f_w_up, f_conv_w, f_conv_b, f_w_down,
              kv_norm, w_k, w_v, k_norm_g, w_f, b_f,
              b_w_q, q_norm_g, b_w_o):
    w = {
        'norm_mix': norm_mix, 'norm_ffn': norm_ffn,
        'a_w_in': a_w_in, 'a_ln_g': a_ln_g, 'a_ln_b': a_ln_b, 'a_w_s': a_w_s, 'a_b_s': a_b_s,
        'a_w_out': a_w_out,
        'f_w_up': f_w_up, 'f_conv_w': f_conv_w, 'f_conv_b': f_conv_b, 'f_w_down': f_w_down,
        'kv_norm': kv_norm, 'w_k': w_k, 'w_v': w_v, 'k_norm_g': k_norm_g, 'w_f': w_f, 'b_f': b_f,
        'b_w_q': b_w_q, 'q_norm_g': q_norm_g, 'b_w_o': b_w_o,
    }
    zero_conv = jnp.zeros((DEPTH, x_prompt.shape[0], CONV_W - 1, 2 * D_FF), x_prompt.dtype)
    y_prompt, _, conv_p, k_p, v_p, lf_p = run_group(x_prompt, zero_conv, None, w)
    y_sample, sgu_v_s, conv_s, k_s, v_s, lf_s = run_group(
        x_sample, cache_ffn_conv, (cache_k, cache_v, cache_logf), w)
    return (y_prompt, y_sample, sgu_v_s, conv_p, conv_s, k_p, v_p, lf_p, k_s, v_s, lf_s)
```

```python
import numpy as np
from contextlib import ExitStack
import concourse.bass as bass
import concourse.mybir as mybir
from concourse.bass_utils import run_bass_kernel_spmd

F32 = mybir.dt.float32
BF16 = mybir.dt.bfloat16
ALU = mybir.AluOpType
AF = mybir.ActivationFunctionType
AX = mybir.AxisListType

D = 1024
DFF = 2816
NM = 22
NH = 16
HD = 64
EPS = 1e-6
NEG = -30000.0

EPOCH = 30000
NDMASEM = 12


class Op:
    __slots__ = ("eng", "fn", "dma", "deps", "has_dep", "sem", "val", "prewait")

    def __init__(self, eng, fn, dma):
        self.eng = eng
        self.fn = fn
        self.dma = dma
        self.deps = ()
        self.has_dep = False
        self.sem = None
        self.val = None
        self.prewait = None


class Prog:
    def __init__(self):
        self.ops = []
        self.last_w = {}
        self.readers = {}
        self.dma_ops = []
        self.bar_dma = 0

    def add(self, eng, fn, r=(), w=(), dma=False):
        op = Op(eng, fn, dma)
        deps = set()
        for k in r:
            o = self.last_w.get(k)
            if o is not None:
                deps.add(o)
        for k in w:
            o = self.last_w.get(k)
            if o is not None:
                deps.add(o)
            for o in self.readers.get(k, ()):
                deps.add(o)
        for k in w:
            self.last_w[k] = op
            self.readers[k] = []
        for k in r:
            self.readers.setdefault(k, []).append(op)
        deps.discard(op)
        if eng == "pe" and not dma:
            deps = {d for d in deps if not (d.eng == "pe" and not d.dma)}
        op.deps = deps
        for d in deps:
            d.has_dep = True
        self.ops.append(op)
        if dma:
            self.dma_ops.append(op)
        return op

    def barrier(self):
        last = {}
        for op in self.ops:
            if not op.dma and op.fn is not None:
                last[op.eng] = op
        deps = set(last.values()) | set(self.dma_ops[self.bar_dma:])
        self.bar_dma = len(self.dma_ops)
        for e in ["pe", "act", "dve", "pool", "sp"]:
            op = Op(e, None, False)
            op.deps = set(deps)
            for d in deps:
                d.has_dep = True
            self.ops.append(op)

    def dma(self, eng, out, in_, r=(), w=(), slow=False):
        if slow:
            return self.add(eng, lambda e: e.dma_start(out=out, in_=in_, allow_slow_non_contiguous=True),
                            r=r, w=w, dma=True)
        return self.add(eng, lambda e: e.dma_start(out=out, in_=in_), r=r, w=w, dma=True)

    def emit(self, nc, stack):
        engs = ["pe", "act", "dve", "pool", "sp"]
        per = {e: [] for e in engs}
        for op in self.ops:
            per[op.eng].append(op)
        sems = {}

        def getsem(name):
            if name not in sems:
                sems[name] = stack.enter_context(nc.semaphore(name))
            return sems[name]

        for e in engs:
            cnt = 0
            ndma = 0
            for op in per[e]:
                if op.dma:
                    slot = ndma % NDMASEM
                    rnd = ndma // NDMASEM
                    op.sem = getsem(f"d_{e}_{slot}")
                    op.val = 16 * (rnd + 1)
                    op.prewait = (op.sem, 16 * rnd) if rnd > 0 else None
                    ndma += 1
                elif op.has_dep:
                    ep = cnt // EPOCH
                    op.sem = getsem(f"c_{e}_{ep}")
                    op.val = cnt % EPOCH + 1
                    cnt += 1
        final_waits = {}
        for op in self.dma_ops:
            k = id(op.sem)
            if k not in final_waits or final_waits[k][1] < op.val:
                final_waits[k] = (op.sem, op.val)
        block = stack.enter_context(nc.Block())

        def run(ename, engine):
            seen = {}

            def wait(sem, val):
                k = id(sem)
                if seen.get(k, 0) >= val:
                    return
                seen[k] = val
                engine.wait_ge(sem, val)

            for op in per[ename]:
                for d in op.deps:
                    wait(d.sem, d.val)
                if op.prewait is not None:
                    wait(*op.prewait)
                if op.fn is None:
                    continue
                ins = op.fn(engine)
                if op.dma:
                    ins.then_inc(op.sem, 16)
                elif op.has_dep:
                    ins.then_inc(op.sem, 1)
            if ename == "sp":
                for sem, val in final_waits.values():
                    wait(sem, val)

        @block.tensor
        def _(eng):
            run("pe", eng)

        @block.scalar
        def _(eng):
            run("act", eng)

        @block.vector
        def _(eng):
            run("dve", eng)

        @block.gpsimd
        def _(eng):
            run("pool", eng)

        @block.sync
        def _(eng):
            run("sp", eng)


def build(NTH=8, PAST=4096, dbg=False, skip=()):
    SEQ = 2 * NTH * 512
    NBH = NTH * 4
    NB = 2 * NBH
    NOWN = NTH * 512 + 128
    NPB = PAST // 128
    nc = bass.Bass("TRN2", target_bir_lowering=False)
    P = Prog()

    def din(name, shape, dt=F32):
        return nc.dram_tensor(name, list(shape), dt, kind="ExternalInput").ap()

    def dout(name, shape, dt=F32):
        return nc.dram_tensor(name, list(shape), dt, kind="ExternalOutput").ap()

    def dscr(name, shape, dt):
        return nc.dram_tensor(name, list(shape), dt, kind="Internal").ap()

    xp = din("xp", [SEQ, D])
    xs = din("xs", [64, D])
    cache_k = din("cache_k", [2, PAST, D])
    cache_v = din("cache_v", [2, PAST, D])
    cache_lf = din("cache_lf", [2, PAST, NH])
    cache_conv = din("cache_conv", [2, 2, 2, 2 * DFF])
    cc_in = din("cc", [128, 4])
    norm_mix = din("norm_mix", [2, D])
    norm_ffn = din("norm_ffn", [2, D])
    a_w_in = din("a_w_in", [1, D, 2 * D])
    a_ln_g = din("a_ln_g", [1, D])
    a_ln_b = din("a_ln_b", [1, D])
    a_w_s = din("a_w_s", [1, 4, 128, 128])
    a_b_s = din("a_b_s", [1, 4, 128])
    a_w_out = din("a_w_out", [1, D, D])
    f_w_up = din("f_w_up", [2, D, 2 * DFF])
    f_conv_w = din("f_conv_w", [2, 3, 2 * DFF])
    f_conv_b = din("f_conv_b", [2, 2 * DFF])
    f_w_down = din("f_w_down", [2, DFF, D])
    kv_norm = din("kv_norm", [D])
    w_k = din("w_k", [D, D])
    w_v = din("w_v", [D, D])
    k_norm_g = din("k_norm_g", [HD])
    w_f = din("w_f", [D, NH])
    b_f = din("b_f", [NH])
    b_w_q = din("b_w_q", [1, D, D])
    q_norm_g = din("q_norm_g", [1, HD])
    b_w_o = din("b_w_o", [1, D, D])
    y_p = dout("y_p", [NTH * 512, D])
    y_s = dout("y_s", [64, D])
    sguv_s = dout("sguv_s", [64, D])
    conv_p = dout("conv_p", [2, 2, 2 * DFF])
    conv_s = dout("conv_s", [2, 2, 2, 2 * DFF])
    k_p = dout("k_p", [SEQ, D])
    v_p = dout("v_p", [SEQ, D])
    lf_p = dout("lf_p", [SEQ, NH])
    k_s = dout("k_s", [64, D])
    v_s = dout("v_s", [64, D])
    lf_s = dout("lf_s", [64, NH])
    NPIECE = 8 + 4 + 4 + 4 + 4 + 4 + 2 * 22 + 2 * 11
    WS = dscr("WS", [NPIECE, 128, 2048], BF16)
    h1s = dscr("h1s", [SEQ, D], F32)
    KTs = dscr("KTs", [NH, 66, SEQ], BF16)
    VXs = dscr("VXs", [NH, 128, NB, 65], BF16)
    QTs = dscr("QTs", [NH, 66, NOWN], BF16)
    ATs = dscr("ATs", [NH // 2, 128, NOWN], BF16)

    with ExitStack() as st:
        def sb(name, shape, dt):
            return st.enter_context(nc.sbuf_tensor(name, list(shape), dt))

        def ps(name, shape, dt):
            return st.enter_context(nc.psum_tensor(name, list(shape), dt))

        HNALL = [("hnT", 0), ("hnT", 1), ("hnT", 2), ("hnT", 3)]

        def MM(out, lhsT, rhs, start, stop, r, w):
            P.add("pe", lambda e: e.matmul(out=out, lhsT=lhsT, rhs=rhs, start=start, stop=stop), r=r, w=w)

        def TR(out, in_, ident, r, w):
            P.add("pe", lambda e: e.transpose(out=out, in_=in_, identity=ident), r=r, w=w)

        def ACT(out, in_, func, r, w, scale=None, bias=None, accum=None):
            kw = {}
            if scale is not None:
                kw["scale"] = scale
            if bias is not None:
                kw["bias"] = bias
            if accum is not None:
                kw["accum_out"] = accum
            P.add("act", lambda e: e.activation(out=out, in_=in_, func=func, **kw), r=r, w=w)

        def TT(eng, out, in0, in1, op, r, w):
            P.add(eng, lambda e: e.tensor_tensor(out=out, in0=in0, in1=in1, op=op), r=r, w=w)

        def TS(eng, out, in0, s1, s2, op0, op1, r, w):
            if s2 is None:
                P.add(eng, lambda e: e.tensor_scalar(out=out, in0=in0, scalar1=s1, scalar2=None, op0=op0), r=r, w=w)
            else:
                P.add(eng, lambda e: e.tensor_scalar(out=out, in0=in0, scalar1=s1, scalar2=s2, op0=op0, op1=op1),
                      r=r, w=w)

        def STT(eng, out, in0, scalar, in1, op0, op1, r, w):
            P.add(eng, lambda e: e.scalar_tensor_tensor(out=out, in0=in0, scalar=scalar, in1=in1, op0=op0, op1=op1),
                  r=r, w=w)

        def CP(eng, out, in_, r, w):
            P.add(eng, lambda e: e.tensor_copy(out=out, in_=in_), r=r, w=w)

        def MS(eng, ap, val, w):
            P.add(eng, lambda e: e.memset(ap, val), w=w)

        def RCP(out, in_, r, w):
            P.add("dve", lambda e: e.reciprocal(out=out, in_=in_), r=r, w=w)

        psA = ps("psA", [128, 1024], F32)
        psB = ps("psB", [128, 1024], F32)
        pb = [psA[:, 0:512], psA[:, 512:1024], psB[:, 0:512], psB[:, 512:1024],
              ps("pb4", [128, 512], F32)[:, :], ps("pb5", [128, 512], F32)[:, :]]
        psC_t = ps("psC", [128, 2048], BF16)
        ptb = [psC_t[:, 0:1024], psC_t[:, 1024:2048]]
        psC = psC_t[:, :].bitcast(F32)
        pbk = [("pb", i) for i in range(6)]
        ptk = [("ptb", i) for i in range(2)]

        identf = sb("identf", [128, 128], F32)
        identb = sb("identb", [128, 128], BF16)
        epsc = sb("epsc", [128, 1], F32)
        onec = sb("onec", [128, 1], F32)
        cc = sb("cc_sb", [128, 4], F32)
        Utri = sb("Utri", [128, 128], F32)
        sel127 = sb("sel127", [128, 128], F32)
        ones_f = sb("ones_f", [128, 128], F32)
        gcol = sb("gcol", [128, 5, 8], F32)
        cwT = sb("cwT", [128, 2, 4, 44], F32)
        lng_bc = sb("lng_bc", [128, D], F32)
        lnb_bc = sb("lnb_bc", [128, D], F32)
        bs_bc = sb("bs_bc", [128, 8, 128], F32)
        bs_bc_s = sb("bs_bc_s", [128, 8, 64], F32)
        wsT = sb("wsT", [128, 4, 128], BF16)
        wsT_s = sb("wsT_s", [64, 4, 64], BF16)
        gk_bc = sb("gk_bc", [128, HD], F32)
        gq_bc = sb("gq_bc", [128, HD], F32)
        bf_bc = sb("bf_bc", [128, NH], F32)
        wf_b = sb("wf_b", [128, 8, NH], BF16)
        Af = sb("Af", [128, 5, 512], F32)
        Atri = sb("Atri", [128, 128], F32)
        As64 = sb("As64", [64, NH, 32], F32)
        junk = sb("junk", [128, D], F32)
        vtmp = sb("vtmp", [128, D], F32)
        As2 = junk[0:64, 0:NH * 32].rearrange("p (h q) -> p h q", h=NH)
        ck_all = sb("ck_all", [128, NB, NH], F32)
        run = sb("run", [1, NH], F32)
        small = sb("small", [128, 64], F32)
        stg = sb("stg", [128, 128], F32)
        stg2 = sb("stg2", [128, 128], F32)

        MS("pool", epsc[:], EPS, ["epsc"])
        MS("pool", onec[:], 1.0, ["onec"])
        MS("pool", ones_f[:], 1.0, ["ones_f"])
        MS("pool", identf[:], 0.0, ["identf"])
        P.add("pool", lambda e: e.affine_select(out=identf[:], in_=identf[:], compare_op=ALU.not_equal, fill=1.0, base=0,
                                                pattern=[[-1, 128]], channel_multiplier=1), r=["identf"], w=["identf"])
        CP("pool", identb[:], identf[:], ["identf"], ["identb"])
        MS("pool", Utri[:], 1.0, ["Utri"])
        P.add("pool", lambda e: e.affine_select(out=Utri[:], in_=Utri[:], compare_op=ALU.is_ge, fill=0.0, base=0,
                                                pattern=[[1, 128]], channel_multiplier=-1), r=["Utri"], w=["Utri"])
        MS("pool", sel127[:], 1.0, ["sel127"])
        P.add("pool", lambda e: e.affine_select(out=sel127[:], in_=sel127[:], compare_op=ALU.is_ge, fill=0.0, base=-127,
                                                pattern=[[0, 128]], channel_multiplier=1), r=["sel127"], w=["sel127"])
        P.dma("sp", cc[:], cc_in, w=["cc"])
        P.dma("sp", lng_bc[:], a_ln_g[0].partition_broadcast(128), w=["lng_bc"])
        P.dma("sp", lnb_bc[:], a_ln_b[0].partition_broadcast(128), w=["lnb_bc"])
        P.dma("sp", gk_bc[:], k_norm_g.partition_broadcast(128), w=["gk_bc"])
        P.dma("sp", gq_bc[:], q_norm_g[0].partition_broadcast(128), w=["gq_bc"])
        TS("dve", gq_bc[:], gq_bc[:], 0.125, None, ALU.mult, None, ["gq_bc"], ["gq_bc"])
        P.dma("sp", bf_bc[:], b_f.partition_broadcast(128), w=["bf_bc"])
        for g in range(4):
            for cc_ in range(2):
                P.dma("sp", bs_bc[:, 2 * g + cc_, :], a_b_s[0, g].partition_broadcast(128), w=["bs_bc"])
                for s in range(2):
                    P.dma("sp", bs_bc_s[:, 2 * g + cc_, s * 32:(s + 1) * 32], a_b_s[0, g, 0:32].partition_broadcast(128),
                          w=["bs_bc_s"])
        MS("pool", Af[:], 0.0, ["Af"])
        for j in range(4):
            P.add("pool", lambda e, j=j: e.affine_select(out=Af[:, j, :], in_=Af[:, j, :], compare_op=ALU.is_ge, fill=NEG,
                                                         base=-128 * j, pattern=[[1, 512]], channel_multiplier=-1),
                  r=["Af"], w=["Af"])
        CP("pool", Atri[:], Af[:, 0, 0:128], ["Af"], ["Atri"])
        MS("pool", Af[:, 4, :], NEG, ["Af"])
        TS("pool", Af[:], Af[:], cc[:, 0:1], None, ALU.mult, None, ["Af", "cc"], ["Af"])
        MS("pool", As64[:], 0.0, ["As64"])
        P.add("pool", lambda e: e.affine_select(out=As64[:], in_=As64[:], compare_op=ALU.is_ge, fill=NEG, base=0,
                                                pattern=[[0, NH], [1, 32]], channel_multiplier=-1), r=["As64"], w=["As64"])
        MS("pool", As2, 0.0, ["junk"])
        P.add("pool", lambda e: e.affine_select(out=As2, in_=As2, compare_op=ALU.is_ge, fill=NEG, base=32,
                                                pattern=[[0, NH], [1, 32]], channel_multiplier=-1), r=["junk"], w=["junk"])
        CP("pool", As64[32:64, :, :], As2[32:64, :, :], ["junk", "As64"], ["As64"])

        def rows_to_cols(rows_ap, n, dst, dkey):
            P.dma("sp", stg[:n, :], rows_ap, w=["stg"])
            TR(pb[5][:, 0:n], stg[:n, :], identf[:n, :n], ["stg", "identf"], [pbk[5]])
            ACT(dst, pb[5][:, 0:n], AF.Copy, [pbk[5]], [dkey])

        for n, src in enumerate([norm_mix[0], norm_ffn[0], kv_norm, norm_mix[1], norm_ffn[1]]):
            rows_to_cols(src.rearrange("(k p) -> k p", p=128), 8, gcol[:, n, :], "gcol")
        for l in range(2):
            for t in range(4):
                src = f_conv_w[l, t] if t < 3 else f_conv_b[l]
                rows_to_cols(src.rearrange("(c p) -> c p", p=128), 44, cwT[:, l, t, :], "cwT")
        for g in range(4):
            P.dma("sp", stg[:, :], a_w_s[0, g], w=["stg"])
            TR(pb[5][:, 0:128], stg[:, :], identf[:], ["stg", "identf"], [pbk[5]])
            ACT(stg2[:], pb[5][:, 0:128], AF.Copy, [pbk[5]], ["stg2"])
            MS("pool", stg2[64:128, 0:64], 0.0, ["stg2"])
            CP("dve", wsT[:, g, :], stg2[:], ["stg2"], ["wsT"])
        wss_f = vtmp[0:64, 0:256].rearrange("p (g i) -> p g i", g=4)
        MS("pool", wss_f, 0.0, ["wss_f"])
        for g in range(4):
            for s in range(2):
                P.dma("sp", wss_f[s * 32:(s + 1) * 32, g, s * 32:(s + 1) * 32],
                      a_w_s[0, g, 0:32, 0:32].rearrange("i j -> j i"), r=["wss_f"], w=[("wss_f", g, s)], slow=True)
        CP("dve", wsT_s[:], wss_f, [("wss_f", g, s) for g in range(4) for s in range(2)], ["vtmp", "wsT_s"])

        big = sb("big", [128, 8192], F32)
        hidT_flat = sb("hidT", [128, NM * 512], BF16)
        cvf = [big[:, 4096 + i * 2048:4096 + (i + 1) * 2048] for i in range(2)]
        cvb = [hidT_flat[:, i * 2048:(i + 1) * 2048] for i in range(2)]
        piece_idx = {}
        npc = [0]

        def convert(name, src_ap, shape, nparts=128):
            i = npc[0]
            npc[0] += 1
            piece_idx[name] = i
            b = i % 2
            fv = cvf[b][:nparts, :]
            bv = cvb[b][:nparts, :]
            kf = [("vbuf", 2 * b), ("vbuf", 2 * b + 1)]
            kb = [("hidT", 4 * b + j) for j in range(4)]
            if len(shape) == 2:
                fv = fv.rearrange("p (a b) -> p a b", a=shape[0])
                bv = bv.rearrange("p (a b) -> p a b", a=shape[0])
            elif len(shape) == 3:
                fv = fv.rearrange("p (a b c) -> p a b c", a=shape[0], b=shape[1])
                bv = bv.rearrange("p (a b c) -> p a b c", a=shape[0], b=shape[1])
            P.dma("sp", fv, src_ap, w=kf)
            eng = "dve" if i % 2 == 0 else "act"
            if eng == "act":
                ACT(cvb[b][:nparts, :], cvf[b][:nparts, :], AF.Copy, kf, kb)
            else:
                CP("dve", cvb[b][:nparts, :], cvf[b][:nparts, :], kf, kb)
            P.dma("pool", WS[i, :nparts, :], cvb[b][:nparts, :], r=kb, w=[("WS", i)])

        def w1024(name, W, ncol):
            v = W.rearrange("(k p) n -> p k n", p=128)
            for q in range(ncol // 256):
                convert((name, q), v[:, :, q * 256:(q + 1) * 256], [8, 256])

        w1024("win", a_w_in[0], 2048)
        w1024("wout", a_w_out[0], 1024)
        w1024("wk", w_k, 1024)
        w1024("wv", w_v, 1024)
        w1024("wq", b_w_q[0], 1024)
        w1024("wo", b_w_o[0], 1024)
        for l in range(2):
            upv = f_w_up[l].rearrange("(k p) (gv m j) -> p gv m k j", p=128, gv=2, m=NM, j=128)
            for m in range(NM):
                convert(("up", l, m), upv[:, :, m], [2, 8, 128])
            dnv = f_w_down[l].rearrange("(m p) n -> p m n", p=128)
            for mp in range(NM // 2):
                convert(("dn", l, mp), dnv[:, 2 * mp:2 * mp + 2, :], [2, 1024])
        assert npc[0] == NPIECE
        P.dma("sp", cvf[0][:, 0:128].rearrange("p (k n) -> p k n", k=8), w_f.rearrange("(k p) n -> p k n", p=128),
              w=[("vbuf", 0), ("vbuf", 1)])
        CP("dve", wf_b[:], cvf[0][:, 0:128].rearrange("p (k n) -> p k n", k=8), [("vbuf", 0), ("vbuf", 1)], ["wf_b"])

        NRING = 4
        ring = [sb(f"ring{i}", [128, 2048], BF16) for i in range(NRING)]
        rcnt = [0]

        def wpiece(name, nparts=128):
            s = rcnt[0] % NRING
            rcnt[0] += 1
            i = piece_idx[name]
            P.dma("sp", ring[s][:nparts, :], WS[i, :nparts, :], r=[("WS", i)], w=[("ring", s)])
            return ring[s], ("ring", s)

        h = big[:, 0:4096].rearrange("p (b d) -> p b d", b=4)
        hnT = sb("hnT", [128, 8, 512], BF16)
        uT = sb("uT", [128, 8, 512], BF16)
        vbuf = big[:, 4096:8192].rearrange("p (b d) -> p b d", b=4)
        vnb = sb("vnb", [128, 4, D], BF16)
        hsb = [sb(f"hsb{i}", [128, D], BF16) for i in range(2)]
        hidT = hidT_flat[:, :].rearrange("p (m t) -> p m t", m=NM)
        vraw = hidT_flat[:, 0:8192].bitcast(F32).rearrange("p (b d) -> p b d", b=4)

        def vrk(b):
            return [("hidT", 4 * b + j) for j in range(4)]
        abuf = [sb(f"abuf{i}", [128, 520], F32) for i in range(4)]
        cbuf = [sb(f"cbuf{i}", [128, 512], F32) for i in range(4)]
        sgb = [sb(f"sgb{i}", [128, 512], F32) for i in range(2)]
        carry = [sb(f"carry{l}", [128, 44, 2, 2], F32) for l in range(2)]
        kaug = sb("kaug", [128, NH, 66], BF16)
        ktT = sb("ktT", [66, NH, 512], BF16)
        vxt = sb("vxt", [128, NH, 4, 65], BF16)
        lfsb = sb("lfsb", [128, NH], F32)
        lf4 = sb("lf4", [128, 4 * NH], F32)
        lnst = sb("lnst", [128, 4, 2], F32)
        tot4 = sb("tot4", [1, 4 * NH], F32)
        Rrow = sb("Rrow", [1, 5, NH], F32)
        lft = sb("lft", [128, NH], F32)
        smc = [0]

        def scol(n=1):
            c = smc[0] % (64 // 4) * 4
            smc[0] += 1
            return small[:, c:c + n], ("small", c)

        MS("pool", kaug[:], 1.0, ["kaug"])
        MS("pool", vxt[:], 1.0, ["vxt"])

        nrm_cnt = [0]

        def norm_tile(srcs, rows, nidx, dst, wkey):
            nb_ = len(srcs)
            ssc, ssk = scol(4)
            for b, (src, rkeys) in enumerate(srcs):
                ACT(junk[:rows, :], src, AF.Square, rkeys, ["junk", ssk], accum=ssc[:rows, b:b + 1])
            ACT(ssc[:rows, 0:nb_], ssc[:rows, 0:nb_], AF.Sqrt, [ssk, "epsc"], [ssk], scale=1.0 / D, bias=epsc[:rows, 0:1])
            RCP(ssc[:rows, 0:nb_], ssc[:rows, 0:nb_], [ssk], [ssk])
            for b, (src, rkeys) in enumerate(srcs):
                hb = nrm_cnt[0] % 2
                nrm_cnt[0] += 1
                ACT(hsb[hb][:rows, :], src, AF.Copy, rkeys + [ssk], [("hsb", hb)], scale=ssc[:rows, b:b + 1])
                pt = ptb[hb]
                for k in range(8):
                    TR(pt[:, k * 128:k * 128 + rows], hsb[hb][:rows, k * 128:(k + 1) * 128], identb[:rows, :rows],
                       [("hsb", hb), "identb"], [ptk[hb]])
                TT("dve", dst[:, :, b * rows:(b + 1) * rows], pt[:, :].rearrange("p (k t) -> p k t", k=8)[:, :, 0:rows],
                   gcol[:, nidx, :].unsqueeze(2).to_broadcast([128, 8, rows]), ALU.mult, [ptk[hb], "gcol"], [(wkey, b)])

        def norm_T(src, rows, nidx, dst, col0, rkeys, wkey):
            assert col0 == 0
            norm_tile([(src, rkeys)], rows, nidx, dst, wkey)

        def proj_tok(wname, nq, rows, nblk, evac, extra_r):
            for q in range(nq):
                wt, wkk = wpiece((wname, q))
                wv_ = wt[:, :].rearrange("p (k n) -> p k n", k=8)
                for b in range(nblk):
                    bank = (q * nblk + b) % 4
                    for k in range(8):
                        MM(pb[bank][:rows, 0:256], hnT[:, k, b * rows:(b + 1) * rows], wv_[:, k, :], k == 0, k == 7,
                           [wkk, ("hnT", b)] + extra_r, [pbk[bank]])
                    evac(b, q, pb[bank][:rows, 0:256], pbk[bank])

        def ffn(l, rows, nblk, nseg, first, halo=False):
            ntok = rows * nblk
            seglen = ntok // nseg
            prev_b = [None]
            for m in range(NM):
                wt, wkk = wpiece(("up", l, m))
                wv_ = wt[:, :].rearrange("p (g k j) -> p g k j", g=2, k=8)
                res = []
                for gv in range(2):
                    c = gv * NM + m
                    bank = (2 * m + gv) % 4
                    for k in range(8):
                        MM(pb[bank][:, 0:ntok], wv_[:, gv, k, :], hnT[:, k, 0:ntok], k == 0, k == 7, [wkk] + HNALL,
                           [pbk[bank]])
                    ab = abuf[bank]
                    abv = ab[:, 0:nseg * (seglen + 2)].rearrange("p (s t) -> p s t", s=nseg)
                    pav = pb[bank][:, 0:ntok].rearrange("p (s t) -> p s t", s=nseg)
                    ck_ = ("carry", l, c)
                    CP("dve", abv[:, :, 0:2], carry[l][:, c, 0:nseg, :], [ck_], [("abufc", bank)])
                    ACT(abv[:, :, 2:], pav, AF.Copy, [pbk[bank]], [("abuf", bank)])
                    CP("dve", carry[l][:, c, 0:nseg, :], abv[:, :, seglen:seglen + 2], [("abuf", bank), ("abufc", bank)], [ck_])
                    if halo:
                        continue
                    cb = cbuf[bank]
                    cbv = cb[:, 0:ntok].rearrange("p (s t) -> p s t", s=nseg)
                    ACT(cbv, pav, AF.Identity, [pbk[bank], "cwT"], [("cbuf", bank)], scale=cwT[:, l, 2, c:c + 1],
                        bias=cwT[:, l, 3, c:c + 1])
                    STT("dve", cbv, abv[:, :, 1:seglen + 1], cwT[:, l, 1, c:c + 1], cbv, ALU.mult, ALU.add,
                        [("abuf", bank), ("abufc", bank), ("cbuf", bank), "cwT"], [("cbuf", bank)])
                    STT("dve", cbv, abv[:, :, 0:seglen], cwT[:, l, 0, c:c + 1], cbv, ALU.mult, ALU.add,
                        [("abuf", bank), ("abufc", bank), ("cbuf", bank), "cwT"], [("cbuf", bank)])
                    res.append((cb, ("cbuf", bank)))
                if halo:
                    continue

                def stage_b(m=m, res=res):
                    sg = sgb[m % 2]
                    ACT(sg[:, 0:ntok], res[0][0][:, 0:ntok], AF.Silu, [res[0][1]], [("sgb", m % 2)])
                    TT("pool", hidT[:, m, 0:ntok], sg[:, 0:ntok], res[1][0][:, 0:ntok], ALU.mult,
                       [("sgb", m % 2), res[1][1]], [("hidT", m)])
                if prev_b[0] is not None:
                    prev_b[0]()
                prev_b[0] = stage_b
            if halo:
                return
            prev_b[0]()
            prev_b[0] = None
            npass = 2 if nblk == 4 else 1
            bpp = nblk // npass
            for ps_ in range(npass):
                for mp in range(NM // 2):
                    wt, wkk = wpiece(("dn", l, mp))
                    wv_ = wt[:, :].rearrange("p (a n) -> p a n", a=2)
                    for bb in range(bpp):
                        b = ps_ * bpp + bb
                        for hf in range(2):
                            bank = bb * 2 + hf
                            for mm in range(2):
                                m = 2 * mp + mm
                                MM(pb[bank][:rows, :], hidT[:, m, b * rows:(b + 1) * rows], wv_[:, mm, hf * 512:(hf + 1) * 512],
                                   m == 0, m == NM - 1, [wkk, ("hidT", m)], [pbk[bank]])
                for bb in range(bpp):
                    b = ps_ * bpp + bb
                    for hf in range(2):
                        bank = bb * 2 + hf
                        TT("dve", h[:rows, b, hf * 512:(hf + 1) * 512], h[:rows, b, hf * 512:(hf + 1) * 512],
                           pb[bank][:rows, :], ALU.add, [pbk[bank], ("h", b)], [("h", b)])

        def l0_tile(xsrc, rows, nblk, nseg, tile_idx, sample):
            ntok = rows * nblk
            for b in range(nblk):
                P.dma("sp", h[:rows, b, :], xsrc[b * rows:(b + 1) * rows, :], w=[("h", b)])
            norm_tile([(h[:rows, b, :], [("h", b)]) for b in range(nblk)], rows, 0, hnT, "hnT")
            for q in range(4):
                wt, wkk = wpiece(("win", 4 + q))
                wv_ = wt[:, :].rearrange("p (k n) -> p k n", k=8)
                for b in range(nblk):
                    bank = (q * nblk + b) % 4
                    for k in range(8):
                        MM(pb[bank][:rows, 0:256], hnT[:, k, b * rows:(b + 1) * rows], wv_[:, k, :], k == 0, k == 7,
                           [wkk, ("hnT", b)], [pbk[bank]])
                    ACT(vbuf[:rows, b, q * 256:(q + 1) * 256], pb[bank][:rows, 0:256], AF.Gelu_apprx_tanh, [pbk[bank]],
                        [("vbuf", b)])
            for b in range(nblk):
                st6 = stg[:, b * 12:(b + 1) * 12].rearrange("p (a b) -> p a b", a=2)
                for c2 in range(2):
                    P.add("dve", lambda e, b=b, c2=c2, st6=st6: e.bn_stats(out=st6[:rows, c2, :], in_=vbuf[:rows, b, c2 * 512:(c2 + 1) * 512]),
                          r=[("vbuf", b)], w=[("st6", b)])
                P.add("dve", lambda e, b=b, st6=st6: e.bn_aggr(out=lnst[:rows, b, :], in_=st6[:rows, :, :]), r=[("st6", b)], w=["lnst"])
            ACT(lnst[:rows, 0:nblk, 1], lnst[:rows, 0:nblk, 1], AF.Sqrt, ["lnst", "epsc"], ["lnst"], bias=epsc[:rows, 0:1])
            RCP(lnst[:rows, 0:nblk, 1], lnst[:rows, 0:nblk, 1], ["lnst"], ["lnst"])
            for b in range(nblk):
                TS("dve", vtmp[:rows, :], vbuf[:rows, b, :], lnst[:rows, b, 0:1], lnst[:rows, b, 1:2], ALU.subtract, ALU.mult,
                   [("vbuf", b), "lnst"], ["vtmp"])
                TT("dve", vtmp[:rows, :], vtmp[:rows, :], lng_bc[:rows, :], ALU.mult, ["vtmp", "lng_bc"], ["vtmp"])
                if sample:
                    TT("dve", vtmp[:rows, :], vtmp[:rows, :], lnb_bc[:rows, :], ALU.add, ["vtmp", "lnb_bc"], ["vtmp"])
                    P.dma("pool", sguv_s, vtmp[:rows, :], r=["vtmp"])
                    ACT(vnb[:rows, b, :], vtmp[:rows, :], AF.Copy, ["vtmp"], [("vnb", b)])
                else:
                    TT("dve", vnb[:rows, b, :], vtmp[:rows, :], lnb_bc[:rows, :], ALU.add, ["vtmp", "lnb_bc"], [("vnb", b)])
            for q in range(4):
                wt, wkk = wpiece(("win", q))
                wv_ = wt[:, :].rearrange("p (k n) -> p k n", k=8)
                for cc_ in range(2):
                    c = q * 2 + cc_
                    bank = c % 4
                    for k in range(8):
                        MM(pb[bank][:, 0:ntok], wv_[:, k, cc_ * 128:(cc_ + 1) * 128], hnT[:, k, 0:ntok], k == 0, k == 7,
                           [wkk] + HNALL, [pbk[bank]])
                    ACT(uT[:, c, 0:ntok], pb[bank][:, 0:ntok], AF.Gelu_apprx_tanh, [pbk[bank]], [("uT", c)])
            for b in range(nblk):
                wst = wsT_s if sample else wsT
                bsb = bs_bc_s if sample else bs_bc
                for c in range(8):
                    bank = 4 + c // 4
                    MM(pb[bank][:, (c % 4) * 128:(c % 4) * 128 + rows], vnb[:rows, b, c * 128:(c + 1) * 128],
                       wst[:rows, c // 2, :rows], True, True, [("vnb", b), "wsT", "wsT_s"], [pbk[bank]])
                for hb in range(2):
                    bank = 4 + hb
                    pv_ = pb[bank][:, :].rearrange("p (c t) -> p c t", c=4)[:, :, 0:rows]
                    tv = junk[:, hb * 512:(hb + 1) * 512].rearrange("p (c t) -> p c t", c=4)[:, :, 0:rows]
                    TT("dve", tv, pv_, bsb[:, hb * 4:(hb + 1) * 4, 0:rows], ALU.add, [pbk[bank], "bs_bc", "bs_bc_s"],
                       ["junk"])
                    uv = uT[:, hb * 4:(hb + 1) * 4, b * rows:(b + 1) * rows]
                    TT("dve", uv, tv, uv, ALU.mult, ["junk"] + [("uT", hb * 4 + c) for c in range(4)],
                       [("uT", hb * 4 + c) for c in range(4)])
            def ev_out(b, q, pap, pk):
                TT("dve", h[:rows, b, q * 256:(q + 1) * 256], h[:rows, b, q * 256:(q + 1) * 256], pap, ALU.add,
                   [pk, ("h", b)], [("h", b)])
            for q in range(4):
                wt, wkk = wpiece(("wout", q))
                wv_ = wt[:, :].rearrange("p (k n) -> p k n", k=8)
                for b in range(nblk):
                    bank = (q * nblk + b) % 4
                    for k in range(8):
                        MM(pb[bank][:rows, 0:256], uT[:, k, b * rows:(b + 1) * rows], wv_[:, k, :], k == 0, k == 7,
                           [wkk, ("uT", k)], [pbk[bank]])
                    ev_out(b, q, pb[bank][:rows, 0:256], pbk[bank])
            norm_tile([(h[:rows, b, :], [("h", b)]) for b in range(nblk)], rows, 1, hnT, "hnT")
            ffn(0, rows, nblk, nseg, tile_idx == 0)
            if sample:
                P.dma("pool", h1ss, h[:rows, 0, :], r=[("h", 0)], w=["h1ss"])
            else:
                for b in range(nblk):
                    P.dma("pool", h1s[tile_idx * 512 + b * 128: tile_idx * 512 + (b + 1) * 128, :], h[:rows, b, :],
                          r=[("h", b)], w=[("h1s", tile_idx * 4 + b)])
            norm_tile([(h[:rows, b, :], [("h", b)]) for b in range(nblk)], rows, 2, hnT, "hnT")

            def ev_k(b, q, pap, pk):
                ACT(vbuf[:rows, b, q * 256:(q + 1) * 256], pap, AF.Copy, [pk], [("vbuf", b)])
            proj_tok("wk", 4, rows, nblk, ev_k, [])
            def ev_v(b, q, pap, pk):
                ACT(vraw[:rows, b, q * 256:(q + 1) * 256], pap, AF.Copy, [pk], vrk(b))
            proj_tok("wv", 4, rows, nblk, ev_v, [])
            for b in range(nblk):
                kr = vbuf[:rows, b, :]
                k3 = kr.rearrange("p (h d) -> p h d", d=HD)
                sc, sck = lft, "lft"
                ACT(vtmp[:rows, :], kr, AF.Square, [("vbuf", b)], ["vtmp"])
                P.add("dve", lambda e, sc=sc: e.tensor_reduce(out=sc[:rows, :], in_=vtmp[:rows, :].rearrange("p (h d) -> p h d", d=HD),
                                                        axis=AX.X, op=ALU.add), r=["vtmp"], w=[sck])
                ACT(sc[:rows, :], sc[:rows, :], AF.Sqrt, [sck, "epsc"], [sck], scale=1.0 / HD, bias=epsc[:rows, 0:1])
                RCP(sc[:rows, :], sc[:rows, :], [sck], [sck])
                TT("dve", k3, k3, sc[:rows, :].unsqueeze(2).to_broadcast([rows, NH, HD]), ALU.mult, [("vbuf", b), sck], [("vbuf", b)])
                TT("dve", k3, k3, gk_bc[:rows, :].unsqueeze(1).to_broadcast([rows, NH, HD]), ALU.mult, [("vbuf", b), "gk_bc"],
                   [("vbuf", b)])
                if sample:
                    P.dma("pool", k_s, kr, r=[("vbuf", b)])
                else:
                    r0 = tile_idx * 512 + b * 128
                    P.dma("pool", k_p[r0:r0 + 128, :], kr, r=[("vbuf", b)])
                ACT(kaug[:rows, :, 0:HD], k3, AF.Copy, [("vbuf", b), "kaug"], ["kaug"])
                for hh in range(NH):
                    pt = ptb[hh // 8]
                    TR(pt[:66, (hh % 8) * 128:(hh % 8) * 128 + rows], kaug[:rows, hh, :], identb[:rows, :rows],
                       ["kaug", "identb"], [ptk[hh // 8]])
                for hb in range(2):
                    ACT(ktT[:, hb * 8:(hb + 1) * 8, b * rows:(b + 1) * rows],
                        ptb[hb][:66, :].rearrange("p (k t) -> p k t", k=8)[:, :, 0:rows], AF.Copy, [ptk[hb]], [("ktT", b)])
            if not sample:
                P.dma("pool", KTs.rearrange("h r s -> r h s")[:, :, tile_idx * 512:(tile_idx + 1) * 512], ktT[:, :, :],
                      r=[("ktT", b) for b in range(4)], w=[("KTs", tile_idx)])

            for b in range(nblk):
                if sample:
                    P.dma("pool", v_s, vraw[:rows, b, :], r=vrk(b))
                else:
                    r0 = tile_idx * 512 + b * 128
                    P.dma("pool", v_p[r0:r0 + 128, :], vraw[:rows, b, :], r=vrk(b))
                ACT(vxt[:rows, :, b, 0:HD], vraw[:rows, b, :].rearrange("p (h d) -> p h d", d=HD), AF.Copy,
                    vrk(b) + ["vxt"], ["vxt"])
            if not sample:
                P.dma("pool", VXs.rearrange("h p n e -> p h n e")[:, :, tile_idx * 4:(tile_idx + 1) * 4, :], vxt[:, :, :, :],
                      r=["vxt"], w=[("VXs", tile_idx)])
            for b in range(nblk):
                for k in range(8):
                    MM(pb[4][:rows, b * NH:(b + 1) * NH], hnT[:, k, b * rows:(b + 1) * rows], wf_b[:, k, :], k == 0, k == 7,
                       [("hnT", b), "wf_b"], [pbk[4]])
            nl = nblk * NH
            l3 = lf4[:rows, 0:nl].rearrange("p (b h) -> p b h", h=NH)
            TT("dve", l3, pb[4][:rows, 0:nl].rearrange("p (b h) -> p b h", h=NH),
               bf_bc[:rows, :].unsqueeze(1).to_broadcast([rows, nblk, NH]), ALU.add, [pbk[4], "bf_bc"], ["lf4"])
            ACT(lf4[:rows, 0:nl], lf4[:rows, 0:nl], AF.Exp, ["lf4"], ["lf4"], scale=-1.0)
            ACT(lf4[:rows, 0:nl], lf4[:rows, 0:nl], AF.Ln, ["lf4", "onec"], ["lf4"], bias=onec[:rows, 0:1])
            TS("dve", lf4[:rows, 0:nl], lf4[:rows, 0:nl], -1.0, None, ALU.mult, None, ["lf4"], ["lf4"])
            if sample:
                P.dma("pool", lf_s, lf4[:rows, 0:NH], r=["lf4"])
                CP("dve", lfn[:, :], lf4[:64, 0:NH], ["lf4"], ["lfn"])
            else:
                r0 = tile_idx * 512
                P.dma("pool", lf_p[r0:r0 + 512, :].rearrange("(b p) h -> p b h", p=128), l3, r=["lf4"])
                if tile_idx == 0:
                    MS("dve", run[:], 0.0, ["run"])
                MM(pb[5][0:1, 0:nl], ones_f[:, 0:1], lf4[:, 0:nl], True, True, ["ones_f", "lf4"], [pbk[5]])
                ACT(tot4[0:1, 0:nl], pb[5][0:1, 0:nl], AF.Copy, [pbk[5]], ["tot4"])
                CP("dve", Rrow[0:1, 0, :], run[0:1, :], ["run"], ["Rrow"])
                for b in range(nblk):
                    TT("dve", Rrow[0:1, b + 1, :], Rrow[0:1, b, :], tot4[0:1, b * NH:(b + 1) * NH], ALU.add, ["Rrow", "tot4"], ["Rrow"])
                for b in range(nblk):
                    MM(pb[5][:, 64 + b * NH:64 + (b + 1) * NH], Utri[:, :], lf4[:, b * NH:(b + 1) * NH], True, False,
                       ["Utri", "lf4"], [pbk[5]])
                    MM(pb[5][:, 64 + b * NH:64 + (b + 1) * NH], ones_f[0:1, :], Rrow[0:1, b, :], False, True,
                       ["ones_f", "Rrow"], [pbk[5]])
                ACT(ck_all[:, tile_idx * 4:tile_idx * 4 + 4, :], pb[5][:, 64:64 + nl].rearrange("p (b h) -> p b h", h=NH), AF.Copy,
                    [pbk[5]], [("ck", tile_idx * 4 + b) for b in range(4)])
                CP("dve", run[0:1, :], Rrow[0:1, nblk, :], ["Rrow"], ["run"])

        GQ = 2 if NTH % 2 == 0 else 1
        NG = NTH // GQ
        NPAIR = sum(NBH + 4 * GQ * (g + 1) for g in range(NG)) + NBH
        assert NPAIR * NH <= 8192
        bias_all = big[:, 0:NPAIR * NH].rearrange("p (n h) -> p n h", h=NH)
        cko = sb("cko", [128, NBH + 1, NH], F32)
        Cb_all = sb("Cb_all", [128, NTH + 1, NH], F32)
        hi_f = sb("hi_f", [128, NH], F32)
        h1ss = dscr("h1ss", [64, D], F32)
        ktn = sb("ktn", [66, NH, 64], BF16)
        vxn = sb("vxn", [64, NH, 65], BF16)
        qts = sb("qts", [66, NH, 64], BF16)
        atts = sb("atts", [64, NH, 64], BF16)
        atts_p = sb("atts_p", [128, NH // 2, 64], BF16)
        Shiftm = sb("Shiftm", [64, 128], BF16)
        MS("pool", Shiftm[:], 0.0, ["Shiftm"])
        CP("pool", Shiftm[0:64, 64:128], identb[0:64, 0:64], ["identb", "Shiftm"], ["Shiftm"])
        lfn = sb("lfn", [64, NH], F32)
        assert 2 * NPB * NH <= D
        cks_all = lng_bc[:, 0:2 * NPB * NH].rearrange("p (n h) -> p n h", h=NH)
        bias_s = lnb_bc[:, 0:2 * NPB * NH].rearrange("p (n h) -> p n h", h=NH)
        ckn = sb("ckn", [64, NH], F32)
        runs = sb("runs", [1, 2, NH], F32)
        runf = sb("runf", [1, 2, NH], F32)
        UtriS = sb("UtriS", [64, 64], F32)
        onesAB = sb("onesAB", [1, 2, 64], F32)
        colAB = sb("colAB", [64, 2], F32)
        CP("pool", UtriS[:, :], Utri[0:64, 0:64], ["Utri"], ["UtriS"])
        MS("pool", UtriS[0:32, 32:64], 0.0, ["UtriS"])
        MS("pool", onesAB[:], 0.0, ["onesAB"])
        MS("pool", onesAB[0:1, 0, 0:32], 1.0, ["onesAB"])
        MS("pool", onesAB[0:1, 1, 32:64], 1.0, ["onesAB"])
        MS("pool", colAB[:], 0.0, ["colAB"])
        MS("pool", colAB[0:32, 0:1], 1.0, ["colAB"])
        MS("pool", colAB[32:64, 1:2], 1.0, ["colAB"])
        Cb128s = sb("Cb128s", [128, 2, NH], F32)

        def load_own(i, b, halo):
            if halo:
                ga = NBH - 1
                P.dma("sp", h[:, b, :], h1s[ga * 128:(ga + 1) * 128, :], r=[("h1s", ga)], w=[("h", b)])
                return
            ga = i * 4 + b
            gb_ = NBH + ga
            P.dma("sp", h[:, b, :], h1s[ga * 128:(ga + 1) * 128, :], r=[("h1s", ga)], w=[("h", b)])
            P.dma("sp", vbuf[:, b, :], h1s[gb_ * 128:(gb_ + 1) * 128, :], r=[("h1s", gb_)], w=[("vbuf", b)])
            TS("dve", h[:, b, :], h[:, b, :], cc[:, 0:1], None, ALU.mult, None, [("h", b), "cc"], [("h", b)])
            STT("dve", h[:, b, :], vbuf[:, b, :], cc[:, 1:2], h[:, b, :], ALU.mult, ALU.add, [("vbuf", b), ("h", b), "cc"],
                [("h", b)])

        def qk_norm_aug(rows, b, gbc, src=None, skeys=None):
            kr = vbuf[:rows, b, :] if src is None else src
            sk = [("vbuf", b)] if skeys is None else skeys
            k3 = kr.rearrange("p (h d) -> p h d", d=HD)
            ACT(vtmp[:rows, :], kr, AF.Square, sk, ["vtmp"])
            P.add("dve", lambda e: e.tensor_reduce(out=lft[:rows, :], in_=vtmp[:rows, :].rearrange("p (h d) -> p h d", d=HD),
                                                   axis=AX.X, op=ALU.add), r=["vtmp"], w=["lft"])
            ACT(lft[:rows, :], lft[:rows, :], AF.Sqrt, ["lft", "epsc"], ["lft"], scale=1.0 / HD, bias=epsc[:rows, 0:1])
            RCP(lft[:rows, :], lft[:rows, :], ["lft"], ["lft"])
            v3 = vtmp[:rows, :].rearrange("p (h d) -> p h d", d=HD)
            TT("dve", v3, k3, lft[:rows, :].unsqueeze(2).to_broadcast([rows, NH, HD]), ALU.mult, sk + ["lft"], ["vtmp"])
            TT("dve", v3, v3, gbc[:rows, :].unsqueeze(1).to_broadcast([rows, NH, HD]), ALU.mult, ["vtmp", "gk_bc", "gq_bc"],
               ["vtmp"])
            ACT(kaug[:rows, :, 0:HD], v3, AF.Copy, ["vtmp", "kaug"], ["kaug"])

        def aug_hilo(rows, crel, crk):
            CP("dve", kaug[:rows, :, 64], crel, [crk, "kaug"], ["kaug"])
            CP("dve", hi_f[:rows, :], kaug[:rows, :, 64], ["kaug"], ["hi_f"])
            TT("dve", kaug[:rows, :, 65], crel, hi_f[:rows, :], ALU.subtract, [crk, "hi_f", "kaug"], ["kaug"])

        def aug_transposes(rows, dst, dkey):
            for hh in range(NH):
                pt = ptb[hh // 8]
                TR(pt[:66, (hh % 8) * 128:(hh % 8) * 128 + rows], kaug[:rows, hh, :], identb[:rows, :rows],
                   ["kaug", "identb"], [ptk[hh // 8]])
            for hb in range(2):
                ACT(dst[:, hb * 8:(hb + 1) * 8, :],
                    ptb[hb][:66, :].rearrange("p (k t) -> p k t", k=8)[:, :, 0:rows], AF.Copy, [ptk[hb]], [dkey])

        def ev_q(rows):
            def f(b, q, pap, pk):
                ACT(vbuf[:rows, b, q * 256:(q + 1) * 256], pap, AF.Copy, [pk], [("vbuf", b)])
            return f

        def l1a_tile(i):
            halo = (i == NTH)
            nblk = 1 if halo else 4
            for b in range(nblk):
                load_own(i, b, halo)
            norm_tile([(h[:, b, :], [("h", b)]) for b in range(nblk)], 128, 3, hnT, "hnT")
            def ev_qv(b, q, pap, pk):
                ACT(vraw[:, b, q * 256:(q + 1) * 256], pap, AF.Copy, [pk], vrk(b))
            proj_tok("wq", 4, 128, nblk, ev_qv, [])
            gi_ = NG if halo else i // GQ
            for b in range(nblk):
                ob = i * 4 + b
                qk_norm_aug(128, b, gq_bc, src=vraw[:, b, :], skeys=vrk(b))
                TT("dve", lfsb[:, :], cko[:, ob, :], Cb_all[:, gi_, :], ALU.subtract, [("cko", ob), ("Cb", gi_)], ["lfsb"])
                aug_hilo(128, lfsb[:, :], "lfsb")
                aug_transposes(128, ktT[:, :, b * 128:(b + 1) * 128], ("ktT", b))
            P.dma("pool", QTs.rearrange("h r s -> r h s")[:, :, i * 512:i * 512 + nblk * 128], ktT[:, :, 0:nblk * 128],
                  r=[("ktT", b) for b in range(nblk)], w=[("QTs", i)])

        def cko_prepass():
            for ob in range(NBH):
                TS("dve", cko[:, ob, :], ck_all[:, ob, :], cc[:, 0:1], None, ALU.mult, None, [("ck", ob), "cc"], [("cko", ob)])
                STT("dve", cko[:, ob, :], ck_all[:, NBH + ob, :], cc[:, 1:2], cko[:, ob, :], ALU.mult, ALU.add,
                    [("ck", NBH + ob), ("cko", ob), "cc"], [("cko", ob)])
            CP("dve", cko[:, NBH, :], ck_all[:, NBH - 1, :], [("ck", NBH - 1)], [("cko", NBH)])
            for g in range(NG + 1):
                lastb = NBH if g == NG else 4 * GQ * (g + 1) - 1
                MM(pb[5][:, 0:NH], sel127[:, :], cko[:, lastb, :], True, True, ["sel127", ("cko", lastb)], [pbk[5]])
                ACT(Cb_all[:, g, :], pb[5][:, 0:NH], AF.Copy, [pbk[5]], [("Cb", g)])

        pair_idx = {}

        def build_bias():
            n = 0
            for g in range(NG + 1):
                halo = (g == NG)
                Cb = Cb_all[:, g, :]
                STT("dve", bias_all[:, n:n + NBH, :], ck_all[:, 0:NBH, :], -1.0, Cb.unsqueeze(1).to_broadcast([128, NBH, NH]),
                    ALU.mult, ALU.add, [("ck", j) for j in range(NBH)] + [("Cb", g)], ["bias_all"])
                for j in range(NBH):
                    pair_idx[(g, 0, j)] = n + j
                if not halo:
                    lo = 4 * GQ * (g + 1)
                    if lo < NBH:
                        TS("dve", bias_all[:, n + lo:n + NBH, :], bias_all[:, n + lo:n + NBH, :], cc[:, 2:3], None, ALU.add, None,
                           ["bias_all", "cc"], ["bias_all"])
                n += NBH
                if not halo:
                    ns = 4 * GQ * (g + 1)
                    STT("dve", bias_all[:, n:n + ns, :], ck_all[:, NBH:NBH + ns, :], -1.0,
                        Cb.unsqueeze(1).to_broadcast([128, ns, NH]), ALU.mult, ALU.add,
                        [("ck", NBH + j) for j in range(ns)] + [("Cb", g)], ["bias_all"])
                    TS("dve", bias_all[:, n:n + ns, :], bias_all[:, n:n + ns, :], cc[:, 2:3], None, ALU.add, None,
                       ["bias_all", "cc"], ["bias_all"])
                    for j in range(ns):
                        pair_idx[(g, 1, j)] = n + j
                    n += ns
            assert n == NPAIR

        KT_h = ktT[:, :, :].rearrange("p h t -> p (h t)")
        VX_h = vxt[:, :, :, :].rearrange("p h n e -> p (h n) e")
        QT_h = hidT_flat[0:66, 0:NOWN]
        AT_h = hidT_flat[0:128, 4352:4352 + NOWN]
        att_tmp = hsb[0]
        rr = cbuf[1]
        bcs = cbuf[0]

        LOOK = 2
        FINLAG = 3
        fin_pend = []
        pend = []
        acnt = [0]
        pT4 = hnT[:, :, :].rearrange("p a t -> p (a t)")
        smb = [vtmp, junk]
        bcp = psA[:, 0:512]

        def flush(upto):
            while len(pend) > upto:
                pend.pop(0)()

        def l1b_head(hh):
            P.dma("sp", KT_h[:, 0:SEQ], KTs[hh], r=[("KTs", t) for t in range(2 * NTH)], w=["KT_h"])
            P.dma("sp", VX_h[:, 0:NB, :], VXs[hh], r=[("VXs", t) for t in range(2 * NTH)], w=["VX_h"])
            P.dma("sp", QT_h, QTs[hh], r=[("QTs", i) for i in range(NTH + 1)], w=["QT_h"])
            for g in range(NG + 1):
                halo = (g == NG)
                if halo:
                    nq, q0, ntile, tw = 128, NTH * 512, 1, 128
                    klist = [(0, j) for j in range(NBH)]
                else:
                    nq, q0, ntile, tw = GQ * 512, g * GQ * 512, GQ, 512
                    klist = [(1, j) for j in range(4 * GQ * (g + 1))] + [(0, j) for j in range(NBH)]
                pai = [4 + ((g * GQ + t) % 2) for t in range(ntile)]
                for n, (half, j) in enumerate(klist):
                    kb = half * NBH + j
                    cnt = acnt[0]
                    acnt[0] += 1
                    Sb = (psA, psB, psC)[cnt % 3]
                    Sk = ([pbk[0], pbk[1]], [pbk[2], pbk[3]], [ptk[0], ptk[1]])[cnt % 3]
                    slot = cnt % 4
                    pT = pT4[:, slot * 1024:(slot + 1) * 1024]
                    pTk = ("pT", slot)
                    sm = smb[cnt % 2]
                    smk = ("smb", cnt % 2)
                    bcol = bias_all[:, pair_idx[(g, half, j)], hh:hh + 1]
                    if halo:
                        diag, jp = (j == NBH - 1), 0
                    else:
                        diag = (4 * GQ * g <= j < 4 * GQ * (g + 1))
                        jp = j - 4 * GQ * g
                    c0 = 128 * jp if (diag and half == 1) else 0
                    for t in range(ntile):
                        lo = max(c0 - t * 512, 0)
                        if lo >= tw:
                            continue
                        MM(Sb[:, t * 512 + lo:t * 512 + tw], KT_h[:, kb * 128:(kb + 1) * 128],
                           QT_h[:, q0 + t * 512 + lo:q0 + t * 512 + tw], True, True, ["KT_h", "QT_h"], Sk)
                    if not diag:
                        ACT(pT[:, 0:nq], Sb[:, 0:nq], AF.Exp, Sk + ["bias_all"], [pTk], bias=bcol)
                    elif half == 0 and not halo:
                        kt, jj = jp // 4, jp % 4
                        for t in range(ntile):
                            cs = slice(t * 512, (t + 1) * 512)
                            if t < kt:
                                TT("dve", sm[:, cs], Sb[:, cs], Af[:, 4, :], ALU.add, Sk + ["Af"], [smk])
                            elif t == kt:
                                TT("dve", sm[:, cs], Sb[:, cs], Af[:, jj, :], ALU.add, Sk + ["Af"], [smk])
                            else:
                                CP("dve", sm[:, cs], Sb[:, cs], Sk, [smk])
                        ACT(pT[:, 0:nq], sm[:, 0:nq], AF.Exp, [smk, "bias_all"], [pTk], bias=bcol)
                    else:
                        TT("dve", sm[:, c0:c0 + 128], Sb[:, c0:c0 + 128], Atri[:, :], ALU.add, Sk + ["Atri"], [smk])
                        ACT(pT[:, c0:c0 + 128], sm[:, c0:c0 + 128], AF.Exp, [smk, "bias_all"], [pTk], bias=bcol)
                        if c0 + 128 < nq:
                            ACT(pT[:, c0 + 128:nq], Sb[:, c0 + 128:nq], AF.Exp, Sk + ["bias_all", pTk], [pTk], bias=bcol)

                    def pv(c0=c0, kb=kb, pT=pT, pTk=pTk, first=(n == 0), last=(n == len(klist) - 1), ntile=ntile, tw=tw,
                           pai=pai):
                        for t in range(ntile):
                            lo = max(c0 - t * 512, 0)
                            if lo >= tw:
                                continue
                            MM(pb[pai[t]][0:65, lo:tw], VX_h[:, kb, :], pT[:, t * 512 + lo:t * 512 + tw], first, last,
                               ["VX_h", pTk], [pbk[pai[t]]])
                    pend.append(pv)
                    flush(LOOK)
                    for fp in list(fin_pend):
                        fp[0] -= 1
                        if fp[0] <= 0:
                            fin_pend.remove(fp)
                            fp[1]()

                for t in range(ntile):
                    def fin(pacc=pb[pai[t]], pacc_i=pai[t], nq=tw, q0=q0 + t * 512):
                        RCP(rr[64:65, 0:nq], pacc[64:65, 0:nq], [pbk[pacc_i]], ["rr"])
                        MM(bcp[0:64, 0:nq], ones_f[64:65, 0:64], rr[64:65, 0:nq], True, True, ["ones_f", "rr"], [pbk[0]])
                        CP("dve", bcs[0:64, 0:nq], bcp[0:64, 0:nq], [pbk[0]], ["bcs"])
                        if hh % 2 == 0:
                            TT("dve", AT_h[0:64, q0:q0 + nq], pacc[0:64, 0:nq], bcs[0:64, 0:nq], ALU.mult, [pbk[pacc_i], "bcs"],
                               ["AT_h"])
                        else:
                            TT("dve", att_tmp[0:64, 0:nq], pacc[0:64, 0:nq], bcs[0:64, 0:nq], ALU.mult, [pbk[pacc_i], "bcs"],
                               ["att_tmp"])
                            MM(bcp[:, 0:nq], Shiftm[0:64, :], att_tmp[0:64, 0:nq], True, True, ["Shiftm", "att_tmp"], [pbk[0]])
                            CP("dve", AT_h[64:128, q0:q0 + nq], bcp[64:128, 0:nq], [pbk[0]], ["AT_h"])
                    fin_pend.append([FINLAG, fin])
                if GQ > 1 or halo:
                    flush(0)
                    for fp in list(fin_pend):
                        fin_pend.remove(fp)
                        fp[1]()
            flush(0)
            for fp in list(fin_pend):
                fin_pend.remove(fp)
                fp[1]()
            if hh % 2 == 1:
                P.dma("pool", ATs[hh // 2], AT_h, r=["AT_h"], w=[("ATs", hh // 2)])

        attT = uT

        def oproj(rows, nblk, att_ap):
            for q in range(4):
                wt, wkk = wpiece(("wo", q))
                wv_ = wt[:, :].rearrange("p (k n) -> p k n", k=8)
                for b in range(nblk):
                    bank = (q * nblk + b) % 4
                    for k in range(8):
                        MM(pb[bank][:rows, 0:256], att_ap[:, k, b * rows:(b + 1) * rows], wv_[:, k, :], k == 0, k == 7,
                           [wkk, "attT"], [pbk[bank]])
                    TT("dve", h[:rows, b, q * 256:(q + 1) * 256], h[:rows, b, q * 256:(q + 1) * 256], pb[bank][:rows, 0:256],
                       ALU.add, [pbk[bank], ("h", b)], [("h", b)])

        def l1c_tile(i):
            halo = (i == NTH)
            nblk = 1 if halo else 4
            for b in range(nblk):
                load_own(i, b, halo)
            P.dma("sp", attT[:, :, 0:nblk * 128], ATs.rearrange("k p s -> p k s")[:, :, i * 512:i * 512 + nblk * 128],
                  r=[("ATs", k) for k in range(NH // 2)], w=["attT"])
            oproj(128, nblk, attT)
            norm_tile([(h[:, b, :], [("h", b)]) for b in range(nblk)], 128, 4, hnT, "hnT")
            ffn(1, 128, nblk, 1, False, halo=halo)
            if halo:
                ck1 = [("carry", 1, c) for c in range(44)]
                TS("dve", carry[1][:, :, 0, :], carry[1][:, :, 0, :], cc[:, 3:4], None, ALU.mult, None, ck1 + ["cc"], ck1)
            else:
                for b in range(nblk):
                    r0 = i * 512 + b * 128
                    P.dma("pool", y_p[r0:r0 + 128, :], h[:, b, :], r=[("h", b)])

        def l1a_sample():
            P.dma("sp", h[:64, 0, :], h1ss, r=["h1ss"], w=[("h", 0)])
            norm_T(h[:64, 0, :], 64, 3, hnT, 0, [("h", 0)], "hnT")
            proj_tok("wq", 4, 64, 1, ev_q(64), [])
            qk_norm_aug(64, 0, gq_bc)
            G = min(4, NPB)
            assert NPB % G == 0
            nl = G * NH
            bias_s_flat = lnb_bc
            for s_ in range(2):
                P.dma("sp", bias_s[:, s_ * NPB:(s_ + 1) * NPB, :], cache_lf[s_].rearrange("(j p) h -> p j h", p=128),
                      w=[("bias_s", s_)], slow=True)
                MS("dve", run[:], 0.0, ["run"])
                for g in range(NPB // G):
                    base = (s_ * NPB + g * G) * NH
                    lf2 = bias_s_flat[:, base:base + nl]
                    MM(pb[5][0:1, 0:nl], ones_f[:, 0:1], lf2, True, True, ["ones_f", ("bias_s", s_)], [pbk[5]])
                    ACT(tot4[0:1, 0:nl], pb[5][0:1, 0:nl], AF.Copy, [pbk[5]], ["tot4"])
                    CP("dve", Rrow[0:1, 0, :], run[0:1, :], ["run"], ["Rrow"])
                    for b in range(G):
                        TT("dve", Rrow[0:1, b + 1, :], Rrow[0:1, b, :], tot4[0:1, b * NH:(b + 1) * NH], ALU.add,
                           ["Rrow", "tot4"], ["Rrow"])
                    for b in range(G):
                        MM(pb[5][:, 64 + b * NH:64 + (b + 1) * NH], Utri[:, :], bias_s_flat[:, base + b * NH:base + (b + 1) * NH],
                           True, False, ["Utri", ("bias_s", s_)], [pbk[5]])
                        MM(pb[5][:, 64 + b * NH:64 + (b + 1) * NH], ones_f[0:1, :], Rrow[0:1, b, :], False, True,
                           ["ones_f", "Rrow"], [pbk[5]])
                    gi0 = s_ * NPB + g * G
                    ACT(cks_all[:, gi0:gi0 + G, :], pb[5][:, 64:64 + nl].rearrange("p (b h) -> p b h", h=NH), AF.Copy,
                        [pbk[5]], [("cks", gi0 + b) for b in range(G)])
                    CP("dve", run[0:1, :], Rrow[0:1, G, :], ["Rrow"], ["run"])
                ACT(runs[0:1, s_, :], run[0:1, :], AF.Copy, ["run"], [("runs", s_)])
            rk = [("runs", 0), ("runs", 1)]
            MM(pb[5][0:64, 128:128 + NH], UtriS[:, :], lfn[:, :], True, False, ["UtriS", "lfn"], [pbk[5]])
            MM(pb[5][0:64, 128:128 + NH], onesAB[0:1, 0, :], runs[0:1, 0, :], False, False, ["onesAB"] + rk, [pbk[5]])
            MM(pb[5][0:64, 128:128 + NH], onesAB[0:1, 1, :], runs[0:1, 1, :], False, True, ["onesAB"] + rk, [pbk[5]])
            ACT(ckn[:, :], pb[5][0:64, 128:128 + NH], AF.Copy, [pbk[5]], ["ckn"])
            for s_ in range(2):
                MM(pb[5][0:1, 64:64 + NH], colAB[:, s_:s_ + 1], lfn[:, :], True, False, ["colAB", "lfn"], [pbk[5]])
                MM(pb[5][0:1, 64:64 + NH], ones_f[0:1, 0:1], runs[0:1, s_, :], False, True, ["ones_f"] + rk, [pbk[5]])
                ACT(runf[0:1, s_, :], pb[5][0:1, 64:64 + NH], AF.Copy, [pbk[5]], [("runf", s_)])
                MM(pb[5][:, 192:192 + NH], ones_f[0:1, :], runf[0:1, s_, :], True, True, ["ones_f", ("runf", s_)], [pbk[5]])
                ACT(Cb128s[:, s_, :], pb[5][:, 192:192 + NH], AF.Copy, [pbk[5]], [("Cb128s", s_)])
                STT("dve", bias_s[:, s_ * NPB:(s_ + 1) * NPB, :], cks_all[:, s_ * NPB:(s_ + 1) * NPB, :], -1.0,
                    Cb128s[:, s_, :].unsqueeze(1).to_broadcast([128, NPB, NH]), ALU.mult, ALU.add,
                    [("cks", s_ * NPB + jb) for jb in range(NPB)] + [("Cb128s", s_)], [("bias_s", s_)])
            rf = [("runf", 0), ("runf", 1)]
            MM(pb[5][0:64, 256:256 + NH], onesAB[0:1, 0, :], runf[0:1, 0, :], True, False, ["onesAB"] + rf, [pbk[5]])
            MM(pb[5][0:64, 256:256 + NH], onesAB[0:1, 1, :], runf[0:1, 1, :], False, True, ["onesAB"] + rf, [pbk[5]])
            TT("dve", lft[:64, :], ckn[:64, :], pb[5][0:64, 256:256 + NH], ALU.subtract, ["ckn", pbk[5]], ["lft"])
            aug_hilo(64, lft[:64, :], "lft")
            aug_transposes(64, qts[:, :, :], "qts")
            TS("dve", ckn[:64, :], lft[:64, :], -1.0, None, ALU.mult, None, ["lft", "ckn"], ["bnew"])

        def l1b_sample():
            pacc = pb[4]
            sm = cbuf[2]
            MS("dve", kaug[:, :, 64:66], 1.0, ["kaug"])
            for s_ in range(2):
                q0, q1 = s_ * 32, (s_ + 1) * 32
                for jb in range(NPB):
                    b2 = jb % 2
                    P.dma("sp", h[:, b2, :], cache_k[s_, jb * 128:(jb + 1) * 128, :], w=[("h", b2)])
                    P.dma("sp", vbuf[:, b2, :], cache_v[s_, jb * 128:(jb + 1) * 128, :], w=[("vbuf", b2)])
                    ACT(kaug[:, :, 0:HD], h[:, b2, :].rearrange("p (h d) -> p h d", d=HD), AF.Copy, [("h", b2), "kaug"], ["kaug"])
                    aug_transposes(128, ktT[:, :, 0:128], ("ktT", 0))
                    CP("dve", vxt[:, :, 0, 0:HD], vbuf[:, b2, :].rearrange("p (h d) -> p h d", d=HD), [("vbuf", b2), "vxt"], ["vxt"])
                    sbank = jb % 2
                    for hh in range(NH):
                        MM(pb[sbank][:, hh * 32:(hh + 1) * 32], ktT[:, hh, 0:128], qts[:, hh, q0:q1], True, True,
                           [("ktT", 0), "qts"], [pbk[sbank]])
                    TT("dve", sm[:, :].rearrange("p (h q) -> p h q", h=NH), pb[sbank][:, :].rearrange("p (h q) -> p h q", h=NH),
                       bias_s[:, s_ * NPB + jb, :].unsqueeze(2).to_broadcast([128, NH, 32]), ALU.add,
                       [pbk[sbank], ("bias_s", s_)], ["sm"])
                    slot = jb % 4
                    pT = hnT[:, slot, :]
                    ACT(pT[:, :], sm[:, :], AF.Exp, ["sm"], [("pT", slot)])
                    for hh in range(NH):
                        MM(pacc[0:65, hh * 32:(hh + 1) * 32], vxt[:, hh, 0, :], pT[:, hh * 32:(hh + 1) * 32], jb == 0, False,
                           ["vxt", ("pT", slot)], [pbk[4]])
                for hh in range(NH):
                    MM(pb[2][q0:q1, hh * 32:(hh + 1) * 32], ktn[:, hh, q0:q1], qts[:, hh, q0:q1], True, True, ["ktn", "qts"], [pbk[2]])
                smv = sm[q0:q1, :].rearrange("p (h q) -> p h q", h=NH)
                TT("dve", smv, pb[2][q0:q1, :].rearrange("p (h q) -> p h q", h=NH),
                   ckn[q0:q1, :].unsqueeze(2).to_broadcast([32, NH, 32]), ALU.add, [pbk[2], "bnew"], ["sm"])
                TT("dve", smv, smv, As64[q0:q1, :, :], ALU.add, ["sm", "As64"], ["sm"])
                pT = hnT[:, 4 + s_, :]
                ACT(pT[q0:q1, :], sm[q0:q1, :], AF.Exp, ["sm"], [("pT", 4 + s_)])
                for hh in range(NH):
                    MM(pacc[0:65, hh * 32:(hh + 1) * 32], vxn[q0:q1, hh, :], pT[q0:q1, hh * 32:(hh + 1) * 32], False, True,
                       ["vxn", ("pT", 4 + s_)], [pbk[4]])
                RCP(rr[64:65, :], pacc[64:65, :], [pbk[4]], ["rr"])
                MM(pb[5][0:64, :], ones_f[64:65, 0:64], rr[64:65, :], True, True, ["ones_f", "rr"], [pbk[5]])
                ACT(bcs[0:64, :], pb[5][0:64, :], AF.Copy, [pbk[5]], ["bcs"])
                TT("dve", atts[:, :, q0:q1], pacc[0:64, :].rearrange("p (h q) -> p h q", h=NH),
                   bcs[0:64, :].rearrange("p (h q) -> p h q", h=NH), ALU.mult, [pbk[4], "bcs"], [("atts", s_)])
            a4 = atts[:, :, :].rearrange("p (k two) q -> p k two q", two=2)
            CP("dve", atts_p[0:64, :, :], a4[:, :, 0, :], [("atts", 0), ("atts", 1)], ["atts_p"])
            for k in range(NH // 2):
                MM(pb[5][:, k * 64:(k + 1) * 64], Shiftm[0:64, :], atts[0:64, 2 * k + 1, :], True, True,
                   ["Shiftm", ("atts", 0), ("atts", 1)], [pbk[5]])
            CP("dve", atts_p[64:128, :, :], pb[5][64:128, :].rearrange("p (k q) -> p k q", k=NH // 2), [pbk[5], "atts_p"],
               ["atts_p"])

        def l1c_sample():
            P.dma("sp", h[:64, 0, :], h1ss, r=["h1ss"], w=[("h", 0)])
            for s_ in range(2):
                for rr_ in range(2):
                    P.dma("sp", carry[1][:, :, s_, rr_], cache_conv[1, s_, rr_].rearrange("(c p) -> p c", p=128),
                          w=[("carry", 1, c) for c in range(44)], slow=True)
            for q in range(4):
                wt, wkk = wpiece(("wo", q))
                wv_ = wt[:, :].rearrange("p (k n) -> p k n", k=8)
                for k in range(8):
                    MM(pb[q][:64, 0:256], atts_p[:, k, :], wv_[:, k, :], k == 0, k == 7, [wkk, "atts_p"], [pbk[q]])
                TT("dve", h[:64, 0, q * 256:(q + 1) * 256], h[:64, 0, q * 256:(q + 1) * 256], pb[q][:64, 0:256], ALU.add,
                   [pbk[q], ("h", 0)], [("h", 0)])
            norm_T(h[:64, 0, :], 64, 4, hnT, 0, [("h", 0)], "hnT")
            ffn(1, 64, 1, 2, False)
            P.dma("pool", y_s, h[:64, 0, :], r=[("h", 0)])
            for s_ in range(2):
                for rr_ in range(2):
                    P.dma("pool", conv_s[1, s_, rr_].rearrange("(c p) -> p c", p=128), carry[1][:, :, s_, rr_],
                          r=[("carry", 1, c) for c in range(44)], slow=True)

        for l in range(2):
            MS("dve", carry[l][:], 0.0, [("carry", l, c) for c in range(44)])
        for t in range(2 * NTH):
            l0_tile(xp[t * 512:(t + 1) * 512, :], 128, 4, 1, t, False)
        for rr_ in range(2):
            P.dma("pool", conv_p[0, rr_].rearrange("(c p) -> p c", p=128), carry[0][:, :, 0, rr_],
                  r=[("carry", 0, c) for c in range(44)], slow=True)
        for s in range(2):
            for rr_ in range(2):
                P.dma("sp", carry[0][:, :, s, rr_], cache_conv[0, s, rr_].rearrange("(c p) -> p c", p=128),
                      w=[("carry", 0, c) for c in range(44)], slow=True)
        l0_tile(xs, 64, 1, 2, 0, True)
        CP("dve", ktn[:, :, :], ktT[:, :, 0:64], [("ktT", 0)], ["ktn"])
        CP("dve", vxn[:, :, :], vxt[:64, :, 0, :], ["vxt"], ["vxn"])
        P.barrier()
        for s in range(2):
            for rr_ in range(2):
                P.dma("pool", conv_s[0, s, rr_].rearrange("(c p) -> p c", p=128), carry[0][:, :, s, rr_],
                      r=[("carry", 0, c) for c in range(44)], slow=True)
        cko_prepass()
        for i in range(NTH + 1):
            l1a_tile(i)
        if "s1" not in skip:
            l1a_sample()
        P.barrier()
        build_bias()
        for hh in range(NH):
            l1b_head(hh)
        P.barrier()
        if "s1" not in skip:
            l1b_sample()
        P.barrier()
        l1c_tile(NTH)
        for i in range(NTH):
            l1c_tile(i)
        for rr_ in range(2):
            P.dma("pool", conv_p[1, rr_].rearrange("(c p) -> p c", p=128), carry[1][:, :, 0, rr_],
                  r=[("carry", 1, c) for c in range(44)], slow=True)
        if "s1" not in skip:
            l1c_sample()

        P.emit(nc, st)
    return nc


WNAMES = ['norm_mix', 'norm_ffn', 'a_w_in', 'a_ln_g', 'a_ln_b', 'a_w_s', 'a_b_s', 'a_w_out', 'f_w_up', 'f_conv_w',
          'f_conv_b', 'f_w_down', 'kv_norm', 'w_k', 'w_v', 'k_norm_g', 'w_f', 'b_f', 'b_w_q', 'q_norm_g', 'b_w_o']


def make_in_maps(inp, NTH=8, PAST=4096):
    SEQ = 2 * NTH * 512
    f32 = lambda a: np.ascontiguousarray(np.asarray(a, dtype=np.float32))
    wts = {k: f32(inp[k]) for k in WNAMES}
    maps = []
    for c in range(8):
        b, r = c // 2, c % 2
        m = dict(wts)
        m["xp"] = f32(inp["x_prompt"][b, :SEQ])
        m["xs"] = f32(inp["x_sample"][2 * c:2 * c + 2]).reshape(64, D)
        m["cache_k"] = f32(inp["cache_k"][2 * c:2 * c + 2, :PAST]).reshape(2, PAST, D)
        m["cache_v"] = f32(inp["cache_v"][2 * c:2 * c + 2, :PAST]).reshape(2, PAST, D)
        m["cache_lf"] = f32(inp["cache_logf"][2 * c:2 * c + 2, :PAST])
        m["cache_conv"] = f32(inp["cache_ffn_conv"][:, 2 * c:2 * c + 2])
        ccv = np.zeros((128, 4), np.float32)
        ccv[:, 0] = 1.0 - r
        ccv[:, 1] = float(r)
        ccv[:, 2] = NEG * (1.0 - r)
        ccv[:, 3] = float(r)
        m["cc"] = ccv
        maps.append(m)
    return maps


_NC_CACHE = {}


def kernel(**inputs):
    if "nc" not in _NC_CACHE:
        _NC_CACHE["nc"] = build()
    nc = _NC_CACHE["nc"]
    in_maps = make_in_maps(inputs)
    res = run_bass_kernel_spmd(nc, in_maps, core_ids=list(range(8))).results
    B, S, H = 4, 8192, 4096
    y_prompt = np.zeros((B, S, D), np.float32)
    conv_p = np.zeros((2, B, 2, 2 * DFF), np.float32)
    k_p = np.zeros((B, S, D), np.float32)
    v_p = np.zeros((B, S, D), np.float32)
    lf_p = np.zeros((B, S, NH), np.float32)
    y_s = np.zeros((16, 32, D), np.float32)
    sgu = np.zeros((1, 16, 32, D), np.float32)
    conv_s = np.zeros((2, 16, 2, 2 * DFF), np.float32)
    k_s = np.zeros((16, 32, D), np.float32)
    v_s = np.zeros((16, 32, D), np.float32)
    lf_s = np.zeros((16, 32, NH), np.float32)
    for c in range(8):
        b, r = c // 2, c % 2
        o = res[c]
        y_prompt[b, r * H:(r + 1) * H] = o["y_p"]
        if r == 1:
            conv_p[0, b] = o["conv_p"][0]
            conv_p[1, b] = o["conv_p"][1]
            k_p[b] = o["k_p"]
            v_p[b] = o["v_p"]
            lf_p[b] = o["lf_p"]
        y_s[2 * c:2 * c + 2] = o["y_s"].reshape(2, 32, D)
        sgu[0, 2 * c:2 * c + 2] = o["sguv_s"].reshape(2, 32, D)
        conv_s[:, 2 * c:2 * c + 2] = o["conv_s"]
        k_s[2 * c:2 * c + 2] = o["k_s"].reshape(2, 32, D)
        v_s[2 * c:2 * c + 2] = o["v_s"].reshape(2, 32, D)
        lf_s[2 * c:2 * c + 2] = o["lf_s"].reshape(2, 32, NH)
    return (y_prompt, y_s, sgu, conv_p, conv_s,
            k_p.reshape(B, S, NH, HD), v_p.reshape(B, S, NH, HD), lf_p,
            k_s.reshape(16, 32, NH, HD), v_s.reshape(16, 32, NH, HD), lf_s)
```

```python
import numpy as np
from contextlib import ExitStack
import concourse.bass as bass
import concourse.mybir as mybir
from concourse.bass_utils import run_bass_kernel_spmd

F32 = mybir.dt.float32
BF16 = mybir.dt.bfloat16
ALU = mybir.AluOpType
AF = mybir.ActivationFunctionType
AX = mybir.AxisListType

D = 1024
DFF = 2816
NM = 22
NH = 16
HD = 64
EPS = 1e-6
NEG = -30000.0

EPOCH = 30000
NDMASEM = 12


class Op:
    __slots__ = ("eng", "fn", "dma", "deps", "has_dep", "sem", "val", "prewait")

    def __init__(self, eng, fn, dma):
        self.eng = eng
        self.fn = fn
        self.dma = dma
        self.deps = ()
        self.has_dep = False
        self.sem = None
        self.val = None
        self.prewait = None


class Prog:
    def __init__(self):
        self.ops = []
        self.last_w = {}
        self.readers = {}
        self.dma_ops = []
        self.bar_dma = 0

    def add(self, eng, fn, r=(), w=(), dma=False):
        op = Op(eng, fn, dma)
        deps = set()
        for k in r:
            o = self.last_w.get(k)
            if o is not None:
                deps.add(o)
        for k in w:
            o = self.last_w.get(k)
            if o is not None:
                deps.add(o)
            for o in self.readers.get(k, ()):
                deps.add(o)
        for k in w:
            self.last_w[k] = op
            self.readers[k] = []
        for k in r:
            self.readers.setdefault(k, []).append(op)
        deps.discard(op)
        if eng == "pe" and not dma:
            deps = {d for d in deps if not (d.eng == "pe" and not d.dma)}
        op.deps = deps
        for d in deps:
            d.has_dep = True
        self.ops.append(op)
        if dma:
            self.dma_ops.append(op)
        return op

    def barrier(self):
        last = {}
        for op in self.ops:
            if not op.dma and op.fn is not None:
                last[op.eng] = op
        deps = set(last.values()) | set(self.dma_ops[self.bar_dma:])
        self.bar_dma = len(self.dma_ops)
        for e in ["pe", "act", "dve", "pool", "sp"]:
            op = Op(e, None, False)
            op.deps = set(deps)
            for d in deps:
                d.has_dep = True
            self.ops.append(op)

    def dma(self, eng, out, in_, r=(), w=(), slow=False):
        if slow:
            return self.add(eng, lambda e: e.dma_start(out=out, in_=in_, allow_slow_non_contiguous=True),
                            r=r, w=w, dma=True)
        return self.add(eng, lambda e: e.dma_start(out=out, in_=in_), r=r, w=w, dma=True)

    def emit(self, nc, stack):
        engs = ["pe", "act", "dve", "pool", "sp"]
        per = {e: [] for e in engs}
        for op in self.ops:
            per[op.eng].append(op)
        sems = {}

        def getsem(name):
            if name not in sems:
                sems[name] = stack.enter_context(nc.semaphore(name))
            return sems[name]

        for e in engs:
            cnt = 0
            ndma = 0
            for op in per[e]:
                if op.dma:
                    slot = ndma % NDMASEM
                    rnd = ndma // NDMASEM
                    op.sem = getsem(f"d_{e}_{slot}")
                    op.val = 16 * (rnd + 1)
                    op.prewait = (op.sem, 16 * rnd) if rnd > 0 else None
                    ndma += 1
                elif op.has_dep:
                    ep = cnt // EPOCH
                    op.sem = getsem(f"c_{e}_{ep}")
                    op.val = cnt % EPOCH + 1
                    cnt += 1
        final_waits = {}
        for op in self.dma_ops:
            k = id(op.sem)
            if k not in final_waits or final_waits[k][1] < op.val:
                final_waits[k] = (op.sem, op.val)
        block = stack.enter_context(nc.Block())

        def run(ename, engine):
            seen = {}

            def wait(sem, val):
                k = id(sem)
                if seen.get(k, 0) >= val:
                    return
                seen[k] = val
                engine.wait_ge(sem, val)

            for op in per[ename]:
                for d in op.deps:
                    wait(d.sem, d.val)
                if op.prewait is not None:
                    wait(*op.prewait)
                if op.fn is None:
                    continue
                ins = op.fn(engine)
                if op.dma:
                    ins.then_inc(op.sem, 16)
                elif op.has_dep:
                    ins.then_inc(op.sem, 1)
            if ename == "sp":
                for sem, val in final_waits.values():
                    wait(sem, val)

        @block.tensor
        def _(eng):
            run("pe", eng)

        @block.scalar
        def _(eng):
            run("act", eng)

        @block.vector
        def _(eng):
            run("dve", eng)

        @block.gpsimd
        def _(eng):
            run("pool", eng)

        @block.sync
        def _(eng):
            run("sp", eng)


def build(NTH=8, PAST=4096, dbg=False, skip=()):
    SEQ = 2 * NTH * 512
    NBH = NTH * 4
    NB = 2 * NBH
    NOWN = NTH * 512 + 128
    NPB = PAST // 128
    nc = bass.Bass("TRN2", target_bir_lowering=False)
    P = Prog()

    def din(name, shape, dt=F32):
        return nc.dram_tensor(name, list(shape), dt, kind="ExternalInput").ap()

    def dout(name, shape, dt=F32):
        return nc.dram_tensor(name, list(shape), dt, kind="ExternalOutput").ap()

    def dscr(name, shape, dt):
        return nc.dram_tensor(name, list(shape), dt, kind="Internal").ap()

    xp = din("xp", [SEQ, D])
    xs = din("xs", [64, D])
    cache_k = din("cache_k", [2, PAST, D])
    cache_v = din("cache_v", [2, PAST, D])
    cache_lf = din("cache_lf", [2, PAST, NH])
    cache_conv = din("cache_conv", [2, 2, 2, 2 * DFF])
    cc_in = din("cc", [128, 4])
    norm_mix = din("norm_mix", [2, D])
    norm_ffn = din("norm_ffn", [2, D])
    a_w_in = din("a_w_in", [1, D, 2 * D])
    a_ln_g = din("a_ln_g", [1, D])
    a_ln_b = din("a_ln_b", [1, D])
    a_w_s = din("a_w_s", [1, 4, 128, 128])
    a_b_s = din("a_b_s", [1, 4, 128])
    a_w_out = din("a_w_out", [1, D, D])
    f_w_up = din("f_w_up", [2, D, 2 * DFF])
    f_conv_w = din("f_conv_w", [2, 3, 2 * DFF])
    f_conv_b = din("f_conv_b", [2, 2 * DFF])
    f_w_down = din("f_w_down", [2, DFF, D])
    kv_norm = din("kv_norm", [D])
    w_k = din("w_k", [D, D])
    w_v = din("w_v", [D, D])
    k_norm_g = din("k_norm_g", [HD])
    w_f = din("w_f", [D, NH])
    b_f = din("b_f", [NH])
    b_w_q = din("b_w_q", [1, D, D])
    q_norm_g = din("q_norm_g", [1, HD])
    b_w_o = din("b_w_o", [1, D, D])
    y_p = dout("y_p", [NTH * 512, D])
    y_s = dout("y_s", [64, D])
    sguv_s = dout("sguv_s", [64, D])
    conv_p = dout("conv_p", [2, 2, 2 * DFF])
    conv_s = dout("conv_s", [2, 2, 2, 2 * DFF])
    k_p = dout("k_p", [SEQ, D])
    v_p = dout("v_p", [SEQ, D])
    lf_p = dout("lf_p", [SEQ, NH])
    k_s = dout("k_s", [64, D])
    v_s = dout("v_s", [64, D])
    lf_s = dout("lf_s", [64, NH])
    NPIECE = 8 + 4 + 4 + 4 + 4 + 4 + 2 * 22 + 2 * 11
    WS = dscr("WS", [NPIECE, 128, 2048], BF16)
    h1s = dscr("h1s", [SEQ, D], F32)
    KTs = dscr("KTs", [NH, 66, SEQ], BF16)
    VXs = dscr("VXs", [NH, 128, NB, 65], BF16)
    QTs = dscr("QTs", [NH, 66, NOWN], BF16)
    ATs = dscr("ATs", [NH // 2, 128, NOWN], BF16)

    with ExitStack() as st:
        def sb(name, shape, dt):
            return st.enter_context(nc.sbuf_tensor(name, list(shape), dt))

        def ps(name, shape, dt):
            return st.enter_context(nc.psum_tensor(name, list(shape), dt))

        HNALL = [("hnT", 0), ("hnT", 1), ("hnT", 2), ("hnT", 3)]

        def MM(out, lhsT, rhs, start, stop, r, w):
            P.add("pe", lambda e: e.matmul(out=out, lhsT=lhsT, rhs=rhs, start=start, stop=stop), r=r, w=w)

        def TR(out, in_, ident, r, w):
            P.add("pe", lambda e: e.transpose(out=out, in_=in_, identity=ident), r=r, w=w)

        def ACT(out, in_, func, r, w, scale=None, bias=None, accum=None):
            kw = {}
            if scale is not None:
                kw["scale"] = scale
            if bias is not None:
                kw["bias"] = bias
            if accum is not None:
                kw["accum_out"] = accum
            P.add("act", lambda e: e.activation(out=out, in_=in_, func=func, **kw), r=r, w=w)

        def TT(eng, out, in0, in1, op, r, w):
            P.add(eng, lambda e: e.tensor_tensor(out=out, in0=in0, in1=in1, op=op), r=r, w=w)

        def TS(eng, out, in0, s1, s2, op0, op1, r, w):
            if s2 is None:
                P.add(eng, lambda e: e.tensor_scalar(out=out, in0=in0, scalar1=s1, scalar2=None, op0=op0), r=r, w=w)
            else:
                P.add(eng, lambda e: e.tensor_scalar(out=out, in0=in0, scalar1=s1, scalar2=s2, op0=op0, op1=op1),
                      r=r, w=w)

        def STT(eng, out, in0, scalar, in1, op0, op1, r, w):
            P.add(eng, lambda e: e.scalar_tensor_tensor(out=out, in0=in0, scalar=scalar, in1=in1, op0=op0, op1=op1),
                  r=r, w=w)

        def CP(eng, out, in_, r, w):
            P.add(eng, lambda e: e.tensor_copy(out=out, in_=in_), r=r, w=w)

        def MS(eng, ap, val, w):
            P.add(eng, lambda e: e.memset(ap, val), w=w)

        def RCP(out, in_, r, w):
            P.add("dve", lambda e: e.reciprocal(out=out, in_=in_), r=r, w=w)

        psA = ps("psA", [128, 1024], F32)
        psB = ps("psB", [128, 1024], F32)
        pb = [psA[:, 0:512], psA[:, 512:1024], psB[:, 0:512], psB[:, 512:1024],
              ps("pb4", [128, 512], F32)[:, :], ps("pb5", [128, 512], F32)[:, :]]
        psC_t = ps("psC", [128, 2048], BF16)
        ptb = [psC_t[:, 0:1024], psC_t[:, 1024:2048]]
        psC = psC_t[:, :].bitcast(F32)
        pbk = [("pb", i) for i in range(6)]
        ptk = [("ptb", i) for i in range(2)]

        identf = sb("identf", [128, 128], F32)
        identb = sb("identb", [128, 128], BF16)
        epsc = sb("epsc", [128, 1], F32)
        onec = sb("onec", [128, 1], F32)
        cc = sb("cc_sb", [128, 4], F32)
        Utri = sb("Utri", [128, 128], F32)
        sel127 = sb("sel127", [128, 128], F32)
        ones_f = sb("ones_f", [128, 128], F32)
        gcol = sb("gcol", [128, 5, 8], F32)
        cwT = sb("cwT", [128, 2, 4, 44], F32)
        lng_bc = sb("lng_bc", [128, D], F32)
        lnb_bc = sb("lnb_bc", [128, D], F32)
        bs_bc = sb("bs_bc", [128, 8, 128], F32)
        bs_bc_s = sb("bs_bc_s", [128, 8, 64], F32)
        wsT = sb("wsT", [128, 4, 128], BF16)
        wsT_s = sb("wsT_s", [64, 4, 64], BF16)
        gk_bc = sb("gk_bc", [128, HD], F32)
        gq_bc = sb("gq_bc", [128, HD], F32)
        bf_bc = sb("bf_bc", [128, NH], F32)
        wf_b = sb("wf_b", [128, 8, NH], BF16)
        Af = sb("Af", [128, 5, 512], F32)
        Atri = sb("Atri", [128, 128], F32)
        As64 = sb("As64", [64, NH, 32], F32)
        junk = sb("junk", [128, D], F32)
        vtmp = sb("vtmp", [128, D], F32)
        As2 = junk[0:64, 0:NH * 32].rearrange("p (h q) -> p h q", h=NH)
        ck_all = sb("ck_all", [128, NB, NH], F32)
        run = sb("run", [1, NH], F32)
        small = sb("small", [128, 64], F32)
        stg = sb("stg", [128, 128], F32)
        stg2 = sb("stg2", [128, 128], F32)

        MS("pool", epsc[:], EPS, ["epsc"])
        MS("pool", onec[:], 1.0, ["onec"])
        MS("pool", ones_f[:], 1.0, ["ones_f"])
        MS("pool", identf[:], 0.0, ["identf"])
        P.add("pool", lambda e: e.affine_select(out=identf[:], in_=identf[:], compare_op=ALU.not_equal, fill=1.0, base=0,
                                                pattern=[[-1, 128]], channel_multiplier=1), r=["identf"], w=["identf"])
        CP("pool", identb[:], identf[:], ["identf"], ["identb"])
        MS("pool", Utri[:], 1.0, ["Utri"])
        P.add("pool", lambda e: e.affine_select(out=Utri[:], in_=Utri[:], compare_op=ALU.is_ge, fill=0.0, base=0,
                                                pattern=[[1, 128]], channel_multiplier=-1), r=["Utri"], w=["Utri"])
        MS("pool", sel127[:], 1.0, ["sel127"])
        P.add("pool", lambda e: e.affine_select(out=sel127[:], in_=sel127[:], compare_op=ALU.is_ge, fill=0.0, base=-127,
                                                pattern=[[0, 128]], channel_multiplier=1), r=["sel127"], w=["sel127"])
        P.dma("sp", cc[:], cc_in, w=["cc"])
        P.dma("sp", lng_bc[:], a_ln_g[0].partition_broadcast(128), w=["lng_bc"])
        P.dma("sp", lnb_bc[:], a_ln_b[0].partition_broadcast(128), w=["lnb_bc"])
        P.dma("sp", gk_bc[:], k_norm_g.partition_broadcast(128), w=["gk_bc"])
        P.dma("sp", gq_bc[:], q_norm_g[0].partition_broadcast(128), w=["gq_bc"])
        TS("dve", gq_bc[:], gq_bc[:], 0.125, None, ALU.mult, None, ["gq_bc"], ["gq_bc"])
        P.dma("sp", bf_bc[:], b_f.partition_broadcast(128), w=["bf_bc"])
        for g in range(4):
            for cc_ in range(2):
                P.dma("sp", bs_bc[:, 2 * g + cc_, :], a_b_s[0, g].partition_broadcast(128), w=["bs_bc"])
                for s in range(2):
                    P.dma("sp", bs_bc_s[:, 2 * g + cc_, s * 32:(s + 1) * 32], a_b_s[0, g, 0:32].partition_broadcast(128),
                          w=["bs_bc_s"])
        MS("pool", Af[:], 0.0, ["Af"])
        for j in range(4):
            P.add("pool", lambda e, j=j: e.affine_select(out=Af[:, j, :], in_=Af[:, j, :], compare_op=ALU.is_ge, fill=NEG,
                                                         base=-128 * j, pattern=[[1, 512]], channel_multiplier=-1),
                  r=["Af"], w=["Af"])
        CP("pool", Atri[:], Af[:, 0, 0:128], ["Af"], ["Atri"])
        MS("pool", Af[:, 4, :], NEG, ["Af"])
        TS("pool", Af[:], Af[:], cc[:, 0:1], None, ALU.mult, None, ["Af", "cc"], ["Af"])
        MS("pool", As64[:], 0.0, ["As64"])
        P.add("pool", lambda e: e.affine_select(out=As64[:], in_=As64[:], compare_op=ALU.is_ge, fill=NEG, base=0,
                                                pattern=[[0, NH], [1, 32]], channel_multiplier=-1), r=["As64"], w=["As64"])
        MS("pool", As2, 0.0, ["junk"])
        P.add("pool", lambda e: e.affine_select(out=As2, in_=As2, compare_op=ALU.is_ge, fill=NEG, base=32,
                                                pattern=[[0, NH], [1, 32]], channel_multiplier=-1), r=["junk"], w=["junk"])
        CP("pool", As64[32:64, :, :], As2[32:64, :, :], ["junk", "As64"], ["As64"])

        def rows_to_cols(rows_ap, n, dst, dkey):
            P.dma("sp", stg[:n, :], rows_ap, w=["stg"])
            TR(pb[5][:, 0:n], stg[:n, :], identf[:n, :n], ["stg", "identf"], [pbk[5]])
            ACT(dst, pb[5][:, 0:n], AF.Copy, [pbk[5]], [dkey])

        for n, src in enumerate([norm_mix[0], norm_ffn[0], kv_norm, norm_mix[1], norm_ffn[1]]):
            rows_to_cols(src.rearrange("(k p) -> k p", p=128), 8, gcol[:, n, :], "gcol")
        for l in range(2):
            for t in range(4):
                src = f_conv_w[l, t] if t < 3 else f_conv_b[l]
                rows_to_cols(src.rearrange("(c p) -> c p", p=128), 44, cwT[:, l, t, :], "cwT")
        for g in range(4):
            P.dma("sp", stg[:, :], a_w_s[0, g], w=["stg"])
            TR(pb[5][:, 0:128], stg[:, :], identf[:], ["stg", "identf"], [pbk[5]])
            ACT(stg2[:], pb[5][:, 0:128], AF.Copy, [pbk[5]], ["stg2"])
            MS("pool", stg2[64:128, 0:64], 0.0, ["stg2"])
            CP("dve", wsT[:, g, :], stg2[:], ["stg2"], ["wsT"])
        wss_f = vtmp[0:64, 0:256].rearrange("p (g i) -> p g i", g=4)
        MS("pool", wss_f, 0.0, ["wss_f"])
        for g in range(4):
            for s in range(2):
                P.dma("sp", wss_f[s * 32:(s + 1) * 32, g, s * 32:(s + 1) * 32],
                      a_w_s[0, g, 0:32, 0:32].rearrange("i j -> j i"), r=["wss_f"], w=[("wss_f", g, s)], slow=True)
        CP("dve", wsT_s[:], wss_f, [("wss_f", g, s) for g in range(4) for s in range(2)], ["vtmp", "wsT_s"])

        big = sb("big", [128, 8192], F32)
        hidT_flat = sb("hidT", [128, NM * 512], BF16)
        cvf = [big[:, 4096 + i * 2048:4096 + (i + 1) * 2048] for i in range(2)]
        cvb = [hidT_flat[:, i * 2048:(i + 1) * 2048] for i in range(2)]
        piece_idx = {}
        npc = [0]

        def convert(name, src_ap, shape, nparts=128):
            i = npc[0]
            npc[0] += 1
            piece_idx[name] = i
            b = i % 2
            fv = cvf[b][:nparts, :]
            bv = cvb[b][:nparts, :]
            kf = [("vbuf", 2 * b), ("vbuf", 2 * b + 1)]
            kb = [("hidT", 4 * b + j) for j in range(4)]
            if len(shape) == 2:
                fv = fv.rearrange("p (a b) -> p a b", a=shape[0])
                bv = bv.rearrange("p (a b) -> p a b", a=shape[0])
            elif len(shape) == 3:
                fv = fv.rearrange("p (a b c) -> p a b c", a=shape[0], b=shape[1])
                bv = bv.rearrange("p (a b c) -> p a b c", a=shape[0], b=shape[1])
            P.dma("sp", fv, src_ap, w=kf)
            eng = "dve" if i % 2 == 0 else "act"
            if eng == "act":
                ACT(cvb[b][:nparts, :], cvf[b][:nparts, :], AF.Copy, kf, kb)
            else:
                CP("dve", cvb[b][:nparts, :], cvf[b][:nparts, :], kf, kb)
            P.dma("pool", WS[i, :nparts, :], cvb[b][:nparts, :], r=kb, w=[("WS", i)])

        def w1024(name, W, ncol):
            v = W.rearrange("(k p) n -> p k n", p=128)
            for q in range(ncol // 256):
                convert((name, q), v[:, :, q * 256:(q + 1) * 256], [8, 256])

        w1024("win", a_w_in[0], 2048)
        w1024("wout", a_w_out[0], 1024)
        w1024("wk", w_k, 1024)
        w1024("wv", w_v, 1024)
        w1024("wq", b_w_q[0], 1024)
        w1024("wo", b_w_o[0], 1024)
        for l in range(2):
            upv = f_w_up[l].rearrange("(k p) (gv m j) -> p gv m k j", p=128, gv=2, m=NM, j=128)
            for m in range(NM):
                convert(("up", l, m), upv[:, :, m], [2, 8, 128])
            dnv = f_w_down[l].rearrange("(m p) n -> p m n", p=128)
            for mp in range(NM // 2):
                convert(("dn", l, mp), dnv[:, 2 * mp:2 * mp + 2, :], [2, 1024])
        assert npc[0] == NPIECE
        P.dma("sp", cvf[0][:, 0:128].rearrange("p (k n) -> p k n", k=8), w_f.rearrange("(k p) n -> p k n", p=128),
              w=[("vbuf", 0), ("vbuf", 1)])
        CP("dve", wf_b[:], cvf[0][:, 0:128].rearrange("p (k n) -> p k n", k=8), [("vbuf", 0), ("vbuf", 1)], ["wf_b"])

        NRING = 4
        ring = [sb(f"ring{i}", [128, 2048], BF16) for i in range(NRING)]
        rcnt = [0]

        def wpiece(name, nparts=128):
            s = rcnt[0] % NRING
            rcnt[0] += 1
            i = piece_idx[name]
            P.dma("sp", ring[s][:nparts, :], WS[i, :nparts, :], r=[("WS", i)], w=[("ring", s)])
            return ring[s], ("ring", s)

        h = big[:, 0:4096].rearrange("p (b d) -> p b d", b=4)
        hnT = sb("hnT", [128, 8, 512], BF16)
        uT = sb("uT", [128, 8, 512], BF16)
        vbuf = big[:, 4096:8192].rearrange("p (b d) -> p b d", b=4)
        vnb = sb("vnb", [128, 4, D], BF16)
        hsb = [sb(f"hsb{i}", [128, D], BF16) for i in range(2)]
        hidT = hidT_flat[:, :].rearrange("p (m t) -> p m t", m=NM)
        vraw = hidT_flat[:, 0:8192].bitcast(F32).rearrange("p (b d) -> p b d", b=4)

        def vrk(b):
            return [("hidT", 4 * b + j) for j in range(4)]
        abuf = [sb(f"abuf{i}", [128, 520], F32) for i in range(4)]
        cbuf = [sb(f"cbuf{i}", [128, 512], F32) for i in range(4)]
        sgb = [sb(f"sgb{i}", [128, 512], F32) for i in range(2)]
        carry = [sb(f"carry{l}", [128, 44, 2, 2], F32) for l in range(2)]
        kaug = sb("kaug", [128, NH, 66], BF16)
        ktT = sb("ktT", [66, NH, 512], BF16)
        vxt = sb("vxt", [128, NH, 4, 65], BF16)
        lfsb = sb("lfsb", [128, NH], F32)
        lf4 = sb("lf4", [128, 4 * NH], F32)
        lnst = sb("lnst", [128, 4, 2], F32)
        tot4 = sb("tot4", [1, 4 * NH], F32)
        Rrow = sb("Rrow", [1, 5, NH], F32)
        lft = sb("lft", [128, NH], F32)
        smc = [0]

        def scol(n=1):
            c = smc[0] % (64 // 4) * 4
            smc[0] += 1
            return small[:, c:c + n], ("small", c)

        MS("pool", kaug[:], 1.0, ["kaug"])
        MS("pool", vxt[:], 1.0, ["vxt"])

        nrm_cnt = [0]

        def norm_tile(srcs, rows, nidx, dst, wkey):
            nb_ = len(srcs)
            ssc, ssk = scol(4)
            for b, (src, rkeys) in enumerate(srcs):
                ACT(junk[:rows, :], src, AF.Square, rkeys, ["junk", ssk], accum=ssc[:rows, b:b + 1])
            ACT(ssc[:rows, 0:nb_], ssc[:rows, 0:nb_], AF.Sqrt, [ssk, "epsc"], [ssk], scale=1.0 / D, bias=epsc[:rows, 0:1])
            RCP(ssc[:rows, 0:nb_], ssc[:rows, 0:nb_], [ssk], [ssk])
            for b, (src, rkeys) in enumerate(srcs):
                hb = nrm_cnt[0] % 2
                nrm_cnt[0] += 1
                ACT(hsb[hb][:rows, :], src, AF.Copy, rkeys + [ssk], [("hsb", hb)], scale=ssc[:rows, b:b + 1])
                pt = ptb[hb]
                for k in range(8):
                    TR(pt[:, k * 128:k * 128 + rows], hsb[hb][:rows, k * 128:(k + 1) * 128], identb[:rows, :rows],
                       [("hsb", hb), "identb"], [ptk[hb]])
                TT("dve", dst[:, :, b * rows:(b + 1) * rows], pt[:, :].rearrange("p (k t) -> p k t", k=8)[:, :, 0:rows],
                   gcol[:, nidx, :].unsqueeze(2).to_broadcast([128, 8, rows]), ALU.mult, [ptk[hb], "gcol"], [(wkey, b)])

        def norm_T(src, rows, nidx, dst, col0, rkeys, wkey):
            assert col0 == 0
            norm_tile([(src, rkeys)], rows, nidx, dst, wkey)

        def proj_tok(wname, nq, rows, nblk, evac, extra_r):
            for q in range(nq):
                wt, wkk = wpiece((wname, q))
                wv_ = wt[:, :].rearrange("p (k n) -> p k n", k=8)
                for b in range(nblk):
                    bank = (q * nblk + b) % 4
                    for k in range(8):
                        MM(pb[bank][:rows, 0:256], hnT[:, k, b * rows:(b + 1) * rows], wv_[:, k, :], k == 0, k == 7,
                           [wkk, ("hnT", b)] + extra_r, [pbk[bank]])
                    evac(b, q, pb[bank][:rows, 0:256], pbk[bank])

        def ffn(l, rows, nblk, nseg, first, halo=False):
            ntok = rows * nblk
            seglen = ntok // nseg
            prev_b = [None]
            for m in range(NM):
                wt, wkk = wpiece(("up", l, m))
                wv_ = wt[:, :].rearrange("p (g k j) -> p g k j", g=2, k=8)
                res = []
                for gv in range(2):
                    c = gv * NM + m
                    bank = (2 * m + gv) % 4
                    for k in range(8):
                        MM(pb[bank][:, 0:ntok], wv_[:, gv, k, :], hnT[:, k, 0:ntok], k == 0, k == 7, [wkk] + HNALL,
                           [pbk[bank]])
                    ab = abuf[bank]
                    abv = ab[:, 0:nseg * (seglen + 2)].rearrange("p (s t) -> p s t", s=nseg)
                    pav = pb[bank][:, 0:ntok].rearrange("p (s t) -> p s t", s=nseg)
                    ck_ = ("carry", l, c)
                    CP("dve", abv[:, :, 0:2], carry[l][:, c, 0:nseg, :], [ck_], [("abufc", bank)])
                    ACT(abv[:, :, 2:], pav, AF.Copy, [pbk[bank]], [("abuf", bank)])
                    CP("dve", carry[l][:, c, 0:nseg, :], abv[:, :, seglen:seglen + 2], [("abuf", bank), ("abufc", bank)], [ck_])
                    if halo:
                        continue
                    cb = cbuf[bank]
                    cbv = cb[:, 0:ntok].rearrange("p (s t) -> p s t", s=nseg)
                    ACT(cbv, pav, AF.Identity, [pbk[bank], "cwT"], [("cbuf", bank)], scale=cwT[:, l, 2, c:c + 1],
                        bias=cwT[:, l, 3, c:c + 1])
                    STT("dve", cbv, abv[:, :, 1:seglen + 1], cwT[:, l, 1, c:c + 1], cbv, ALU.mult, ALU.add,
                        [("abuf", bank), ("abufc", bank), ("cbuf", bank), "cwT"], [("cbuf", bank)])
                    STT("dve", cbv, abv[:, :, 0:seglen], cwT[:, l, 0, c:c + 1], cbv, ALU.mult, ALU.add,
                        [("abuf", bank), ("abufc", bank), ("cbuf", bank), "cwT"], [("cbuf", bank)])
                    res.append((cb, ("cbuf", bank)))
                if halo:
                    continue

                def stage_b(m=m, res=res):
                    sg = sgb[m % 2]
                    ACT(sg[:, 0:ntok], res[0][0][:, 0:ntok], AF.Silu, [res[0][1]], [("sgb", m % 2)])
                    TT("pool", hidT[:, m, 0:ntok], sg[:, 0:ntok], res[1][0][:, 0:ntok], ALU.mult,
                       [("sgb", m % 2), res[1][1]], [("hidT", m)])
                if prev_b[0] is not None:
                    prev_b[0]()
                prev_b[0] = stage_b
            if halo:
                return
            prev_b[0]()
            prev_b[0] = None
            npass = 2 if nblk == 4 else 1
            bpp = nblk // npass
            for ps_ in range(npass):
                for mp in range(NM // 2):
                    wt, wkk = wpiece(("dn", l, mp))
                    wv_ = wt[:, :].rearrange("p (a n) -> p a n", a=2)
                    for bb in range(bpp):
                        b = ps_ * bpp + bb
                        for hf in range(2):
                            bank = bb * 2 + hf
                            for mm in range(2):
                                m = 2 * mp + mm
                                MM(pb[bank][:rows, :], hidT[:, m, b * rows:(b + 1) * rows], wv_[:, mm, hf * 512:(hf + 1) * 512],
                                   m == 0, m == NM - 1, [wkk, ("hidT", m)], [pbk[bank]])
                for bb in range(bpp):
                    b = ps_ * bpp + bb
                    for hf in range(2):
                        bank = bb * 2 + hf
                        TT("dve", h[:rows, b, hf * 512:(hf + 1) * 512], h[:rows, b, hf * 512:(hf + 1) * 512],
                           pb[bank][:rows, :], ALU.add, [pbk[bank], ("h", b)], [("h", b)])

        def l0_tile(xsrc, rows, nblk, nseg, tile_idx, sample):
            ntok = rows * nblk
            for b in range(nblk):
                P.dma("sp", h[:rows, b, :], xsrc[b * rows:(b + 1) * rows, :], w=[("h", b)])
            norm_tile([(h[:rows, b, :], [("h", b)]) for b in range(nblk)], rows, 0, hnT, "hnT")
            for q in range(4):
                wt, wkk = wpiece(("win", 4 + q))
                wv_ = wt[:, :].rearrange("p (k n) -> p k n", k=8)
                for b in range(nblk):
                    bank = (q * nblk + b) % 4
                    for k in range(8):
                        MM(pb[bank][:rows, 0:256], hnT[:, k, b * rows:(b + 1) * rows], wv_[:, k, :], k == 0, k == 7,
                           [wkk, ("hnT", b)], [pbk[bank]])
                    ACT(vbuf[:rows, b, q * 256:(q + 1) * 256], pb[bank][:rows, 0:256], AF.Gelu_apprx_tanh, [pbk[bank]],
                        [("vbuf", b)])
            for b in range(nblk):
                st6 = stg[:, b * 12:(b + 1) * 12].rearrange("p (a b) -> p a b", a=2)
                for c2 in range(2):
                    P.add("dve", lambda e, b=b, c2=c2, st6=st6: e.bn_stats(out=st6[:rows, c2, :], in_=vbuf[:rows, b, c2 * 512:(c2 + 1) * 512]),
                          r=[("vbuf", b)], w=[("st6", b)])
                P.add("dve", lambda e, b=b, st6=st6: e.bn_aggr(out=lnst[:rows, b, :], in_=st6[:rows, :, :]), r=[("st6", b)], w=["lnst"])
            ACT(lnst[:rows, 0:nblk, 1], lnst[:rows, 0:nblk, 1], AF.Sqrt, ["lnst", "epsc"], ["lnst"], bias=epsc[:rows, 0:1])
            RCP(lnst[:rows, 0:nblk, 1], lnst[:rows, 0:nblk, 1], ["lnst"], ["lnst"])
            for b in range(nblk):
                TS("dve", vtmp[:rows, :], vbuf[:rows, b, :], lnst[:rows, b, 0:1], lnst[:rows, b, 1:2], ALU.subtract, ALU.mult,
                   [("vbuf", b), "lnst"], ["vtmp"])
                TT("dve", vtmp[:rows, :], vtmp[:rows, :], lng_bc[:rows, :], ALU.mult, ["vtmp", "lng_bc"], ["vtmp"])
                if sample:
                    TT("dve", vtmp[:rows, :], vtmp[:rows, :], lnb_bc[:rows, :], ALU.add, ["vtmp", "lnb_bc"], ["vtmp"])
                    P.dma("pool", sguv_s, vtmp[:rows, :], r=["vtmp"])
                    ACT(vnb[:rows, b, :], vtmp[:rows, :], AF.Copy, ["vtmp"], [("vnb", b)])
                else:
                    TT("dve", vnb[:rows, b, :], vtmp[:rows, :], lnb_bc[:rows, :], ALU.add, ["vtmp", "lnb_bc"], [("vnb", b)])
            for q in range(4):
                wt, wkk = wpiece(("win", q))
                wv_ = wt[:, :].rearrange("p (k n) -> p k n", k=8)
                for cc_ in range(2):
                    c = q * 2 + cc_
                    bank = c % 4
                    for k in range(8):
                        MM(pb[bank][:, 0:ntok], wv_[:, k, cc_ * 128:(cc_ + 1) * 128], hnT[:, k, 0:ntok], k == 0, k == 7,
                           [wkk] + HNALL, [pbk[bank]])
                    ACT(uT[:, c, 0:ntok], pb[bank][:, 0:ntok], AF.Gelu_apprx_tanh, [pbk[bank]], [("uT", c)])
            for b in range(nblk):
                wst = wsT_s if sample else wsT
                bsb = bs_bc_s if sample else bs_bc
                for c in range(8):
                    bank = 4 + c // 4
                    MM(pb[bank][:, (c % 4) * 128:(c % 4) * 128 + rows], vnb[:rows, b, c * 128:(c + 1) * 128],
                       wst[:rows, c // 2, :rows], True, True, [("vnb", b), "wsT", "wsT_s"], [pbk[bank]])
                for hb in range(2):
                    bank = 4 + hb
                    pv_ = pb[bank][:, :].rearrange("p (c t) -> p c t", c=4)[:, :, 0:rows]
                    tv = junk[:, hb * 512:(hb + 1) * 512].rearrange("p (c t) -> p c t", c=4)[:, :, 0:rows]
                    TT("dve", tv, pv_, bsb[:, hb * 4:(hb + 1) * 4, 0:rows], ALU.add, [pbk[bank], "bs_bc", "bs_bc_s"],
                       ["junk"])
                    uv = uT[:, hb * 4:(hb + 1) * 4, b * rows:(b + 1) * rows]
                    TT("dve", uv, tv, uv, ALU.mult, ["junk"] + [("uT", hb * 4 + c) for c in range(4)],
                       [("uT", hb * 4 + c) for c in range(4)])
            def ev_out(b, q, pap, pk):
                TT("dve", h[:rows, b, q * 256:(q + 1) * 256], h[:rows, b, q * 256:(q + 1) * 256], pap, ALU.add,
                   [pk, ("h", b)], [("h", b)])
            for q in range(4):
                wt, wkk = wpiece(("wout", q))
                wv_ = wt[:, :].rearrange("p (k n) -> p k n", k=8)
                for b in range(nblk):
                    bank = (q * nblk + b) % 4
                    for k in range(8):
                        MM(pb[bank][:rows, 0:256], uT[:, k, b * rows:(b + 1) * rows], wv_[:, k, :], k == 0, k == 7,
                           [wkk, ("uT", k)], [pbk[bank]])
                    ev_out(b, q, pb[bank][:rows, 0:256], pbk[bank])
            norm_tile([(h[:rows, b, :], [("h", b)]) for b in range(nblk)], rows, 1, hnT, "hnT")
            ffn(0, rows, nblk, nseg, tile_idx == 0)
            if sample:
                P.dma("pool", h1ss, h[:rows, 0, :], r=[("h", 0)], w=["h1ss"])
            else:
                for b in range(nblk):
                    P.dma("pool", h1s[tile_idx * 512 + b * 128: tile_idx * 512 + (b + 1) * 128, :], h[:rows, b, :],
                          r=[("h", b)], w=[("h1s", tile_idx * 4 + b)])
            norm_tile([(h[:rows, b, :], [("h", b)]) for b in range(nblk)], rows, 2, hnT, "hnT")

            def ev_k(b, q, pap, pk):
                ACT(vbuf[:rows, b, q * 256:(q + 1) * 256], pap, AF.Copy, [pk], [("vbuf", b)])
            proj_tok("wk", 4, rows, nblk, ev_k, [])
            def ev_v(b, q, pap, pk):
                ACT(vraw[:rows, b, q * 256:(q + 1) * 256], pap, AF.Copy, [pk], vrk(b))
            proj_tok("wv", 4, rows, nblk, ev_v, [])
            for b in range(nblk):
                kr = vbuf[:rows, b, :]
                k3 = kr.rearrange("p (h d) -> p h d", d=HD)
                sc, sck = lft, "lft"
                TT("dve", vtmp[:rows, :], kr, kr, ALU.mult, [("vbuf", b)], ["vtmp"])
                P.add("dve", lambda e, sc=sc: e.tensor_reduce(out=sc[:rows, :], in_=vtmp[:rows, :].rearrange("p (h d) -> p h d", d=HD),
                                                        axis=AX.X, op=ALU.add), r=["vtmp"], w=[sck])
                ACT(sc[:rows, :], sc[:rows, :], AF.Sqrt, [sck, "epsc"], [sck], scale=1.0 / HD, bias=epsc[:rows, 0:1])
                RCP(sc[:rows, :], sc[:rows, :], [sck], [sck])
                TT("dve", k3, k3, sc[:rows, :].unsqueeze(2).to_broadcast([rows, NH, HD]), ALU.mult, [("vbuf", b), sck], [("vbuf", b)])
                TT("dve", k3, k3, gk_bc[:rows, :].unsqueeze(1).to_broadcast([rows, NH, HD]), ALU.mult, [("vbuf", b), "gk_bc"],
                   [("vbuf", b)])
                if sample:
                    P.dma("pool", k_s, kr, r=[("vbuf", b)])
                else:
                    r0 = tile_idx * 512 + b * 128
                    P.dma("pool", k_p[r0:r0 + 128, :], kr, r=[("vbuf", b)])
                ACT(kaug[:rows, :, 0:HD], k3, AF.Copy, [("vbuf", b), "kaug"], ["kaug"])
                for hh in range(NH):
                    pt = ptb[hh // 8]
                    TR(pt[:66, (hh % 8) * 128:(hh % 8) * 128 + rows], kaug[:rows, hh, :], identb[:rows, :rows],
                       ["kaug", "identb"], [ptk[hh // 8]])
                for hb in range(2):
                    ACT(ktT[:, hb * 8:(hb + 1) * 8, b * rows:(b + 1) * rows],
                        ptb[hb][:66, :].rearrange("p (k t) -> p k t", k=8)[:, :, 0:rows], AF.Copy, [ptk[hb]], [("ktT", b)])
            if not sample:
                P.dma("pool", KTs.rearrange("h r s -> r h s")[:, :, tile_idx * 512:(tile_idx + 1) * 512], ktT[:, :, :],
                      r=[("ktT", b) for b in range(4)], w=[("KTs", tile_idx)])

            for b in range(nblk):
                if sample:
                    P.dma("pool", v_s, vraw[:rows, b, :], r=vrk(b))
                else:
                    r0 = tile_idx * 512 + b * 128
                    P.dma("pool", v_p[r0:r0 + 128, :], vraw[:rows, b, :], r=vrk(b))
                ACT(vxt[:rows, :, b, 0:HD], vraw[:rows, b, :].rearrange("p (h d) -> p h d", d=HD), AF.Copy,
                    vrk(b) + ["vxt"], ["vxt"])
            if not sample:
                P.dma("pool", VXs.rearrange("h p n e -> p h n e")[:, :, tile_idx * 4:(tile_idx + 1) * 4, :], vxt[:, :, :, :],
                      r=["vxt"], w=[("VXs", tile_idx)])
            for b in range(nblk):
                for k in range(8):
                    MM(pb[4][:rows, b * NH:(b + 1) * NH], hnT[:, k, b * rows:(b + 1) * rows], wf_b[:, k, :], k == 0, k == 7,
                       [("hnT", b), "wf_b"], [pbk[4]])
            nl = nblk * NH
            l3 = lf4[:rows, 0:nl].rearrange("p (b h) -> p b h", h=NH)
            TT("dve", l3, pb[4][:rows, 0:nl].rearrange("p (b h) -> p b h", h=NH),
               bf_bc[:rows, :].unsqueeze(1).to_broadcast([rows, nblk, NH]), ALU.add, [pbk[4], "bf_bc"], ["lf4"])
            ACT(lf4[:rows, 0:nl], lf4[:rows, 0:nl], AF.Exp, ["lf4"], ["lf4"], scale=-1.0)
            ACT(lf4[:rows, 0:nl], lf4[:rows, 0:nl], AF.Ln, ["lf4", "onec"], ["lf4"], bias=onec[:rows, 0:1])
            TS("dve", lf4[:rows, 0:nl], lf4[:rows, 0:nl], -1.0, None, ALU.mult, None, ["lf4"], ["lf4"])
            if sample:
                P.dma("pool", lf_s, lf4[:rows, 0:NH], r=["lf4"])
                CP("dve", lfn[:, :], lf4[:64, 0:NH], ["lf4"], ["lfn"])
            else:
                r0 = tile_idx * 512
                P.dma("pool", lf_p[r0:r0 + 512, :].rearrange("(b p) h -> p b h", p=128), l3, r=["lf4"])
                if tile_idx == 0:
                    MS("dve", run[:], 0.0, ["run"])
                MM(pb[5][0:1, 0:nl], ones_f[:, 0:1], lf4[:, 0:nl], True, True, ["ones_f", "lf4"], [pbk[5]])
                ACT(tot4[0:1, 0:nl], pb[5][0:1, 0:nl], AF.Copy, [pbk[5]], ["tot4"])
                CP("dve", Rrow[0:1, 0, :], run[0:1, :], ["run"], ["Rrow"])
                for b in range(nblk):
                    TT("dve", Rrow[0:1, b + 1, :], Rrow[0:1, b, :], tot4[0:1, b * NH:(b + 1) * NH], ALU.add, ["Rrow", "tot4"], ["Rrow"])
                for b in range(nblk):
                    MM(pb[5][:, 64 + b * NH:64 + (b + 1) * NH], Utri[:, :], lf4[:, b * NH:(b + 1) * NH], True, False,
                       ["Utri", "lf4"], [pbk[5]])
                    MM(pb[5][:, 64 + b * NH:64 + (b + 1) * NH], ones_f[0:1, :], Rrow[0:1, b, :], False, True,
                       ["ones_f", "Rrow"], [pbk[5]])
                ACT(ck_all[:, tile_idx * 4:tile_idx * 4 + 4, :], pb[5][:, 64:64 + nl].rearrange("p (b h) -> p b h", h=NH), AF.Copy,
                    [pbk[5]], [("ck", tile_idx * 4 + b) for b in range(4)])
                CP("dve", run[0:1, :], Rrow[0:1, nblk, :], ["Rrow"], ["run"])

        GQ = 2 if NTH % 2 == 0 else 1
        NG = NTH // GQ
        NPAIR = sum(NBH + 4 * GQ * (g + 1) for g in range(NG)) + NBH
        assert NPAIR * NH <= 8192
        bias_all = big[:, 0:NPAIR * NH].rearrange("p (n h) -> p n h", h=NH)
        cko = sb("cko", [128, NBH + 1, NH], F32)
        Cb_all = sb("Cb_all", [128, NTH + 1, NH], F32)
        hi_f = sb("hi_f", [128, NH], F32)
        h1ss = dscr("h1ss", [64, D], F32)
        ktn = sb("ktn", [66, NH, 64], BF16)
        vxn = sb("vxn", [64, NH, 65], BF16)
        qts = sb("qts", [66, NH, 64], BF16)
        atts = sb("atts", [64, NH, 64], BF16)
        atts_p = sb("atts_p", [128, NH // 2, 64], BF16)
        Shiftm = sb("Shiftm", [64, 128], BF16)
        MS("pool", Shiftm[:], 0.0, ["Shiftm"])
        CP("pool", Shiftm[0:64, 64:128], identb[0:64, 0:64], ["identb", "Shiftm"], ["Shiftm"])
        lfn = sb("lfn", [64, NH], F32)
        assert 2 * NPB * NH <= D
        cks_all = lng_bc[:, 0:2 * NPB * NH].rearrange("p (n h) -> p n h", h=NH)
        bias_s = lnb_bc[:, 0:2 * NPB * NH].rearrange("p (n h) -> p n h", h=NH)
        ckn = sb("ckn", [64, NH], F32)
        runs = sb("runs", [1, 2, NH], F32)
        runf = sb("runf", [1, 2, NH], F32)
        UtriS = sb("UtriS", [64, 64], F32)
        onesAB = sb("onesAB", [1, 2, 64], F32)
        colAB = sb("colAB", [64, 2], F32)
        CP("pool", UtriS[:, :], Utri[0:64, 0:64], ["Utri"], ["UtriS"])
        MS("pool", UtriS[0:32, 32:64], 0.0, ["UtriS"])
        MS("pool", onesAB[:], 0.0, ["onesAB"])
        MS("pool", onesAB[0:1, 0, 0:32], 1.0, ["onesAB"])
        MS("pool", onesAB[0:1, 1, 32:64], 1.0, ["onesAB"])
        MS("pool", colAB[:], 0.0, ["colAB"])
        MS("pool", colAB[0:32, 0:1], 1.0, ["colAB"])
        MS("pool", colAB[32:64, 1:2], 1.0, ["colAB"])
        Cb128s = sb("Cb128s", [128, 2, NH], F32)

        def load_own(i, b, halo):
            if halo:
                ga = NBH - 1
                P.dma("sp", h[:, b, :], h1s[ga * 128:(ga + 1) * 128, :], r=[("h1s", ga)], w=[("h", b)])
                return
            ga = i * 4 + b
            gb_ = NBH + ga
            P.dma("sp", h[:, b, :], h1s[ga * 128:(ga + 1) * 128, :], r=[("h1s", ga)], w=[("h", b)])
            P.dma("sp", vbuf[:, b, :], h1s[gb_ * 128:(gb_ + 1) * 128, :], r=[("h1s", gb_)], w=[("vbuf", b)])
            TS("dve", h[:, b, :], h[:, b, :], cc[:, 0:1], None, ALU.mult, None, [("h", b), "cc"], [("h", b)])
            STT("dve", h[:, b, :], vbuf[:, b, :], cc[:, 1:2], h[:, b, :], ALU.mult, ALU.add, [("vbuf", b), ("h", b), "cc"],
                [("h", b)])

        def qk_norm_aug(rows, b, gbc, src=None, skeys=None):
            kr = vbuf[:rows, b, :] if src is None else src
            sk = [("vbuf", b)] if skeys is None else skeys
            k3 = kr.rearrange("p (h d) -> p h d", d=HD)
            TT("dve", vtmp[:rows, :], kr, kr, ALU.mult, sk, ["vtmp"])
            P.add("dve", lambda e: e.tensor_reduce(out=lft[:rows, :], in_=vtmp[:rows, :].rearrange("p (h d) -> p h d", d=HD),
                                                   axis=AX.X, op=ALU.add), r=["vtmp"], w=["lft"])
            ACT(lft[:rows, :], lft[:rows, :], AF.Sqrt, ["lft", "epsc"], ["lft"], scale=1.0 / HD, bias=epsc[:rows, 0:1])
            RCP(lft[:rows, :], lft[:rows, :], ["lft"], ["lft"])
            v3 = vtmp[:rows, :].rearrange("p (h d) -> p h d", d=HD)
            TT("dve", v3, k3, lft[:rows, :].unsqueeze(2).to_broadcast([rows, NH, HD]), ALU.mult, sk + ["lft"], ["vtmp"])
            TT("dve", v3, v3, gbc[:rows, :].unsqueeze(1).to_broadcast([rows, NH, HD]), ALU.mult, ["vtmp", "gk_bc", "gq_bc"],
               ["vtmp"])
            ACT(kaug[:rows, :, 0:HD], v3, AF.Copy, ["vtmp", "kaug"], ["kaug"])

        def aug_hilo(rows, crel, crk):
            CP("dve", kaug[:rows, :, 64], crel, [crk, "kaug"], ["kaug"])
            CP("dve", hi_f[:rows, :], kaug[:rows, :, 64], ["kaug"], ["hi_f"])
            TT("dve", kaug[:rows, :, 65], crel, hi_f[:rows, :], ALU.subtract, [crk, "hi_f", "kaug"], ["kaug"])

        def aug_transposes(rows, dst, dkey):
            for hh in range(NH):
                pt = ptb[hh // 8]
                TR(pt[:66, (hh % 8) * 128:(hh % 8) * 128 + rows], kaug[:rows, hh, :], identb[:rows, :rows],
                   ["kaug", "identb"], [ptk[hh // 8]])
            for hb in range(2):
                ACT(dst[:, hb * 8:(hb + 1) * 8, :],
                    ptb[hb][:66, :].rearrange("p (k t) -> p k t", k=8)[:, :, 0:rows], AF.Copy, [ptk[hb]], [dkey])

        def ev_q(rows):
            def f(b, q, pap, pk):
                ACT(vbuf[:rows, b, q * 256:(q + 1) * 256], pap, AF.Copy, [pk], [("vbuf", b)])
            return f

        def l1a_tile(i):
            halo = (i == NTH)
            nblk = 1 if halo else 4
            for b in range(nblk):
                load_own(i, b, halo)
            norm_tile([(h[:, b, :], [("h", b)]) for b in range(nblk)], 128, 3, hnT, "hnT")
            def ev_qv(b, q, pap, pk):
                ACT(vraw[:, b, q * 256:(q + 1) * 256], pap, AF.Copy, [pk], vrk(b))
            proj_tok("wq", 4, 128, nblk, ev_qv, [])
            gi_ = NG if halo else i // GQ
            for b in range(nblk):
                ob = i * 4 + b
                qk_norm_aug(128, b, gq_bc, src=vraw[:, b, :], skeys=vrk(b))
                TT("dve", lfsb[:, :], cko[:, ob, :], Cb_all[:, gi_, :], ALU.subtract, [("cko", ob), ("Cb", gi_)], ["lfsb"])
                aug_hilo(128, lfsb[:, :], "lfsb")
                aug_transposes(128, ktT[:, :, b * 128:(b + 1) * 128], ("ktT", b))
            P.dma("pool", QTs.rearrange("h r s -> r h s")[:, :, i * 512:i * 512 + nblk * 128], ktT[:, :, 0:nblk * 128],
                  r=[("ktT", b) for b in range(nblk)], w=[("QTs", i)])

        def cko_prepass():
            for ob in range(NBH):
                TS("dve", cko[:, ob, :], ck_all[:, ob, :], cc[:, 0:1], None, ALU.mult, None, [("ck", ob), "cc"], [("cko", ob)])
                STT("dve", cko[:, ob, :], ck_all[:, NBH + ob, :], cc[:, 1:2], cko[:, ob, :], ALU.mult, ALU.add,
                    [("ck", NBH + ob), ("cko", ob), "cc"], [("cko", ob)])
            CP("dve", cko[:, NBH, :], ck_all[:, NBH - 1, :], [("ck", NBH - 1)], [("cko", NBH)])
            for g in range(NG + 1):
                lastb = NBH if g == NG else 4 * GQ * (g + 1) - 1
                MM(pb[5][:, 0:NH], sel127[:, :], cko[:, lastb, :], True, True, ["sel127", ("cko", lastb)], [pbk[5]])
                ACT(Cb_all[:, g, :], pb[5][:, 0:NH], AF.Copy, [pbk[5]], [("Cb", g)])

        pair_idx = {}

        def build_bias():
            n = 0
            for g in range(NG + 1):
                halo = (g == NG)
                Cb = Cb_all[:, g, :]
                STT("dve", bias_all[:, n:n + NBH, :], ck_all[:, 0:NBH, :], -1.0, Cb.unsqueeze(1).to_broadcast([128, NBH, NH]),
                    ALU.mult, ALU.add, [("ck", j) for j in range(NBH)] + [("Cb", g)], ["bias_all"])
                for j in range(NBH):
                    pair_idx[(g, 0, j)] = n + j
                if not halo:
                    lo = 4 * GQ * (g + 1)
                    if lo < NBH:
                        TS("dve", bias_all[:, n + lo:n + NBH, :], bias_all[:, n + lo:n + NBH, :], cc[:, 2:3], None, ALU.add, None,
                           ["bias_all", "cc"], ["bias_all"])
                n += NBH
                if not halo:
                    ns = 4 * GQ * (g + 1)
                    STT("dve", bias_all[:, n:n + ns, :], ck_all[:, NBH:NBH + ns, :], -1.0,
                        Cb.unsqueeze(1).to_broadcast([128, ns, NH]), ALU.mult, ALU.add,
                        [("ck", NBH + j) for j in range(ns)] + [("Cb", g)], ["bias_all"])
                    TS("dve", bias_all[:, n:n + ns, :], bias_all[:, n:n + ns, :], cc[:, 2:3], None, ALU.add, None,
                       ["bias_all", "cc"], ["bias_all"])
                    for j in range(ns):
                        pair_idx[(g, 1, j)] = n + j
                    n += ns
            assert n == NPAIR

        KT_h = ktT[:, :, :].rearrange("p h t -> p (h t)")
        VX_h = vxt[:, :, :, :].rearrange("p h n e -> p (h n) e")
        QT_h = hidT_flat[0:66, 0:NOWN]
        AT_h = hidT_flat[0:128, 4352:4352 + NOWN]
        att_tmp = hsb[0]
        rr = cbuf[1]
        bcs = cbuf[0]

        LOOK = 2
        FINLAG = 3
        fin_pend = []
        pend = []
        acnt = [0]
        pT4 = hnT[:, :, :].rearrange("p a t -> p (a t)")
        smb = [vtmp, junk]
        bcp = psA[:, 0:512]

        def flush(upto):
            while len(pend) > upto:
                pend.pop(0)()

        def l1b_head(hh):
            P.dma("sp", KT_h[:, 0:SEQ], KTs[hh], r=[("KTs", t) for t in range(2 * NTH)], w=["KT_h"])
            P.dma("sp", VX_h[:, 0:NB, :], VXs[hh], r=[("VXs", t) for t in range(2 * NTH)], w=["VX_h"])
            P.dma("sp", QT_h, QTs[hh], r=[("QTs", i) for i in range(NTH + 1)], w=["QT_h"])
            for g in range(NG + 1):
                halo = (g == NG)
                if halo:
                    nq, q0, ntile, tw = 128, NTH * 512, 1, 128
                    klist = [(0, j) for j in range(NBH)]
                else:
                    nq, q0, ntile, tw = GQ * 512, g * GQ * 512, GQ, 512
                    klist = [(1, j) for j in range(4 * GQ * (g + 1))] + [(0, j) for j in range(NBH)]
                pai = [4 + ((g * GQ + t) % 2) for t in range(ntile)]
                for n, (half, j) in enumerate(klist):
                    kb = half * NBH + j
                    cnt = acnt[0]
                    acnt[0] += 1
                    Sb = (psA, psB, psC)[cnt % 3]
                    Sk = ([pbk[0], pbk[1]], [pbk[2], pbk[3]], [ptk[0], ptk[1]])[cnt % 3]
                    slot = cnt % 4
                    pT = pT4[:, slot * 1024:(slot + 1) * 1024]
                    pTk = ("pT", slot)
                    sm = smb[cnt % 2]
                    smk = ("smb", cnt % 2)
                    bcol = bias_all[:, pair_idx[(g, half, j)], hh:hh + 1]
                    if halo:
                        diag, jp = (j == NBH - 1), 0
                    else:
                        diag = (4 * GQ * g <= j < 4 * GQ * (g + 1))
                        jp = j - 4 * GQ * g
                    c0 = 128 * jp if (diag and half == 1) else 0
                    for t in range(ntile):
                        lo = max(c0 - t * 512, 0)
                        if lo >= tw:
                            continue
                        MM(Sb[:, t * 512 + lo:t * 512 + tw], KT_h[:, kb * 128:(kb + 1) * 128],
                           QT_h[:, q0 + t * 512 + lo:q0 + t * 512 + tw], True, True, ["KT_h", "QT_h"], Sk)
                    if not diag:
                        ACT(pT[:, 0:nq], Sb[:, 0:nq], AF.Exp, Sk + ["bias_all"], [pTk], bias=bcol)
                    elif half == 0 and not halo:
                        kt, jj = jp // 4, jp % 4
                        for t in range(ntile):
                            cs = slice(t * 512, (t + 1) * 512)
                            if t < kt:
                                TT("dve", sm[:, cs], Sb[:, cs], Af[:, 4, :], ALU.add, Sk + ["Af"], [smk])
                            elif t == kt:
                                TT("dve", sm[:, cs], Sb[:, cs], Af[:, jj, :], ALU.add, Sk + ["Af"], [smk])
                            else:
                                CP("dve", sm[:, cs], Sb[:, cs], Sk, [smk])
                        ACT(pT[:, 0:nq], sm[:, 0:nq], AF.Exp, [smk, "bias_all"], [pTk], bias=bcol)
                    else:
                        TT("dve", sm[:, c0:c0 + 128], Sb[:, c0:c0 + 128], Atri[:, :], ALU.add, Sk + ["Atri"], [smk])
                        ACT(pT[:, c0:c0 + 128], sm[:, c0:c0 + 128], AF.Exp, [smk, "bias_all"], [pTk], bias=bcol)
                        if c0 + 128 < nq:
                            ACT(pT[:, c0 + 128:nq], Sb[:, c0 + 128:nq], AF.Exp, Sk + ["bias_all", pTk], [pTk], bias=bcol)

                    def pv(c0=c0, kb=kb, pT=pT, pTk=pTk, first=(n == 0), last=(n == len(klist) - 1), ntile=ntile, tw=tw,
                           pai=pai):
                        for t in range(ntile):
                            lo = max(c0 - t * 512, 0)
                            if lo >= tw:
                                continue
                            MM(pb[pai[t]][0:65, lo:tw], VX_h[:, kb, :], pT[:, t * 512 + lo:t * 512 + tw], first, last,
                               ["VX_h", pTk], [pbk[pai[t]]])
                    pend.append(pv)
                    flush(LOOK)
                    for fp in list(fin_pend):
                        fp[0] -= 1
                        if fp[0] <= 0:
                            fin_pend.remove(fp)
                            fp[1]()

                for t in range(ntile):
                    def fin(pacc=pb[pai[t]], pacc_i=pai[t], nq=tw, q0=q0 + t * 512):
                        RCP(rr[64:65, 0:nq], pacc[64:65, 0:nq], [pbk[pacc_i]], ["rr"])
                        MM(bcp[0:64, 0:nq], ones_f[64:65, 0:64], rr[64:65, 0:nq], True, True, ["ones_f", "rr"], [pbk[0]])
                        CP("dve", bcs[0:64, 0:nq], bcp[0:64, 0:nq], [pbk[0]], ["bcs"])
                        if hh % 2 == 0:
                            TT("dve", AT_h[0:64, q0:q0 + nq], pacc[0:64, 0:nq], bcs[0:64, 0:nq], ALU.mult, [pbk[pacc_i], "bcs"],
                               ["AT_h"])
                        else:
                            TT("dve", att_tmp[0:64, 0:nq], pacc[0:64, 0:nq], bcs[0:64, 0:nq], ALU.mult, [pbk[pacc_i], "bcs"],
                               ["att_tmp"])
                            MM(bcp[:, 0:nq], Shiftm[0:64, :], att_tmp[0:64, 0:nq], True, True, ["Shiftm", "att_tmp"], [pbk[0]])
                            CP("dve", AT_h[64:128, q0:q0 + nq], bcp[64:128, 0:nq], [pbk[0]], ["AT_h"])
                    fin_pend.append([FINLAG, fin])
                if GQ > 1 or halo:
                    flush(0)
                    for fp in list(fin_pend):
                        fin_pend.remove(fp)
                        fp[1]()
            flush(0)
            for fp in list(fin_pend):
                fin_pend.remove(fp)
                fp[1]()
            if hh % 2 == 1:
                P.dma("pool", ATs[hh // 2], AT_h, r=["AT_h"], w=[("ATs", hh // 2)])

        attT = uT

        def oproj(rows, nblk, att_ap):
            for q in range(4):
                wt, wkk = wpiece(("wo", q))
                wv_ = wt[:, :].rearrange("p (k n) -> p k n", k=8)
                for b in range(nblk):
                    bank = (q * nblk + b) % 4
                    for k in range(8):
                        MM(pb[bank][:rows, 0:256], att_ap[:, k, b * rows:(b + 1) * rows], wv_[:, k, :], k == 0, k == 7,
                           [wkk, "attT"], [pbk[bank]])
                    TT("dve", h[:rows, b, q * 256:(q + 1) * 256], h[:rows, b, q * 256:(q + 1) * 256], pb[bank][:rows, 0:256],
                       ALU.add, [pbk[bank], ("h", b)], [("h", b)])

        def l1c_tile(i):
            halo = (i == NTH)
            nblk = 1 if halo else 4
            for b in range(nblk):
                load_own(i, b, halo)
            P.dma("sp", attT[:, :, 0:nblk * 128], ATs.rearrange("k p s -> p k s")[:, :, i * 512:i * 512 + nblk * 128],
                  r=[("ATs", k) for k in range(NH // 2)], w=["attT"])
            oproj(128, nblk, attT)
            norm_tile([(h[:, b, :], [("h", b)]) for b in range(nblk)], 128, 4, hnT, "hnT")
            ffn(1, 128, nblk, 1, False, halo=halo)
            if halo:
                ck1 = [("carry", 1, c) for c in range(44)]
                TS("dve", carry[1][:, :, 0, :], carry[1][:, :, 0, :], cc[:, 3:4], None, ALU.mult, None, ck1 + ["cc"], ck1)
            else:
                for b in range(nblk):
                    r0 = i * 512 + b * 128
                    P.dma("pool", y_p[r0:r0 + 128, :], h[:, b, :], r=[("h", b)])

        def l1a_sample():
            P.dma("sp", h[:64, 0, :], h1ss, r=["h1ss"], w=[("h", 0)])
            norm_T(h[:64, 0, :], 64, 3, hnT, 0, [("h", 0)], "hnT")
            proj_tok("wq", 4, 64, 1, ev_q(64), [])
            qk_norm_aug(64, 0, gq_bc)
            G = min(4, NPB)
            assert NPB % G == 0
            nl = G * NH
            bias_s_flat = lnb_bc
            for s_ in range(2):
                P.dma("sp", bias_s[:, s_ * NPB:(s_ + 1) * NPB, :], cache_lf[s_].rearrange("(j p) h -> p j h", p=128),
                      w=[("bias_s", s_)], slow=True)
                MS("dve", run[:], 0.0, ["run"])
                for g in range(NPB // G):
                    base = (s_ * NPB + g * G) * NH
                    lf2 = bias_s_flat[:, base:base + nl]
                    MM(pb[5][0:1, 0:nl], ones_f[:, 0:1], lf2, True, True, ["ones_f", ("bias_s", s_)], [pbk[5]])
                    ACT(tot4[0:1, 0:nl], pb[5][0:1, 0:nl], AF.Copy, [pbk[5]], ["tot4"])
                    CP("dve", Rrow[0:1, 0, :], run[0:1, :], ["run"], ["Rrow"])
                    for b in range(G):
                        TT("dve", Rrow[0:1, b + 1, :], Rrow[0:1, b, :], tot4[0:1, b * NH:(b + 1) * NH], ALU.add,
                           ["Rrow", "tot4"], ["Rrow"])
                    for b in range(G):
                        MM(pb[5][:, 64 + b * NH:64 + (b + 1) * NH], Utri[:, :], bias_s_flat[:, base + b * NH:base + (b + 1) * NH],
                           True, False, ["Utri", ("bias_s", s_)], [pbk[5]])
                        MM(pb[5][:, 64 + b * NH:64 + (b + 1) * NH], ones_f[0:1, :], Rrow[0:1, b, :], False, True,
                           ["ones_f", "Rrow"], [pbk[5]])
                    gi0 = s_ * NPB + g * G
                    ACT(cks_all[:, gi0:gi0 + G, :], pb[5][:, 64:64 + nl].rearrange("p (b h) -> p b h", h=NH), AF.Copy,
                        [pbk[5]], [("cks", gi0 + b) for b in range(G)])
                    CP("dve", run[0:1, :], Rrow[0:1, G, :], ["Rrow"], ["run"])
                ACT(runs[0:1, s_, :], run[0:1, :], AF.Copy, ["run"], [("runs", s_)])
            rk = [("runs", 0), ("runs", 1)]
            MM(pb[5][0:64, 128:128 + NH], UtriS[:, :], lfn[:, :], True, False, ["UtriS", "lfn"], [pbk[5]])
            MM(pb[5][0:64, 128:128 + NH], onesAB[0:1, 0, :], runs[0:1, 0, :], False, False, ["onesAB"] + rk, [pbk[5]])
            MM(pb[5][0:64, 128:128 + NH], onesAB[0:1, 1, :], runs[0:1, 1, :], False, True, ["onesAB"] + rk, [pbk[5]])
            ACT(ckn[:, :], pb[5][0:64, 128:128 + NH], AF.Copy, [pbk[5]], ["ckn"])
            for s_ in range(2):
                MM(pb[5][0:1, 64:64 + NH], colAB[:, s_:s_ + 1], lfn[:, :], True, False, ["colAB", "lfn"], [pbk[5]])
                MM(pb[5][0:1, 64:64 + NH], ones_f[0:1, 0:1], runs[0:1, s_, :], False, True, ["ones_f"] + rk, [pbk[5]])
                ACT(runf[0:1, s_, :], pb[5][0:1, 64:64 + NH], AF.Copy, [pbk[5]], [("runf", s_)])
                MM(pb[5][:, 192:192 + NH], ones_f[0:1, :], runf[0:1, s_, :], True, True, ["ones_f", ("runf", s_)], [pbk[5]])
                ACT(Cb128s[:, s_, :], pb[5][:, 192:192 + NH], AF.Copy, [pbk[5]], [("Cb128s", s_)])
                STT("dve", bias_s[:, s_ * NPB:(s_ + 1) * NPB, :], cks_all[:, s_ * NPB:(s_ + 1) * NPB, :], -1.0,
                    Cb128s[:, s_, :].unsqueeze(1).to_broadcast([128, NPB, NH]), ALU.mult, ALU.add,
                    [("cks", s_ * NPB + jb) for jb in range(NPB)] + [("Cb128s", s_)], [("bias_s", s_)])
            rf = [("runf", 0), ("runf", 1)]
            MM(pb[5][0:64, 256:256 + NH], onesAB[0:1, 0, :], runf[0:1, 0, :], True, False, ["onesAB"] + rf, [pbk[5]])
            MM(pb[5][0:64, 256:256 + NH], onesAB[0:1, 1, :], runf[0:1, 1, :], False, True, ["onesAB"] + rf, [pbk[5]])
            TT("dve", lft[:64, :], ckn[:64, :], pb[5][0:64, 256:256 + NH], ALU.subtract, ["ckn", pbk[5]], ["lft"])
            aug_hilo(64, lft[:64, :], "lft")
            aug_transposes(64, qts[:, :, :], "qts")
            TS("dve", ckn[:64, :], lft[:64, :], -1.0, None, ALU.mult, None, ["lft", "ckn"], ["bnew"])

        def l1b_sample():
            pacc = pb[4]
            sm = cbuf[2]
            MS("dve", kaug[:, :, 64:66], 1.0, ["kaug"])
            for s_ in range(2):
                q0, q1 = s_ * 32, (s_ + 1) * 32
                for jb in range(NPB):
                    b2 = jb % 2
                    P.dma("sp", h[:, b2, :], cache_k[s_, jb * 128:(jb + 1) * 128, :], w=[("h", b2)])
                    P.dma("sp", vbuf[:, b2, :], cache_v[s_, jb * 128:(jb + 1) * 128, :], w=[("vbuf", b2)])
                    ACT(kaug[:, :, 0:HD], h[:, b2, :].rearrange("p (h d) -> p h d", d=HD), AF.Copy, [("h", b2), "kaug"], ["kaug"])
                    aug_transposes(128, ktT[:, :, 0:128], ("ktT", 0))
                    CP("dve", vxt[:, :, 0, 0:HD], vbuf[:, b2, :].rearrange("p (h d) -> p h d", d=HD), [("vbuf", b2), "vxt"], ["vxt"])
                    sbank = jb % 2
                    for hh in range(NH):
                        MM(pb[sbank][:, hh * 32:(hh + 1) * 32], ktT[:, hh, 0:128], qts[:, hh, q0:q1], True, True,
                           [("ktT", 0), "qts"], [pbk[sbank]])
                    TT("dve", sm[:, :].rearrange("p (h q) -> p h q", h=NH), pb[sbank][:, :].rearrange("p (h q) -> p h q", h=NH),
                       bias_s[:, s_ * NPB + jb, :].unsqueeze(2).to_broadcast([128, NH, 32]), ALU.add,
                       [pbk[sbank], ("bias_s", s_)], ["sm"])
                    slot = jb % 4
                    pT = hnT[:, slot, :]
                    ACT(pT[:, :], sm[:, :], AF.Exp, ["sm"], [("pT", slot)])
                    for hh in range(NH):
                        MM(pacc[0:65, hh * 32:(hh + 1) * 32], vxt[:, hh, 0, :], pT[:, hh * 32:(hh + 1) * 32], jb == 0, False,
                           ["vxt", ("pT", slot)], [pbk[4]])
                for hh in range(NH):
                    MM(pb[2][q0:q1, hh * 32:(hh + 1) * 32], ktn[:, hh, q0:q1], qts[:, hh, q0:q1], True, True, ["ktn", "qts"], [pbk[2]])
                smv = sm[q0:q1, :].rearrange("p (h q) -> p h q", h=NH)
                TT("dve", smv, pb[2][q0:q1, :].rearrange("p (h q) -> p h q", h=NH),
                   ckn[q0:q1, :].unsqueeze(2).to_broadcast([32, NH, 32]), ALU.add, [pbk[2], "bnew"], ["sm"])
                TT("dve", smv, smv, As64[q0:q1, :, :], ALU.add, ["sm", "As64"], ["sm"])
                pT = hnT[:, 4 + s_, :]
                ACT(pT[q0:q1, :], sm[q0:q1, :], AF.Exp, ["sm"], [("pT", 4 + s_)])
                for hh in range(NH):
                    MM(pacc[0:65, hh * 32:(hh + 1) * 32], vxn[q0:q1, hh, :], pT[q0:q1, hh * 32:(hh + 1) * 32], False, True,
                       ["vxn", ("pT", 4 + s_)], [pbk[4]])
                RCP(rr[64:65, :], pacc[64:65, :], [pbk[4]], ["rr"])
                MM(pb[5][0:64, :], ones_f[64:65, 0:64], rr[64:65, :], True, True, ["ones_f", "rr"], [pbk[5]])
                ACT(bcs[0:64, :], pb[5][0:64, :], AF.Copy, [pbk[5]], ["bcs"])
                TT("dve", atts[:, :, q0:q1], pacc[0:64, :].rearrange("p (h q) -> p h q", h=NH),
                   bcs[0:64, :].rearrange("p (h q) -> p h q", h=NH), ALU.mult, [pbk[4], "bcs"], [("atts", s_)])
            a4 = atts[:, :, :].rearrange("p (k two) q -> p k two q", two=2)
            CP("dve", atts_p[0:64, :, :], a4[:, :, 0, :], [("atts", 0), ("atts", 1)], ["atts_p"])
            for k in range(NH // 2):
                MM(pb[5][:, k * 64:(k + 1) * 64], Shiftm[0:64, :], atts[0:64, 2 * k + 1, :], True, True,
                   ["Shiftm", ("atts", 0), ("atts", 1)], [pbk[5]])
            CP("dve", atts_p[64:128, :, :], pb[5][64:128, :].rearrange("p (k q) -> p k q", k=NH // 2), [pbk[5], "atts_p"],
               ["atts_p"])

        def l1c_sample():
            P.dma("sp", h[:64, 0, :], h1ss, r=["h1ss"], w=[("h", 0)])
            for s_ in range(2):
                for rr_ in range(2):
                    P.dma("sp", carry[1][:, :, s_, rr_], cache_conv[1, s_, rr_].rearrange("(c p) -> p c", p=128),
                          w=[("carry", 1, c) for c in range(44)], slow=True)
            for q in range(4):
                wt, wkk = wpiece(("wo", q))
                wv_ = wt[:, :].rearrange("p (k n) -> p k n", k=8)
                for k in range(8):
                    MM(pb[q][:64, 0:256], atts_p[:, k, :], wv_[:, k, :], k == 0, k == 7, [wkk, "atts_p"], [pbk[q]])
                TT("dve", h[:64, 0, q * 256:(q + 1) * 256], h[:64, 0, q * 256:(q + 1) * 256], pb[q][:64, 0:256], ALU.add,
                   [pbk[q], ("h", 0)], [("h", 0)])
            norm_T(h[:64, 0, :], 64, 4, hnT, 0, [("h", 0)], "hnT")
            ffn(1, 64, 1, 2, False)
            P.dma("pool", y_s, h[:64, 0, :], r=[("h", 0)])
            for s_ in range(2):
                for rr_ in range(2):
                    P.dma("pool", conv_s[1, s_, rr_].rearrange("(c p) -> p c", p=128), carry[1][:, :, s_, rr_],
                          r=[("carry", 1, c) for c in range(44)], slow=True)

        for l in range(2):
            MS("dve", carry[l][:], 0.0, [("carry", l, c) for c in range(44)])
        for t in range(2 * NTH):
            l0_tile(xp[t * 512:(t + 1) * 512, :], 128, 4, 1, t, False)
        for rr_ in range(2):
            P.dma("pool", conv_p[0, rr_].rearrange("(c p) -> p c", p=128), carry[0][:, :, 0, rr_],
                  r=[("carry", 0, c) for c in range(44)], slow=True)
        for s in range(2):
            for rr_ in range(2):
                P.dma("sp", carry[0][:, :, s, rr_], cache_conv[0, s, rr_].rearrange("(c p) -> p c", p=128),
                      w=[("carry", 0, c) for c in range(44)], slow=True)
        l0_tile(xs, 64, 1, 2, 0, True)
        CP("dve", ktn[:, :, :], ktT[:, :, 0:64], [("ktT", 0)], ["ktn"])
        CP("dve", vxn[:, :, :], vxt[:64, :, 0, :], ["vxt"], ["vxn"])
        P.barrier()
        for s in range(2):
            for rr_ in range(2):
                P.dma("pool", conv_s[0, s, rr_].rearrange("(c p) -> p c", p=128), carry[0][:, :, s, rr_],
                      r=[("carry", 0, c) for c in range(44)], slow=True)
        cko_prepass()
        for i in range(NTH + 1):
            l1a_tile(i)
        if "s1" not in skip:
            l1a_sample()
        P.barrier()
        build_bias()
        for hh in range(NH):
            l1b_head(hh)
        P.barrier()
        if "s1" not in skip:
            l1b_sample()
        P.barrier()
        l1c_tile(NTH)
        for i in range(NTH):
            l1c_tile(i)
        for rr_ in range(2):
            P.dma("pool", conv_p[1, rr_].rearrange("(c p) -> p c", p=128), carry[1][:, :, 0, rr_],
                  r=[("carry", 1, c) for c in range(44)], slow=True)
        if "s1" not in skip:
            l1c_sample()

        P.emit(nc, st)
    return nc


WNAMES = ['norm_mix', 'norm_ffn', 'a_w_in', 'a_ln_g', 'a_ln_b', 'a_w_s', 'a_b_s', 'a_w_out', 'f_w_up', 'f_conv_w',
          'f_conv_b', 'f_w_down', 'kv_norm', 'w_k', 'w_v', 'k_norm_g', 'w_f', 'b_f', 'b_w_q', 'q_norm_g', 'b_w_o']


def make_in_maps(inp, NTH=8, PAST=4096):
    SEQ = 2 * NTH * 512
    f32 = lambda a: np.ascontiguousarray(np.asarray(a, dtype=np.float32))
    wts = {k: f32(inp[k]) for k in WNAMES}
    maps = []
    for c in range(8):
        b, r = c // 2, c % 2
        m = dict(wts)
        m["xp"] = f32(inp["x_prompt"][b, :SEQ])
        m["xs"] = f32(inp["x_sample"][2 * c:2 * c + 2]).reshape(64, D)
        m["cache_k"] = f32(inp["cache_k"][2 * c:2 * c + 2, :PAST]).reshape(2, PAST, D)
        m["cache_v"] = f32(inp["cache_v"][2 * c:2 * c + 2, :PAST]).reshape(2, PAST, D)
        m["cache_lf"] = f32(inp["cache_logf"][2 * c:2 * c + 2, :PAST])
        m["cache_conv"] = f32(inp["cache_ffn_conv"][:, 2 * c:2 * c + 2])
        ccv = np.zeros((128, 4), np.float32)
        ccv[:, 0] = 1.0 - r
        ccv[:, 1] = float(r)
        ccv[:, 2] = NEG * (1.0 - r)
        ccv[:, 3] = float(r)
        m["cc"] = ccv
        maps.append(m)
    return maps


_NC_CACHE = {}


def kernel(**inputs):
    if "nc" not in _NC_CACHE:
        _NC_CACHE["nc"] = build()
    nc = _NC_CACHE["nc"]
    in_maps = make_in_maps(inputs)
    res = run_bass_kernel_spmd(nc, in_maps, core_ids=list(range(8))).results
    B, S, H = 4, 8192, 4096
    y_prompt = np.zeros((B, S, D), np.float32)
    conv_p = np.zeros((2, B, 2, 2 * DFF), np.float32)
    k_p = np.zeros((B, S, D), np.float32)
    v_p = np.zeros((B, S, D), np.float32)
    lf_p = np.zeros((B, S, NH), np.float32)
    y_s = np.zeros((16, 32, D), np.float32)
    sgu = np.zeros((1, 16, 32, D), np.float32)
    conv_s = np.zeros((2, 16, 2, 2 * DFF), np.float32)
    k_s = np.zeros((16, 32, D), np.float32)
    v_s = np.zeros((16, 32, D), np.float32)
    lf_s = np.zeros((16, 32, NH), np.float32)
    for c in range(8):
        b, r = c // 2, c % 2
        o = res[c]
        y_prompt[b, r * H:(r + 1) * H] = o["y_p"]
        if r == 1:
            conv_p[0, b] = o["conv_p"][0]
            conv_p[1, b] = o["conv_p"][1]
            k_p[b] = o["k_p"]
            v_p[b] = o["v_p"]
            lf_p[b] = o["lf_p"]
        y_s[2 * c:2 * c + 2] = o["y_s"].reshape(2, 32, D)
        sgu[0, 2 * c:2 * c + 2] = o["sguv_s"].reshape(2, 32, D)
        conv_s[:, 2 * c:2 * c + 2] = o["conv_s"]
        k_s[2 * c:2 * c + 2] = o["k_s"].reshape(2, 32, D)
        v_s[2 * c:2 * c + 2] = o["v_s"].reshape(2, 32, D)
        lf_s[2 * c:2 * c + 2] = o["lf_s"].reshape(2, 32, NH)
    return (y_prompt, y_s, sgu, conv_p, conv_s,
            k_p.reshape(B, S, NH, HD), v_p.reshape(B, S, NH, HD), lf_p,
            k_s.reshape(16, 32, NH, HD), v_s.reshape(16, 32, NH, HD), lf_s)
```

```python
import numpy as np
from contextlib import ExitStack
import concourse.bass as bass
import concourse.mybir as mybir
from concourse.bass_utils import run_bass_kernel_spmd

F32 = mybir.dt.float32
BF16 = mybir.dt.bfloat16
ALU = mybir.AluOpType
AF = mybir.ActivationFunctionType
AX = mybir.AxisListType

D = 1024
DFF = 2816
NM = 22
NH = 16
HD = 64
EPS = 1e-6
NEG = -30000.0

EPOCH = 30000
NDMASEM = 12


class Op:
    __slots__ = ("eng", "fn", "dma", "deps", "has_dep", "sem", "val", "prewait")

    def __init__(self, eng, fn, dma):
        self.eng = eng
        self.fn = fn
        self.dma = dma
        self.deps = ()
        self.has_dep = False
        self.sem = None
        self.val = None
        self.prewait = None


class Prog:
    def __init__(self):
        self.ops = []
        self.last_w = {}
        self.readers = {}
        self.dma_ops = []
        self.bar_dma = 0

    def add(self, eng, fn, r=(), w=(), dma=False):
        op = Op(eng, fn, dma)
        deps = set()
        for k in r:
            o = self.last_w.get(k)
            if o is not None:
                deps.add(o)
        for k in w:
            o = self.last_w.get(k)
            if o is not None:
                deps.add(o)
            for o in self.readers.get(k, ()):
                deps.add(o)
        for k in w:
            self.last_w[k] = op
            self.readers[k] = []
        for k in r:
            self.readers.setdefault(k, []).append(op)
        deps.discard(op)
        if eng == "pe" and not dma:
            deps = {d for d in deps if not (d.eng == "pe" and not d.dma)}
        op.deps = deps
        for d in deps:
            d.has_dep = True
        self.ops.append(op)
        if dma:
            self.dma_ops.append(op)
        return op

    def barrier(self):
        last = {}
        for op in self.ops:
            if not op.dma and op.fn is not None:
                last[op.eng] = op
        deps = set(last.values()) | set(self.dma_ops[self.bar_dma:])
        self.bar_dma = len(self.dma_ops)
        for e in ["pe", "act", "dve", "pool", "sp"]:
            op = Op(e, None, False)
            op.deps = set(deps)
            for d in deps:
                d.has_dep = True
            self.ops.append(op)

    def dma(self, eng, out, in_, r=(), w=(), slow=False):
        if slow:
            return self.add(eng, lambda e: e.dma_start(out=out, in_=in_, allow_slow_non_contiguous=True),
                            r=r, w=w, dma=True)
        return self.add(eng, lambda e: e.dma_start(out=out, in_=in_), r=r, w=w, dma=True)

    def emit(self, nc, stack):
        engs = ["pe", "act", "dve", "pool", "sp"]
        per = {e: [] for e in engs}
        for op in self.ops:
            per[op.eng].append(op)
        sems = {}

        def getsem(name):
            if name not in sems:
                sems[name] = stack.enter_context(nc.semaphore(name))
            return sems[name]

        for e in engs:
            cnt = 0
            ndma = 0
            for op in per[e]:
                if op.dma:
                    slot = ndma % NDMASEM
                    rnd = ndma // NDMASEM
                    op.sem = getsem(f"d_{e}_{slot}")
                    op.val = 16 * (rnd + 1)
                    op.prewait = (op.sem, 16 * rnd) if rnd > 0 else None
                    ndma += 1
                elif op.has_dep:
                    ep = cnt // EPOCH
                    op.sem = getsem(f"c_{e}_{ep}")
                    op.val = cnt % EPOCH + 1
                    cnt += 1
        final_waits = {}
        for op in self.dma_ops:
            k = id(op.sem)
            if k not in final_waits or final_waits[k][1] < op.val:
                final_waits[k] = (op.sem, op.val)
        block = stack.enter_context(nc.Block())

        def run(ename, engine):
            seen = {}

            def wait(sem, val):
                k = id(sem)
                if seen.get(k, 0) >= val:
                    return
                seen[k] = val
                engine.wait_ge(sem, val)

            for op in per[ename]:
                for d in op.deps:
                    wait(d.sem, d.val)
                if op.prewait is not None:
                    wait(*op.prewait)
                if op.fn is None:
                    continue
                ins = op.fn(engine)
                if op.dma:
                    ins.then_inc(op.sem, 16)
                elif op.has_dep:
                    ins.then_inc(op.sem, 1)
            if ename == "sp":
                for sem, val in final_waits.values():
                    wait(sem, val)

        @block.tensor
        def _(eng):
            run("pe", eng)

        @block.scalar
        def _(eng):
            run("act", eng)

        @block.vector
        def _(eng):
            run("dve", eng)

        @block.gpsimd
        def _(eng):
            run("pool", eng)

        @block.sync
        def _(eng):
            run("sp", eng)


def build(NTH=8, PAST=4096, dbg=False, skip=()):
    SEQ = 2 * NTH * 512
    NBH = NTH * 4
    NB = 2 * NBH
    NOWN = NTH * 512 + 128
    NPB = PAST // 128
    nc = bass.Bass("TRN2", target_bir_lowering=False)
    P = Prog()

    def din(name, shape, dt=F32):
        return nc.dram_tensor(name, list(shape), dt, kind="ExternalInput").ap()

    def dout(name, shape, dt=F32):
        return nc.dram_tensor(name, list(shape), dt, kind="ExternalOutput").ap()

    def dscr(name, shape, dt):
        return nc.dram_tensor(name, list(shape), dt, kind="Internal").ap()

    xp = din("xp", [SEQ, D])
    xs = din("xs", [64, D])
    cache_k = din("cache_k", [2, PAST, D])
    cache_v = din("cache_v", [2, PAST, D])
    cache_lf = din("cache_lf", [2, PAST, NH])
    cache_conv = din("cache_conv", [2, 2, 2, 2 * DFF])
    cc_in = din("cc", [128, 4])
    norm_mix = din("norm_mix", [2, D])
    norm_ffn = din("norm_ffn", [2, D])
    a_w_in = din("a_w_in", [1, D, 2 * D])
    a_ln_g = din("a_ln_g", [1, D])
    a_ln_b = din("a_ln_b", [1, D])
    a_w_s = din("a_w_s", [1, 4, 128, 128])
    a_b_s = din("a_b_s", [1, 4, 128])
    a_w_out = din("a_w_out", [1, D, D])
    f_w_up = din("f_w_up", [2, D, 2 * DFF])
    f_conv_w = din("f_conv_w", [2, 3, 2 * DFF])
    f_conv_b = din("f_conv_b", [2, 2 * DFF])
    f_w_down = din("f_w_down", [2, DFF, D])
    kv_norm = din("kv_norm", [D])
    w_k = din("w_k", [D, D])
    w_v = din("w_v", [D, D])
    k_norm_g = din("k_norm_g", [HD])
    w_f = din("w_f", [D, NH])
    b_f = din("b_f", [NH])
    b_w_q = din("b_w_q", [1, D, D])
    q_norm_g = din("q_norm_g", [1, HD])
    b_w_o = din("b_w_o", [1, D, D])
    y_p = dout("y_p", [NTH * 512, D])
    y_s = dout("y_s", [64, D])
    sguv_s = dout("sguv_s", [64, D])
    conv_p = dout("conv_p", [2, 2, 2 * DFF])
    conv_s = dout("conv_s", [2, 2, 2, 2 * DFF])
    k_p = dout("k_p", [SEQ, D])
    v_p = dout("v_p", [SEQ, D])
    lf_p = dout("lf_p", [SEQ, NH])
    k_s = dout("k_s", [64, D])
    v_s = dout("v_s", [64, D])
    lf_s = dout("lf_s", [64, NH])
    NPIECE = 8 + 4 + 4 + 4 + 4 + 4 + 2 * 22 + 2 * 11
    WS = dscr("WS", [NPIECE, 128, 2048], BF16)
    h1s = dscr("h1s", [SEQ, D], F32)
    KTs = dscr("KTs", [NH, 66, SEQ], BF16)
    VXs = dscr("VXs", [NH, 128, NB, 65], BF16)
    QTs = dscr("QTs", [NH, 66, NOWN], BF16)
    ATs = dscr("ATs", [NH // 2, 128, NOWN], BF16)

    with ExitStack() as st:
        def sb(name, shape, dt):
            return st.enter_context(nc.sbuf_tensor(name, list(shape), dt))

        def ps(name, shape, dt):
            return st.enter_context(nc.psum_tensor(name, list(shape), dt))

        HNALL = [("hnT", 0), ("hnT", 1), ("hnT", 2), ("hnT", 3)]

        def MM(out, lhsT, rhs, start, stop, r, w):
            P.add("pe", lambda e: e.matmul(out=out, lhsT=lhsT, rhs=rhs, start=start, stop=stop), r=r, w=w)

        def TR(out, in_, ident, r, w):
            P.add("pe", lambda e: e.transpose(out=out, in_=in_, identity=ident), r=r, w=w)

        def ACT(out, in_, func, r, w, scale=None, bias=None, accum=None):
            kw = {}
            if scale is not None:
                kw["scale"] = scale
            if bias is not None:
                kw["bias"] = bias
            if accum is not None:
                kw["accum_out"] = accum
            P.add("act", lambda e: e.activation(out=out, in_=in_, func=func, **kw), r=r, w=w)

        def TT(eng, out, in0, in1, op, r, w):
            P.add(eng, lambda e: e.tensor_tensor(out=out, in0=in0, in1=in1, op=op), r=r, w=w)

        def TS(eng, out, in0, s1, s2, op0, op1, r, w):
            if s2 is None:
                P.add(eng, lambda e: e.tensor_scalar(out=out, in0=in0, scalar1=s1, scalar2=None, op0=op0), r=r, w=w)
            else:
                P.add(eng, lambda e: e.tensor_scalar(out=out, in0=in0, scalar1=s1, scalar2=s2, op0=op0, op1=op1),
                      r=r, w=w)

        def STT(eng, out, in0, scalar, in1, op0, op1, r, w):
            P.add(eng, lambda e: e.scalar_tensor_tensor(out=out, in0=in0, scalar=scalar, in1=in1, op0=op0, op1=op1),
                  r=r, w=w)

        def CP(eng, out, in_, r, w):
            P.add(eng, lambda e: e.tensor_copy(out=out, in_=in_), r=r, w=w)

        def MS(eng, ap, val, w):
            P.add(eng, lambda e: e.memset(ap, val), w=w)

        def RCP(out, in_, r, w):
            P.add("dve", lambda e: e.reciprocal(out=out, in_=in_), r=r, w=w)

        psA = ps("psA", [128, 1024], F32)
        psB = ps("psB", [128, 1024], F32)
        pb = [psA[:, 0:512], psA[:, 512:1024], psB[:, 0:512], psB[:, 512:1024],
              ps("pb4", [128, 512], F32)[:, :], ps("pb5", [128, 512], F32)[:, :]]
        psC_t = ps("psC", [128, 2048], BF16)
        ptb = [psC_t[:, 0:1024], psC_t[:, 1024:2048]]
        psC = psC_t[:, :].bitcast(F32)
        pbk = [("pb", i) for i in range(6)]
        ptk = [("ptb", i) for i in range(2)]

        identf = sb("identf", [128, 128], F32)
        identb = sb("identb", [128, 128], BF16)
        epsc = sb("epsc", [128, 1], F32)
        onec = sb("onec", [128, 1], F32)
        cc = sb("cc_sb", [128, 4], F32)
        Utri = sb("Utri", [128, 128], F32)
        sel127 = sb("sel127", [128, 128], F32)
        ones_f = sb("ones_f", [128, 128], F32)
        gcol = sb("gcol", [128, 5, 8], F32)
        cwT = sb("cwT", [128, 2, 4, 44], F32)
        lng_bc = sb("lng_bc", [128, D], F32)
        lnb_bc = sb("lnb_bc", [128, D], F32)
        bs_bc = sb("bs_bc", [128, 8, 128], F32)
        bs_bc_s = sb("bs_bc_s", [128, 8, 64], F32)
        wsT = sb("wsT", [128, 4, 128], BF16)
        wsT_s = sb("wsT_s", [64, 4, 64], BF16)
        gk_bc = sb("gk_bc", [128, HD], F32)
        gq_bc = sb("gq_bc", [128, HD], F32)
        bf_bc = sb("bf_bc", [128, NH], F32)
        wf_b = sb("wf_b", [128, 8, NH], BF16)
        Af = sb("Af", [128, 5, 512], F32)
        Atri = sb("Atri", [128, 128], F32)
        As64 = sb("As64", [64, NH, 32], F32)
        junk = sb("junk", [128, D], F32)
        vtmp = sb("vtmp", [128, D], F32)
        As2 = junk[0:64, 0:NH * 32].rearrange("p (h q) -> p h q", h=NH)
        ck_all = sb("ck_all", [128, NB, NH], F32)
        run = sb("run", [1, NH], F32)
        small = sb("small", [128, 64], F32)
        stg = sb("stg", [128, 128], F32)
        stg2 = sb("stg2", [128, 128], F32)

        MS("pool", epsc[:], EPS, ["epsc"])
        MS("pool", onec[:], 1.0, ["onec"])
        MS("pool", ones_f[:], 1.0, ["ones_f"])
        MS("pool", identf[:], 0.0, ["identf"])
        P.add("pool", lambda e: e.affine_select(out=identf[:], in_=identf[:], compare_op=ALU.not_equal, fill=1.0, base=0,
                                                pattern=[[-1, 128]], channel_multiplier=1), r=["identf"], w=["identf"])
        CP("pool", identb[:], identf[:], ["identf"], ["identb"])
        MS("pool", Utri[:], 1.0, ["Utri"])
        P.add("pool", lambda e: e.affine_select(out=Utri[:], in_=Utri[:], compare_op=ALU.is_ge, fill=0.0, base=0,
                                                pattern=[[1, 128]], channel_multiplier=-1), r=["Utri"], w=["Utri"])
        MS("pool", sel127[:], 1.0, ["sel127"])
        P.add("pool", lambda e: e.affine_select(out=sel127[:], in_=sel127[:], compare_op=ALU.is_ge, fill=0.0, base=-127,
                                                pattern=[[0, 128]], channel_multiplier=1), r=["sel127"], w=["sel127"])
        P.dma("sp", cc[:], cc_in, w=["cc"])
        P.dma("sp", lng_bc[:], a_ln_g[0].partition_broadcast(128), w=["lng_bc"])
        P.dma("sp", lnb_bc[:], a_ln_b[0].partition_broadcast(128), w=["lnb_bc"])
        P.dma("sp", gk_bc[:], k_norm_g.partition_broadcast(128), w=["gk_bc"])
        P.dma("sp", gq_bc[:], q_norm_g[0].partition_broadcast(128), w=["gq_bc"])
        TS("dve", gq_bc[:], gq_bc[:], 0.125, None, ALU.mult, None, ["gq_bc"], ["gq_bc"])
        P.dma("sp", bf_bc[:], b_f.partition_broadcast(128), w=["bf_bc"])
        for g in range(4):
            for cc_ in range(2):
                P.dma("sp", bs_bc[:, 2 * g + cc_, :], a_b_s[0, g].partition_broadcast(128), w=["bs_bc"])
                for s in range(2):
                    P.dma("sp", bs_bc_s[:, 2 * g + cc_, s * 32:(s + 1) * 32], a_b_s[0, g, 0:32].partition_broadcast(128),
                          w=["bs_bc_s"])
        MS("pool", Af[:], 0.0, ["Af"])
        for j in range(4):
            P.add("pool", lambda e, j=j: e.affine_select(out=Af[:, j, :], in_=Af[:, j, :], compare_op=ALU.is_ge, fill=NEG,
                                                         base=-128 * j, pattern=[[1, 512]], channel_multiplier=-1),
                  r=["Af"], w=["Af"])
        CP("pool", Atri[:], Af[:, 0, 0:128], ["Af"], ["Atri"])
        MS("pool", Af[:, 4, :], NEG, ["Af"])
        TS("pool", Af[:], Af[:], cc[:, 0:1], None, ALU.mult, None, ["Af", "cc"], ["Af"])
        MS("pool", As64[:], 0.0, ["As64"])
        P.add("pool", lambda e: e.affine_select(out=As64[:], in_=As64[:], compare_op=ALU.is_ge, fill=NEG, base=0,
                                                pattern=[[0, NH], [1, 32]], channel_multiplier=-1), r=["As64"], w=["As64"])
        MS("pool", As2, 0.0, ["junk"])
        P.add("pool", lambda e: e.affine_select(out=As2, in_=As2, compare_op=ALU.is_ge, fill=NEG, base=32,
                                                pattern=[[0, NH], [1, 32]], channel_multiplier=-1), r=["junk"], w=["junk"])
        CP("pool", As64[32:64, :, :], As2[32:64, :, :], ["junk", "As64"], ["As64"])

        def rows_to_cols(rows_ap, n, dst, dkey):
            P.dma("sp", stg[:n, :], rows_ap, w=["stg"])
            TR(pb[5][:, 0:n], stg[:n, :], identf[:n, :n], ["stg", "identf"], [pbk[5]])
            ACT(dst, pb[5][:, 0:n], AF.Copy, [pbk[5]], [dkey])

        for n, src in enumerate([norm_mix[0], norm_ffn[0], kv_norm, norm_mix[1], norm_ffn[1]]):
            rows_to_cols(src.rearrange("(k p) -> k p", p=128), 8, gcol[:, n, :], "gcol")
        for l in range(2):
            for t in range(4):
                src = f_conv_w[l, t] if t < 3 else f_conv_b[l]
                rows_to_cols(src.rearrange("(c p) -> c p", p=128), 44, cwT[:, l, t, :], "cwT")
        for g in range(4):
            P.dma("sp", stg[:, :], a_w_s[0, g], w=["stg"])
            TR(pb[5][:, 0:128], stg[:, :], identf[:], ["stg", "identf"], [pbk[5]])
            ACT(stg2[:], pb[5][:, 0:128], AF.Copy, [pbk[5]], ["stg2"])
            MS("pool", stg2[64:128, 0:64], 0.0, ["stg2"])
            CP("dve", wsT[:, g, :], stg2[:], ["stg2"], ["wsT"])
        wss_f = vtmp[0:64, 0:256].rearrange("p (g i) -> p g i", g=4)
        MS("pool", wss_f, 0.0, ["wss_f"])
        for g in range(4):
            for s in range(2):
                P.dma("sp", wss_f[s * 32:(s + 1) * 32, g, s * 32:(s + 1) * 32],
                      a_w_s[0, g, 0:32, 0:32].rearrange("i j -> j i"), r=["wss_f"], w=[("wss_f", g, s)], slow=True)
        CP("dve", wsT_s[:], wss_f, [("wss_f", g, s) for g in range(4) for s in range(2)], ["vtmp", "wsT_s"])

        big = sb("big", [128, 8192], F32)
        hidT_flat = sb("hidT", [128, NM * 512], BF16)
        cvf = [big[:, 4096 + i * 2048:4096 + (i + 1) * 2048] for i in range(2)]
        cvb = [hidT_flat[:, i * 2048:(i + 1) * 2048] for i in range(2)]
        piece_idx = {}
        npc = [0]

        def convert(name, src_ap, shape, nparts=128):
            i = npc[0]
            npc[0] += 1
            piece_idx[name] = i
            b = i % 2
            fv = cvf[b][:nparts, :]
            bv = cvb[b][:nparts, :]
            kf = [("vbuf", 2 * b), ("vbuf", 2 * b + 1)]
            kb = [("hidT", 4 * b + j) for j in range(4)]
            if len(shape) == 2:
                fv = fv.rearrange("p (a b) -> p a b", a=shape[0])
                bv = bv.rearrange("p (a b) -> p a b", a=shape[0])
            elif len(shape) == 3:
                fv = fv.rearrange("p (a b c) -> p a b c", a=shape[0], b=shape[1])
                bv = bv.rearrange("p (a b c) -> p a b c", a=shape[0], b=shape[1])
            P.dma("sp", fv, src_ap, w=kf)
            eng = "dve" if i % 2 == 0 else "act"
            if eng == "act":
                ACT(cvb[b][:nparts, :], cvf[b][:nparts, :], AF.Copy, kf, kb)
            else:
                CP("dve", cvb[b][:nparts, :], cvf[b][:nparts, :], kf, kb)
            P.dma("pool", WS[i, :nparts, :], cvb[b][:nparts, :], r=kb, w=[("WS", i)])

        def w1024(name, W, ncol):
            v = W.rearrange("(k p) n -> p k n", p=128)
            for q in range(ncol // 256):
                convert((name, q), v[:, :, q * 256:(q + 1) * 256], [8, 256])

        w1024("win", a_w_in[0], 2048)
        w1024("wout", a_w_out[0], 1024)
        w1024("wk", w_k, 1024)
        w1024("wv", w_v, 1024)
        w1024("wq", b_w_q[0], 1024)
        w1024("wo", b_w_o[0], 1024)
        for l in range(2):
            upv = f_w_up[l].rearrange("(k p) (gv m j) -> p gv m k j", p=128, gv=2, m=NM, j=128)
            for m in range(NM):
                convert(("up", l, m), upv[:, :, m], [2, 8, 128])
            dnv = f_w_down[l].rearrange("(m p) n -> p m n", p=128)
            for mp in range(NM // 2):
                convert(("dn", l, mp), dnv[:, 2 * mp:2 * mp + 2, :], [2, 1024])
        assert npc[0] == NPIECE
        P.dma("sp", cvf[0][:, 0:128].rearrange("p (k n) -> p k n", k=8), w_f.rearrange("(k p) n -> p k n", p=128),
              w=[("vbuf", 0), ("vbuf", 1)])
        CP("dve", wf_b[:], cvf[0][:, 0:128].rearrange("p (k n) -> p k n", k=8), [("vbuf", 0), ("vbuf", 1)], ["wf_b"])

        NRING = 4
        ring = [sb(f"ring{i}", [128, 2048], BF16) for i in range(NRING)]
        rcnt = [0]

        def wpiece(name, nparts=128):
            s = rcnt[0] % NRING
            rcnt[0] += 1
            i = piece_idx[name]
            P.dma("sp", ring[s][:nparts, :], WS[i, :nparts, :], r=[("WS", i)], w=[("ring", s)])
            return ring[s], ("ring", s)

        h = big[:, 0:4096].rearrange("p (b d) -> p b d", b=4)
        hnT = sb("hnT", [128, 8, 512], BF16)
        uT = sb("uT", [128, 8, 512], BF16)
        vbuf = big[:, 4096:8192].rearrange("p (b d) -> p b d", b=4)
        vnb = sb("vnb", [128, 4, D], BF16)
        hsb = [sb(f"hsb{i}", [128, D], BF16) for i in range(2)]
        hidT = hidT_flat[:, :].rearrange("p (m t) -> p m t", m=NM)
        vraw = hidT_flat[:, 0:8192].bitcast(F32).rearrange("p (b d) -> p b d", b=4)

        def vrk(b):
            return [("hidT", 4 * b + j) for j in range(4)]
        abuf = [sb(f"abuf{i}", [128, 520], F32) for i in range(4)]
        cbuf = [sb(f"cbuf{i}", [128, 512], F32) for i in range(4)]
        sgb = [sb(f"sgb{i}", [128, 512], F32) for i in range(2)]
        carry = [sb(f"carry{l}", [128, 44, 2, 2], F32) for l in range(2)]
        kaug = sb("kaug", [128, NH, 66], BF16)
        ktT = sb("ktT", [66, NH, 512], BF16)
        vxt = sb("vxt", [128, NH, 4, 65], BF16)
        lfsb = sb("lfsb", [128, NH], F32)
        lf4 = sb("lf4", [128, 4 * NH], F32)
        lnst = sb("lnst", [128, 4, 2], F32)
        tot4 = sb("tot4", [1, 4 * NH], F32)
        Rrow = sb("Rrow", [1, 5, NH], F32)
        lft = sb("lft", [128, NH], F32)
        smc = [0]

        def scol(n=1):
            c = smc[0] % (64 // 4) * 4
            smc[0] += 1
            return small[:, c:c + n], ("small", c)

        MS("pool", kaug[:], 1.0, ["kaug"])
        MS("pool", vxt[:], 1.0, ["vxt"])

        nrm_cnt = [0]

        def norm_tile(srcs, rows, nidx, dst, wkey):
            nb_ = len(srcs)
            ssc, ssk = scol(4)
            for b, (src, rkeys) in enumerate(srcs):
                ACT(junk[:rows, :], src, AF.Square, rkeys, ["junk", ssk], accum=ssc[:rows, b:b + 1])
            ACT(ssc[:rows, 0:nb_], ssc[:rows, 0:nb_], AF.Sqrt, [ssk, "epsc"], [ssk], scale=1.0 / D, bias=epsc[:rows, 0:1])
            RCP(ssc[:rows, 0:nb_], ssc[:rows, 0:nb_], [ssk], [ssk])
            for b, (src, rkeys) in enumerate(srcs):
                hb = nrm_cnt[0] % 2
                nrm_cnt[0] += 1
                ACT(hsb[hb][:rows, :], src, AF.Copy, rkeys + [ssk], [("hsb", hb)], scale=ssc[:rows, b:b + 1])
                pt = ptb[hb]
                for k in range(8):
                    TR(pt[:, k * 128:k * 128 + rows], hsb[hb][:rows, k * 128:(k + 1) * 128], identb[:rows, :rows],
                       [("hsb", hb), "identb"], [ptk[hb]])
                TT("dve", dst[:, :, b * rows:(b + 1) * rows], pt[:, :].rearrange("p (k t) -> p k t", k=8)[:, :, 0:rows],
                   gcol[:, nidx, :].unsqueeze(2).to_broadcast([128, 8, rows]), ALU.mult, [ptk[hb], "gcol"], [(wkey, b)])

        def norm_T(src, rows, nidx, dst, col0, rkeys, wkey):
            assert col0 == 0
            norm_tile([(src, rkeys)], rows, nidx, dst, wkey)

        def proj_tok(wname, nq, rows, nblk, evac, extra_r):
            for q in range(nq):
                wt, wkk = wpiece((wname, q))
                wv_ = wt[:, :].rearrange("p (k n) -> p k n", k=8)
                for b in range(nblk):
                    bank = (q * nblk + b) % 4
                    for k in range(8):
                        MM(pb[bank][:rows, 0:256], hnT[:, k, b * rows:(b + 1) * rows], wv_[:, k, :], k == 0, k == 7,
                           [wkk, ("hnT", b)] + extra_r, [pbk[bank]])
                    evac(b, q, pb[bank][:rows, 0:256], pbk[bank])

        def ffn(l, rows, nblk, nseg, first, halo=False):
            ntok = rows * nblk
            seglen = ntok // nseg
            prev_b = [None]
            for m in range(NM):
                wt, wkk = wpiece(("up", l, m))
                wv_ = wt[:, :].rearrange("p (g k j) -> p g k j", g=2, k=8)
                res = []
                for gv in range(2):
                    c = gv * NM + m
                    bank = (2 * m + gv) % 4
                    for k in range(8):
                        MM(pb[bank][:, 0:ntok], wv_[:, gv, k, :], hnT[:, k, 0:ntok], k == 0, k == 7, [wkk] + HNALL,
                           [pbk[bank]])
                    ab = abuf[bank]
                    abv = ab[:, 0:nseg * (seglen + 2)].rearrange("p (s t) -> p s t", s=nseg)
                    pav = pb[bank][:, 0:ntok].rearrange("p (s t) -> p s t", s=nseg)
                    ck_ = ("carry", l, c)
                    CP("dve", abv[:, :, 0:2], carry[l][:, c, 0:nseg, :], [ck_], [("abufc", bank)])
                    ACT(abv[:, :, 2:], pav, AF.Copy, [pbk[bank]], [("abuf", bank)])
                    CP("dve", carry[l][:, c, 0:nseg, :], abv[:, :, seglen:seglen + 2], [("abuf", bank), ("abufc", bank)], [ck_])
                    if halo:
                        continue
                    cb = cbuf[bank]
                    cbv = cb[:, 0:ntok].rearrange("p (s t) -> p s t", s=nseg)
                    ACT(cbv, pav, AF.Identity, [pbk[bank], "cwT"], [("cbuf", bank)], scale=cwT[:, l, 2, c:c + 1],
                        bias=cwT[:, l, 3, c:c + 1])
                    STT("dve", cbv, abv[:, :, 1:seglen + 1], cwT[:, l, 1, c:c + 1], cbv, ALU.mult, ALU.add,
                        [("abuf", bank), ("abufc", bank), ("cbuf", bank), "cwT"], [("cbuf", bank)])
                    STT("dve", cbv, abv[:, :, 0:seglen], cwT[:, l, 0, c:c + 1], cbv, ALU.mult, ALU.add,
                        [("abuf", bank), ("abufc", bank), ("cbuf", bank), "cwT"], [("cbuf", bank)])
                    res.append((cb, ("cbuf", bank)))
                if halo:
                    continue

                def stage_b(m=m, res=res):
                    sg = sgb[m % 2]
                    ACT(sg[:, 0:ntok], res[0][0][:, 0:ntok], AF.Silu, [res[0][1]], [("sgb", m % 2)])
                    TT("pool", hidT[:, m, 0:ntok], sg[:, 0:ntok], res[1][0][:, 0:ntok], ALU.mult,
                       [("sgb", m % 2), res[1][1]], [("hidT", m)])
                if prev_b[0] is not None:
                    prev_b[0]()
                prev_b[0] = stage_b
            if halo:
                return
            prev_b[0]()
            prev_b[0] = None
            npass = 2 if nblk == 4 else 1
            bpp = nblk // npass
            for ps_ in range(npass):
                for mp in range(NM // 2):
                    wt, wkk = wpiece(("dn", l, mp))
                    wv_ = wt[:, :].rearrange("p (a n) -> p a n", a=2)
                    for bb in range(bpp):
                        b = ps_ * bpp + bb
                        for hf in range(2):
                            bank = bb * 2 + hf
                            for mm in range(2):
                                m = 2 * mp + mm
                                MM(pb[bank][:rows, :], hidT[:, m, b * rows:(b + 1) * rows], wv_[:, mm, hf * 512:(hf + 1) * 512],
                                   m == 0, m == NM - 1, [wkk, ("hidT", m)], [pbk[bank]])
                for bb in range(bpp):
                    b = ps_ * bpp + bb
                    for hf in range(2):
                        bank = bb * 2 + hf
                        TT("dve", h[:rows, b, hf * 512:(hf + 1) * 512], h[:rows, b, hf * 512:(hf + 1) * 512],
                           pb[bank][:rows, :], ALU.add, [pbk[bank], ("h", b)], [("h", b)])

        def l0_tile(xsrc, rows, nblk, nseg, tile_idx, sample):
            ntok = rows * nblk
            for b in range(nblk):
                P.dma("sp", h[:rows, b, :], xsrc[b * rows:(b + 1) * rows, :], w=[("h", b)])
            norm_tile([(h[:rows, b, :], [("h", b)]) for b in range(nblk)], rows, 0, hnT, "hnT")
            for q in range(4):
                wt, wkk = wpiece(("win", 4 + q))
                wv_ = wt[:, :].rearrange("p (k n) -> p k n", k=8)
                for b in range(nblk):
                    bank = (q * nblk + b) % 4
                    for k in range(8):
                        MM(pb[bank][:rows, 0:256], hnT[:, k, b * rows:(b + 1) * rows], wv_[:, k, :], k == 0, k == 7,
                           [wkk, ("hnT", b)], [pbk[bank]])
                    ACT(vbuf[:rows, b, q * 256:(q + 1) * 256], pb[bank][:rows, 0:256], AF.Gelu_apprx_tanh, [pbk[bank]],
                        [("vbuf", b)])
            for b in range(nblk):
                st6 = stg[:, b * 12:(b + 1) * 12].rearrange("p (a b) -> p a b", a=2)
                for c2 in range(2):
                    P.add("dve", lambda e, b=b, c2=c2, st6=st6: e.bn_stats(out=st6[:rows, c2, :], in_=vbuf[:rows, b, c2 * 512:(c2 + 1) * 512]),
                          r=[("vbuf", b)], w=[("st6", b)])
                P.add("dve", lambda e, b=b, st6=st6: e.bn_aggr(out=lnst[:rows, b, :], in_=st6[:rows, :, :]), r=[("st6", b)], w=["lnst"])
            ACT(lnst[:rows, 0:nblk, 1], lnst[:rows, 0:nblk, 1], AF.Sqrt, ["lnst", "epsc"], ["lnst"], bias=epsc[:rows, 0:1])
            RCP(lnst[:rows, 0:nblk, 1], lnst[:rows, 0:nblk, 1], ["lnst"], ["lnst"])
            for b in range(nblk):
                TS("dve", vtmp[:rows, :], vbuf[:rows, b, :], lnst[:rows, b, 0:1], lnst[:rows, b, 1:2], ALU.subtract, ALU.mult,
                   [("vbuf", b), "lnst"], ["vtmp"])
                TT("dve", vtmp[:rows, :], vtmp[:rows, :], lng_bc[:rows, :], ALU.mult, ["vtmp", "lng_bc"], ["vtmp"])
                if sample:
                    TT("dve", vtmp[:rows, :], vtmp[:rows, :], lnb_bc[:rows, :], ALU.add, ["vtmp", "lnb_bc"], ["vtmp"])
                    P.dma("pool", sguv_s, vtmp[:rows, :], r=["vtmp"])
                    ACT(vnb[:rows, b, :], vtmp[:rows, :], AF.Copy, ["vtmp"], [("vnb", b)])
                else:
                    TT("dve", vnb[:rows, b, :], vtmp[:rows, :], lnb_bc[:rows, :], ALU.add, ["vtmp", "lnb_bc"], [("vnb", b)])
            for q in range(4):
                wt, wkk = wpiece(("win", q))
                wv_ = wt[:, :].rearrange("p (k n) -> p k n", k=8)
                for cc_ in range(2):
                    c = q * 2 + cc_
                    bank = c % 4
                    for k in range(8):
                        MM(pb[bank][:, 0:ntok], wv_[:, k, cc_ * 128:(cc_ + 1) * 128], hnT[:, k, 0:ntok], k == 0, k == 7,
                           [wkk] + HNALL, [pbk[bank]])
                    ACT(uT[:, c, 0:ntok], pb[bank][:, 0:ntok], AF.Gelu_apprx_tanh, [pbk[bank]], [("uT", c)])
            for b in range(nblk):
                wst = wsT_s if sample else wsT
                bsb = bs_bc_s if sample else bs_bc
                for c in range(8):
                    bank = 4 + c // 4
                    MM(pb[bank][:, (c % 4) * 128:(c % 4) * 128 + rows], vnb[:rows, b, c * 128:(c + 1) * 128],
                       wst[:rows, c // 2, :rows], True, True, [("vnb", b), "wsT", "wsT_s"], [pbk[bank]])
                for hb in range(2):
                    bank = 4 + hb
                    pv_ = pb[bank][:, :].rearrange("p (c t) -> p c t", c=4)[:, :, 0:rows]
                    tv = junk[:, hb * 512:(hb + 1) * 512].rearrange("p (c t) -> p c t", c=4)[:, :, 0:rows]
                    TT("dve", tv, pv_, bsb[:, hb * 4:(hb + 1) * 4, 0:rows], ALU.add, [pbk[bank], "bs_bc", "bs_bc_s"],
                       ["junk"])
                    uv = uT[:, hb * 4:(hb + 1) * 4, b * rows:(b + 1) * rows]
                    TT("dve", uv, tv, uv, ALU.mult, ["junk"] + [("uT", hb * 4 + c) for c in range(4)],
                       [("uT", hb * 4 + c) for c in range(4)])
            def ev_out(b, q, pap, pk):
                TT("dve", h[:rows, b, q * 256:(q + 1) * 256], h[:rows, b, q * 256:(q + 1) * 256], pap, ALU.add,
                   [pk, ("h", b)], [("h", b)])
            for q in range(4):
                wt, wkk = wpiece(("wout", q))
                wv_ = wt[:, :].rearrange("p (k n) -> p k n", k=8)
                for b in range(nblk):
                    bank = (q * nblk + b) % 4
                    for k in range(8):
                        MM(pb[bank][:rows, 0:256], uT[:, k, b * rows:(b + 1) * rows], wv_[:, k, :], k == 0, k == 7,
                           [wkk, ("uT", k)], [pbk[bank]])
                    ev_out(b, q, pb[bank][:rows, 0:256], pbk[bank])
            norm_tile([(h[:rows, b, :], [("h", b)]) for b in range(nblk)], rows, 1, hnT, "hnT")
            ffn(0, rows, nblk, nseg, tile_idx == 0)
            if sample:
                P.dma("pool", h1ss, h[:rows, 0, :], r=[("h", 0)], w=["h1ss"])
            else:
                for b in range(nblk):
                    P.dma("pool", h1s[tile_idx * 512 + b * 128: tile_idx * 512 + (b + 1) * 128, :], h[:rows, b, :],
                          r=[("h", b)], w=[("h1s", tile_idx * 4 + b)])
            norm_tile([(h[:rows, b, :], [("h", b)]) for b in range(nblk)], rows, 2, hnT, "hnT")

            def ev_k(b, q, pap, pk):
                ACT(vbuf[:rows, b, q * 256:(q + 1) * 256], pap, AF.Copy, [pk], [("vbuf", b)])
            proj_tok("wk", 4, rows, nblk, ev_k, [])
            def ev_v(b, q, pap, pk):
                ACT(vraw[:rows, b, q * 256:(q + 1) * 256], pap, AF.Copy, [pk], vrk(b))
            proj_tok("wv", 4, rows, nblk, ev_v, [])
            for b in range(nblk):
                kr = vbuf[:rows, b, :]
                k3 = kr.rearrange("p (h d) -> p h d", d=HD)
                sc, sck = lft, "lft"
                TT("dve", vtmp[:rows, :], kr, kr, ALU.mult, [("vbuf", b)], ["vtmp"])
                P.add("dve", lambda e, sc=sc: e.tensor_reduce(out=sc[:rows, :], in_=vtmp[:rows, :].rearrange("p (h d) -> p h d", d=HD),
                                                        axis=AX.X, op=ALU.add), r=["vtmp"], w=[sck])
                ACT(sc[:rows, :], sc[:rows, :], AF.Sqrt, [sck, "epsc"], [sck], scale=1.0 / HD, bias=epsc[:rows, 0:1])
                RCP(sc[:rows, :], sc[:rows, :], [sck], [sck])
                TT("dve", k3, k3, sc[:rows, :].unsqueeze(2).to_broadcast([rows, NH, HD]), ALU.mult, [("vbuf", b), sck], [("vbuf", b)])
                TT("dve", k3, k3, gk_bc[:rows, :].unsqueeze(1).to_broadcast([rows, NH, HD]), ALU.mult, [("vbuf", b), "gk_bc"],
                   [("vbuf", b)])
                if sample:
                    P.dma("pool", k_s, kr, r=[("vbuf", b)])
                else:
                    r0 = tile_idx * 512 + b * 128
                    P.dma("pool", k_p[r0:r0 + 128, :], kr, r=[("vbuf", b)])
                ACT(kaug[:rows, :, 0:HD], k3, AF.Copy, [("vbuf", b), "kaug"], ["kaug"])
                for hh in range(NH):
                    pt = ptb[hh // 8]
                    TR(pt[:66, (hh % 8) * 128:(hh % 8) * 128 + rows], kaug[:rows, hh, :], identb[:rows, :rows],
                       ["kaug", "identb"], [ptk[hh // 8]])
                for hb in range(2):
                    ACT(ktT[:, hb * 8:(hb + 1) * 8, b * rows:(b + 1) * rows],
                        ptb[hb][:66, :].rearrange("p (k t) -> p k t", k=8)[:, :, 0:rows], AF.Copy, [ptk[hb]], [("ktT", b)])
            if not sample:
                P.dma("pool", KTs.rearrange("h r s -> r h s")[:, :, tile_idx * 512:(tile_idx + 1) * 512], ktT[:, :, :],
                      r=[("ktT", b) for b in range(4)], w=[("KTs", tile_idx)])

            for b in range(nblk):
                if sample:
                    P.dma("pool", v_s, vraw[:rows, b, :], r=vrk(b))
                else:
                    r0 = tile_idx * 512 + b * 128
                    P.dma("pool", v_p[r0:r0 + 128, :], vraw[:rows, b, :], r=vrk(b))
                ACT(vxt[:rows, :, b, 0:HD], vraw[:rows, b, :].rearrange("p (h d) -> p h d", d=HD), AF.Copy,
                    vrk(b) + ["vxt"], ["vxt"])
            if not sample:
                P.dma("pool", VXs.rearrange("h p n e -> p h n e")[:, :, tile_idx * 4:(tile_idx + 1) * 4, :], vxt[:, :, :, :],
                      r=["vxt"], w=[("VXs", tile_idx)])
            for b in range(nblk):
                for k in range(8):
                    MM(pb[4][:rows, b * NH:(b + 1) * NH], hnT[:, k, b * rows:(b + 1) * rows], wf_b[:, k, :], k == 0, k == 7,
                       [("hnT", b), "wf_b"], [pbk[4]])
            nl = nblk * NH
            l3 = lf4[:rows, 0:nl].rearrange("p (b h) -> p b h", h=NH)
            TT("dve", l3, pb[4][:rows, 0:nl].rearrange("p (b h) -> p b h", h=NH),
               bf_bc[:rows, :].unsqueeze(1).to_broadcast([rows, nblk, NH]), ALU.add, [pbk[4], "bf_bc"], ["lf4"])
            ACT(lf4[:rows, 0:nl], lf4[:rows, 0:nl], AF.Exp, ["lf4"], ["lf4"], scale=-1.0)
            ACT(lf4[:rows, 0:nl], lf4[:rows, 0:nl], AF.Ln, ["lf4", "onec"], ["lf4"], bias=onec[:rows, 0:1])
            TS("dve", lf4[:rows, 0:nl], lf4[:rows, 0:nl], -1.0, None, ALU.mult, None, ["lf4"], ["lf4"])
            if sample:
                P.dma("pool", lf_s, lf4[:rows, 0:NH], r=["lf4"])
                CP("dve", lfn[:, :], lf4[:64, 0:NH], ["lf4"], ["lfn"])
            else:
                r0 = tile_idx * 512
                P.dma("pool", lf_p[r0:r0 + 512, :].rearrange("(b p) h -> p b h", p=128), l3, r=["lf4"])
                if tile_idx == 0:
                    MS("dve", run[:], 0.0, ["run"])
                MM(pb[5][0:1, 0:nl], ones_f[:, 0:1], lf4[:, 0:nl], True, True, ["ones_f", "lf4"], [pbk[5]])
                ACT(tot4[0:1, 0:nl], pb[5][0:1, 0:nl], AF.Copy, [pbk[5]], ["tot4"])
                CP("dve", Rrow[0:1, 0, :], run[0:1, :], ["run"], ["Rrow"])
                for b in range(nblk):
                    TT("dve", Rrow[0:1, b + 1, :], Rrow[0:1, b, :], tot4[0:1, b * NH:(b + 1) * NH], ALU.add, ["Rrow", "tot4"], ["Rrow"])
                for b in range(nblk):
                    MM(pb[5][:, 64 + b * NH:64 + (b + 1) * NH], Utri[:, :], lf4[:, b * NH:(b + 1) * NH], True, False,
                       ["Utri", "lf4"], [pbk[5]])
                    MM(pb[5][:, 64 + b * NH:64 + (b + 1) * NH], ones_f[0:1, :], Rrow[0:1, b, :], False, True,
                       ["ones_f", "Rrow"], [pbk[5]])
                ACT(ck_all[:, tile_idx * 4:tile_idx * 4 + 4, :], pb[5][:, 64:64 + nl].rearrange("p (b h) -> p b h", h=NH), AF.Copy,
                    [pbk[5]], [("ck", tile_idx * 4 + b) for b in range(4)])
                CP("dve", run[0:1, :], Rrow[0:1, nblk, :], ["Rrow"], ["run"])

        GQ = 2 if NTH % 2 == 0 else 1
        NG = NTH // GQ
        NPAIR = sum(NBH + 4 * GQ * (g + 1) for g in range(NG)) + NBH
        assert NPAIR * NH <= 8192
        bias_all = big[:, 0:NPAIR * NH].rearrange("p (n h) -> p n h", h=NH)
        cko = sb("cko", [128, NBH + 1, NH], F32)
        Cb_all = sb("Cb_all", [128, NTH + 1, NH], F32)
        hi_f = sb("hi_f", [128, NH], F32)
        h1ss = dscr("h1ss", [64, D], F32)
        ktn = sb("ktn", [66, NH, 64], BF16)
        vxn = sb("vxn", [64, NH, 65], BF16)
        qts = sb("qts", [66, NH, 64], BF16)
        atts = sb("atts", [64, NH, 64], BF16)
        atts_p = sb("atts_p", [128, NH // 2, 64], BF16)
        Shiftm = sb("Shiftm", [64, 128], BF16)
        MS("pool", Shiftm[:], 0.0, ["Shiftm"])
        CP("pool", Shiftm[0:64, 64:128], identb[0:64, 0:64], ["identb", "Shiftm"], ["Shiftm"])
        lfn = sb("lfn", [64, NH], F32)
        assert 2 * NPB * NH <= D
        cks_all = lng_bc[:, 0:2 * NPB * NH].rearrange("p (n h) -> p n h", h=NH)
        bias_s = lnb_bc[:, 0:2 * NPB * NH].rearrange("p (n h) -> p n h", h=NH)
        ckn = sb("ckn", [64, NH], F32)
        runs = sb("runs", [1, 2, NH], F32)
        runf = sb("runf", [1, 2, NH], F32)
        UtriS = sb("UtriS", [64, 64], F32)
        onesAB = sb("onesAB", [1, 2, 64], F32)
        colAB = sb("colAB", [64, 2], F32)
        CP("pool", UtriS[:, :], Utri[0:64, 0:64], ["Utri"], ["UtriS"])
        MS("pool", UtriS[0:32, 32:64], 0.0, ["UtriS"])
        MS("pool", onesAB[:], 0.0, ["onesAB"])
        MS("pool", onesAB[0:1, 0, 0:32], 1.0, ["onesAB"])
        MS("pool", onesAB[0:1, 1, 32:64], 1.0, ["onesAB"])
        MS("pool", colAB[:], 0.0, ["colAB"])
        MS("pool", colAB[0:32, 0:1], 1.0, ["colAB"])
        MS("pool", colAB[32:64, 1:2], 1.0, ["colAB"])
        Cb128s = sb("Cb128s", [128, 2, NH], F32)

        def load_own(i, b, halo):
            if halo:
                ga = NBH - 1
                P.dma("sp", h[:, b, :], h1s[ga * 128:(ga + 1) * 128, :], r=[("h1s", ga)], w=[("h", b)])
                return
            ga = i * 4 + b
            gb_ = NBH + ga
            P.dma("sp", h[:, b, :], h1s[ga * 128:(ga + 1) * 128, :], r=[("h1s", ga)], w=[("h", b)])
            P.dma("sp", vbuf[:, b, :], h1s[gb_ * 128:(gb_ + 1) * 128, :], r=[("h1s", gb_)], w=[("vbuf", b)])
            TS("dve", h[:, b, :], h[:, b, :], cc[:, 0:1], None, ALU.mult, None, [("h", b), "cc"], [("h", b)])
            STT("dve", h[:, b, :], vbuf[:, b, :], cc[:, 1:2], h[:, b, :], ALU.mult, ALU.add, [("vbuf", b), ("h", b), "cc"],
                [("h", b)])

        def qk_norm_aug(rows, b, gbc, src=None, skeys=None):
            kr = vbuf[:rows, b, :] if src is None else src
            sk = [("vbuf", b)] if skeys is None else skeys
            k3 = kr.rearrange("p (h d) -> p h d", d=HD)
            TT("dve", vtmp[:rows, :], kr, kr, ALU.mult, sk, ["vtmp"])
            P.add("dve", lambda e: e.tensor_reduce(out=lft[:rows, :], in_=vtmp[:rows, :].rearrange("p (h d) -> p h d", d=HD),
                                                   axis=AX.X, op=ALU.add), r=["vtmp"], w=["lft"])
            ACT(lft[:rows, :], lft[:rows, :], AF.Sqrt, ["lft", "epsc"], ["lft"], scale=1.0 / HD, bias=epsc[:rows, 0:1])
            RCP(lft[:rows, :], lft[:rows, :], ["lft"], ["lft"])
            v3 = vtmp[:rows, :].rearrange("p (h d) -> p h d", d=HD)
            TT("dve", v3, k3, lft[:rows, :].unsqueeze(2).to_broadcast([rows, NH, HD]), ALU.mult, sk + ["lft"], ["vtmp"])
            TT("dve", v3, v3, gbc[:rows, :].unsqueeze(1).to_broadcast([rows, NH, HD]), ALU.mult, ["vtmp", "gk_bc", "gq_bc"],
               ["vtmp"])
            ACT(kaug[:rows, :, 0:HD], v3, AF.Copy, ["vtmp", "kaug"], ["kaug"])

        def aug_hilo(rows, crel, crk):
            CP("dve", kaug[:rows, :, 64], crel, [crk, "kaug"], ["kaug"])
            CP("dve", hi_f[:rows, :], kaug[:rows, :, 64], ["kaug"], ["hi_f"])
            TT("dve", kaug[:rows, :, 65], crel, hi_f[:rows, :], ALU.subtract, [crk, "hi_f", "kaug"], ["kaug"])

        def aug_transposes(rows, dst, dkey):
            for hh in range(NH):
                pt = ptb[hh // 8]
                TR(pt[:66, (hh % 8) * 128:(hh % 8) * 128 + rows], kaug[:rows, hh, :], identb[:rows, :rows],
                   ["kaug", "identb"], [ptk[hh // 8]])
            for hb in range(2):
                ACT(dst[:, hb * 8:(hb + 1) * 8, :],
                    ptb[hb][:66, :].rearrange("p (k t) -> p k t", k=8)[:, :, 0:rows], AF.Copy, [ptk[hb]], [dkey])

        def ev_q(rows):
            def f(b, q, pap, pk):
                ACT(vbuf[:rows, b, q * 256:(q + 1) * 256], pap, AF.Copy, [pk], [("vbuf", b)])
            return f

        def l1a_tile(i):
            halo = (i == NTH)
            nblk = 1 if halo else 4
            for b in range(nblk):
                load_own(i, b, halo)
            norm_tile([(h[:, b, :], [("h", b)]) for b in range(nblk)], 128, 3, hnT, "hnT")
            def ev_qv(b, q, pap, pk):
                ACT(vraw[:, b, q * 256:(q + 1) * 256], pap, AF.Copy, [pk], vrk(b))
            proj_tok("wq", 4, 128, nblk, ev_qv, [])
            gi_ = NG if halo else i // GQ
            for b in range(nblk):
                ob = i * 4 + b
                qk_norm_aug(128, b, gq_bc, src=vraw[:, b, :], skeys=vrk(b))
                TT("dve", lfsb[:, :], cko[:, ob, :], Cb_all[:, gi_, :], ALU.subtract, [("cko", ob), ("Cb", gi_)], ["lfsb"])
                aug_hilo(128, lfsb[:, :], "lfsb")
                aug_transposes(128, ktT[:, :, b * 128:(b + 1) * 128], ("ktT", b))
            P.dma("pool", QTs.rearrange("h r s -> r h s")[:, :, i * 512:i * 512 + nblk * 128], ktT[:, :, 0:nblk * 128],
                  r=[("ktT", b) for b in range(nblk)], w=[("QTs", i)])

        def cko_prepass():
            for ob in range(NBH):
                TS("dve", cko[:, ob, :], ck_all[:, ob, :], cc[:, 0:1], None, ALU.mult, None, [("ck", ob), "cc"], [("cko", ob)])
                STT("dve", cko[:, ob, :], ck_all[:, NBH + ob, :], cc[:, 1:2], cko[:, ob, :], ALU.mult, ALU.add,
                    [("ck", NBH + ob), ("cko", ob), "cc"], [("cko", ob)])
            CP("dve", cko[:, NBH, :], ck_all[:, NBH - 1, :], [("ck", NBH - 1)], [("cko", NBH)])
            for g in range(NG + 1):
                lastb = NBH if g == NG else 4 * GQ * (g + 1) - 1
                MM(pb[5][:, 0:NH], sel127[:, :], cko[:, lastb, :], True, True, ["sel127", ("cko", lastb)], [pbk[5]])
                ACT(Cb_all[:, g, :], pb[5][:, 0:NH], AF.Copy, [pbk[5]], [("Cb", g)])

        pair_idx = {}

        def build_bias():
            n = 0
            for g in range(NG + 1):
                halo = (g == NG)
                Cb = Cb_all[:, g, :]
                STT("dve", bias_all[:, n:n + NBH, :], ck_all[:, 0:NBH, :], -1.0, Cb.unsqueeze(1).to_broadcast([128, NBH, NH]),
                    ALU.mult, ALU.add, [("ck", j) for j in range(NBH)] + [("Cb", g)], ["bias_all"])
                for j in range(NBH):
                    pair_idx[(g, 0, j)] = n + j
                if not halo:
                    lo = 4 * GQ * (g + 1)
                    if lo < NBH:
                        TS("dve", bias_all[:, n + lo:n + NBH, :], bias_all[:, n + lo:n + NBH, :], cc[:, 2:3], None, ALU.add, None,
                           ["bias_all", "cc"], ["bias_all"])
                n += NBH
                if not halo:
                    ns = 4 * GQ * (g + 1)
                    STT("dve", bias_all[:, n:n + ns, :], ck_all[:, NBH:NBH + ns, :], -1.0,
                        Cb.unsqueeze(1).to_broadcast([128, ns, NH]), ALU.mult, ALU.add,
                        [("ck", NBH + j) for j in range(ns)] + [("Cb", g)], ["bias_all"])
                    TS("dve", bias_all[:, n:n + ns, :], bias_all[:, n:n + ns, :], cc[:, 2:3], None, ALU.add, None,
                       ["bias_all", "cc"], ["bias_all"])
                    for j in range(ns):
                        pair_idx[(g, 1, j)] = n + j
                    n += ns
            assert n == NPAIR

        KT_h = ktT[:, :, :].rearrange("p h t -> p (h t)")
        VX_h = vxt[:, :, :, :].rearrange("p h n e -> p (h n) e")
        QT_h = hidT_flat[0:66, 0:NOWN]
        AT_h = hidT_flat[0:128, 4352:4352 + NOWN]
        att_tmp = hsb[0]
        rr = cbuf[1]
        bcs = cbuf[0]

        LOOK = 2
        FINLAG = 3
        fin_pend = []
        pend = []
        acnt = [0]
        pT4 = hnT[:, :, :].rearrange("p a t -> p (a t)")
        smb = [vtmp, junk]

        def flush(upto):
            while len(pend) > upto:
                pend.pop(0)()

        def l1b_head(hh):
            P.dma("sp", KT_h[:, 0:SEQ], KTs[hh], r=[("KTs", t) for t in range(2 * NTH)], w=["KT_h"])
            P.dma("sp", VX_h[:, 0:NB, :], VXs[hh], r=[("VXs", t) for t in range(2 * NTH)], w=["VX_h"])
            P.dma("sp", QT_h, QTs[hh], r=[("QTs", i) for i in range(NTH + 1)], w=["QT_h"])
            for g in range(NG + 1):
                halo = (g == NG)
                if halo:
                    nq, q0, ntile, tw = 128, NTH * 512, 1, 128
                    klist = [(0, j) for j in range(NBH)]
                else:
                    nq, q0, ntile, tw = GQ * 512, g * GQ * 512, GQ, 512
                    klist = [(1, j) for j in range(4 * GQ * (g + 1))] + [(0, j) for j in range(NBH)]
                pai = [4 + ((g * GQ + t) % 2) for t in range(ntile)]
                for n, (half, j) in enumerate(klist):
                    kb = half * NBH + j
                    cnt = acnt[0]
                    acnt[0] += 1
                    Sb = (psA, psB, psC)[cnt % 3]
                    Sk = ([pbk[0], pbk[1]], [pbk[2], pbk[3]], [ptk[0], ptk[1]])[cnt % 3]
                    slot = cnt % 4
                    pT = pT4[:, slot * 1024:(slot + 1) * 1024]
                    pTk = ("pT", slot)
                    sm = smb[cnt % 2]
                    smk = ("smb", cnt % 2)
                    bcol = bias_all[:, pair_idx[(g, half, j)], hh:hh + 1]
                    if halo:
                        diag, jp = (j == NBH - 1), 0
                    else:
                        diag = (4 * GQ * g <= j < 4 * GQ * (g + 1))
                        jp = j - 4 * GQ * g
                    c0 = 128 * jp if (diag and half == 1) else 0
                    for t in range(ntile):
                        lo = max(c0 - t * 512, 0)
                        if lo >= tw:
                            continue
                        MM(Sb[:, t * 512 + lo:t * 512 + tw], KT_h[:, kb * 128:(kb + 1) * 128],
                           QT_h[:, q0 + t * 512 + lo:q0 + t * 512 + tw], True, True, ["KT_h", "QT_h"], Sk)
                    if not diag:
                        ACT(pT[:, 0:nq], Sb[:, 0:nq], AF.Exp, Sk + ["bias_all"], [pTk], bias=bcol)
                    elif half == 0 and not halo:
                        kt, jj = jp // 4, jp % 4
                        for t in range(ntile):
                            cs = slice(t * 512, (t + 1) * 512)
                            if t < kt:
                                TT("dve", sm[:, cs], Sb[:, cs], Af[:, 4, :], ALU.add, Sk + ["Af"], [smk])
                            elif t == kt:
                                TT("dve", sm[:, cs], Sb[:, cs], Af[:, jj, :], ALU.add, Sk + ["Af"], [smk])
                            else:
                                CP("dve", sm[:, cs], Sb[:, cs], Sk, [smk])
                        ACT(pT[:, 0:nq], sm[:, 0:nq], AF.Exp, [smk, "bias_all"], [pTk], bias=bcol)
                    else:
                        TT("dve", sm[:, c0:c0 + 128], Sb[:, c0:c0 + 128], Atri[:, :], ALU.add, Sk + ["Atri"], [smk])
                        ACT(pT[:, c0:c0 + 128], sm[:, c0:c0 + 128], AF.Exp, [smk, "bias_all"], [pTk], bias=bcol)
                        if c0 + 128 < nq:
                            ACT(pT[:, c0 + 128:nq], Sb[:, c0 + 128:nq], AF.Exp, Sk + ["bias_all", pTk], [pTk], bias=bcol)

                    def pv(c0=c0, kb=kb, pT=pT, pTk=pTk, first=(n == 0), last=(n == len(klist) - 1), ntile=ntile, tw=tw,
                           pai=pai):
                        for t in range(ntile):
                            lo = max(c0 - t * 512, 0)
                            if lo >= tw:
                                continue
                            MM(pb[pai[t]][0:65, lo:tw], VX_h[:, kb, :], pT[:, t * 512 + lo:t * 512 + tw], first, last,
                               ["VX_h", pTk], [pbk[pai[t]]])
                    pend.append(pv)
                    flush(LOOK)
                    for fp in list(fin_pend):
                        fp[0] -= 1
                        if fp[0] <= 0:
                            fin_pend.remove(fp)
                            fp[1]()

                bsel = (acnt[0] + 2) % 3
                bcp = (psA, psB, psC)[bsel][:, 0:512]
                bck = (pbk[0], pbk[2], ptk[0])[bsel]
                for t in range(ntile):
                    def fin(pacc=pb[pai[t]], pacc_i=pai[t], nq=tw, q0=q0 + t * 512, bcp=bcp, bck=bck):
                        RCP(rr[64:65, 0:nq], pacc[64:65, 0:nq], [pbk[pacc_i]], ["rr"])
                        MM(bcp[0:64, 0:nq], ones_f[64:65, 0:64], rr[64:65, 0:nq], True, True, ["ones_f", "rr"], [bck])
                        CP("dve", bcs[0:64, 0:nq], bcp[0:64, 0:nq], [bck], ["bcs"])
                        if hh % 2 == 0:
                            TT("dve", AT_h[0:64, q0:q0 + nq], pacc[0:64, 0:nq], bcs[0:64, 0:nq], ALU.mult, [pbk[pacc_i], "bcs"],
                               ["AT_h"])
                        else:
                            TT("dve", att_tmp[0:64, 0:nq], pacc[0:64, 0:nq], bcs[0:64, 0:nq], ALU.mult, [pbk[pacc_i], "bcs"],
                               ["att_tmp"])
                            MM(bcp[:, 0:nq], Shiftm[0:64, :], att_tmp[0:64, 0:nq], True, True, ["Shiftm", "att_tmp"], [bck])
                            CP("dve", AT_h[64:128, q0:q0 + nq], bcp[64:128, 0:nq], [bck], ["AT_h"])
                    if GQ > 1 or halo:
                        pend.append(fin)
                    else:
                        fin_pend.append([FINLAG, fin])
            flush(0)
            for fp in list(fin_pend):
                fin_pend.remove(fp)
                fp[1]()
            if hh % 2 == 1:
                P.dma("pool", ATs[hh // 2], AT_h, r=["AT_h"], w=[("ATs", hh // 2)])

        attT = uT

        def oproj(rows, nblk, att_ap):
            for q in range(4):
                wt, wkk = wpiece(("wo", q))
                wv_ = wt[:, :].rearrange("p (k n) -> p k n", k=8)
                for b in range(nblk):
                    bank = (q * nblk + b) % 4
                    for k in range(8):
                        MM(pb[bank][:rows, 0:256], att_ap[:, k, b * rows:(b + 1) * rows], wv_[:, k, :], k == 0, k == 7,
                           [wkk, "attT"], [pbk[bank]])
                    TT("dve", h[:rows, b, q * 256:(q + 1) * 256], h[:rows, b, q * 256:(q + 1) * 256], pb[bank][:rows, 0:256],
                       ALU.add, [pbk[bank], ("h", b)], [("h", b)])

        def l1c_tile(i):
            halo = (i == NTH)
            nblk = 1 if halo else 4
            for b in range(nblk):
                load_own(i, b, halo)
            P.dma("sp", attT[:, :, 0:nblk * 128], ATs.rearrange("k p s -> p k s")[:, :, i * 512:i * 512 + nblk * 128],
                  r=[("ATs", k) for k in range(NH // 2)], w=["attT"])
            oproj(128, nblk, attT)
            norm_tile([(h[:, b, :], [("h", b)]) for b in range(nblk)], 128, 4, hnT, "hnT")
            ffn(1, 128, nblk, 1, False, halo=halo)
            if halo:
                ck1 = [("carry", 1, c) for c in range(44)]
                TS("dve", carry[1][:, :, 0, :], carry[1][:, :, 0, :], cc[:, 3:4], None, ALU.mult, None, ck1 + ["cc"], ck1)
            else:
                for b in range(nblk):
                    r0 = i * 512 + b * 128
                    P.dma("pool", y_p[r0:r0 + 128, :], h[:, b, :], r=[("h", b)])

        def l1a_sample():
            P.dma("sp", h[:64, 0, :], h1ss, r=["h1ss"], w=[("h", 0)])
            norm_T(h[:64, 0, :], 64, 3, hnT, 0, [("h", 0)], "hnT")
            proj_tok("wq", 4, 64, 1, ev_q(64), [])
            qk_norm_aug(64, 0, gq_bc)
            G = min(4, NPB)
            assert NPB % G == 0
            nl = G * NH
            bias_s_flat = lnb_bc
            for s_ in range(2):
                P.dma("sp", bias_s[:, s_ * NPB:(s_ + 1) * NPB, :], cache_lf[s_].rearrange("(j p) h -> p j h", p=128),
                      w=[("bias_s", s_)], slow=True)
                MS("dve", run[:], 0.0, ["run"])
                for g in range(NPB // G):
                    base = (s_ * NPB + g * G) * NH
                    lf2 = bias_s_flat[:, base:base + nl]
                    MM(pb[5][0:1, 0:nl], ones_f[:, 0:1], lf2, True, True, ["ones_f", ("bias_s", s_)], [pbk[5]])
                    ACT(tot4[0:1, 0:nl], pb[5][0:1, 0:nl], AF.Copy, [pbk[5]], ["tot4"])
                    CP("dve", Rrow[0:1, 0, :], run[0:1, :], ["run"], ["Rrow"])
                    for b in range(G):
                        TT("dve", Rrow[0:1, b + 1, :], Rrow[0:1, b, :], tot4[0:1, b * NH:(b + 1) * NH], ALU.add,
                           ["Rrow", "tot4"], ["Rrow"])
                    for b in range(G):
                        MM(pb[5][:, 64 + b * NH:64 + (b + 1) * NH], Utri[:, :], bias_s_flat[:, base + b * NH:base + (b + 1) * NH],
                           True, False, ["Utri", ("bias_s", s_)], [pbk[5]])
                        MM(pb[5][:, 64 + b * NH:64 + (b + 1) * NH], ones_f[0:1, :], Rrow[0:1, b, :], False, True,
                           ["ones_f", "Rrow"], [pbk[5]])
                    gi0 = s_ * NPB + g * G
                    ACT(cks_all[:, gi0:gi0 + G, :], pb[5][:, 64:64 + nl].rearrange("p (b h) -> p b h", h=NH), AF.Copy,
                        [pbk[5]], [("cks", gi0 + b) for b in range(G)])
                    CP("dve", run[0:1, :], Rrow[0:1, G, :], ["Rrow"], ["run"])
                ACT(runs[0:1, s_, :], run[0:1, :], AF.Copy, ["run"], [("runs", s_)])
            rk = [("runs", 0), ("runs", 1)]
            MM(pb[5][0:64, 128:128 + NH], UtriS[:, :], lfn[:, :], True, False, ["UtriS", "lfn"], [pbk[5]])
            MM(pb[5][0:64, 128:128 + NH], onesAB[0:1, 0, :], runs[0:1, 0, :], False, False, ["onesAB"] + rk, [pbk[5]])
            MM(pb[5][0:64, 128:128 + NH], onesAB[0:1, 1, :], runs[0:1, 1, :], False, True, ["onesAB"] + rk, [pbk[5]])
            ACT(ckn[:, :], pb[5][0:64, 128:128 + NH], AF.Copy, [pbk[5]], ["ckn"])
            for s_ in range(2):
                MM(pb[5][0:1, 64:64 + NH], colAB[:, s_:s_ + 1], lfn[:, :], True, False, ["colAB", "lfn"], [pbk[5]])
                MM(pb[5][0:1, 64:64 + NH], ones_f[0:1, 0:1], runs[0:1, s_, :], False, True, ["ones_f"] + rk, [pbk[5]])
                ACT(runf[0:1, s_, :], pb[5][0:1, 64:64 + NH], AF.Copy, [pbk[5]], [("runf", s_)])
                MM(pb[5][:, 192:192 + NH], ones_f[0:1, :], runf[0:1, s_, :], True, True, ["ones_f", ("runf", s_)], [pbk[5]])
                ACT(Cb128s[:, s_, :], pb[5][:, 192:192 + NH], AF.Copy, [pbk[5]], [("Cb128s", s_)])
                STT("dve", bias_s[:, s_ * NPB:(s_ + 1) * NPB, :], cks_all[:, s_ * NPB:(s_ + 1) * NPB, :], -1.0,
                    Cb128s[:, s_, :].unsqueeze(1).to_broadcast([128, NPB, NH]), ALU.mult, ALU.add,
                    [("cks", s_ * NPB + jb) for jb in range(NPB)] + [("Cb128s", s_)], [("bias_s", s_)])
            rf = [("runf", 0), ("runf", 1)]
            MM(pb[5][0:64, 256:256 + NH], onesAB[0:1, 0, :], runf[0:1, 0, :], True, False, ["onesAB"] + rf, [pbk[5]])
            MM(pb[5][0:64, 256:256 + NH], onesAB[0:1, 1, :], runf[0:1, 1, :], False, True, ["onesAB"] + rf, [pbk[5]])
            TT("dve", lft[:64, :], ckn[:64, :], pb[5][0:64, 256:256 + NH], ALU.subtract, ["ckn", pbk[5]], ["lft"])
            aug_hilo(64, lft[:64, :], "lft")
            aug_transposes(64, qts[:, :, :], "qts")
            TS("dve", ckn[:64, :], lft[:64, :], -1.0, None, ALU.mult, None, ["lft", "ckn"], ["bnew"])

        def l1b_sample():
            pacc = pb[4]
            sm = cbuf[2]
            MS("dve", kaug[:, :, 64:66], 1.0, ["kaug"])
            for s_ in range(2):
                q0, q1 = s_ * 32, (s_ + 1) * 32
                for jb in range(NPB):
                    b2 = jb % 2
                    P.dma("sp", h[:, b2, :], cache_k[s_, jb * 128:(jb + 1) * 128, :], w=[("h", b2)])
                    P.dma("sp", vbuf[:, b2, :], cache_v[s_, jb * 128:(jb + 1) * 128, :], w=[("vbuf", b2)])
                    ACT(kaug[:, :, 0:HD], h[:, b2, :].rearrange("p (h d) -> p h d", d=HD), AF.Copy, [("h", b2), "kaug"], ["kaug"])
                    aug_transposes(128, ktT[:, :, 0:128], ("ktT", 0))
                    CP("dve", vxt[:, :, 0, 0:HD], vbuf[:, b2, :].rearrange("p (h d) -> p h d", d=HD), [("vbuf", b2), "vxt"], ["vxt"])
                    sbank = jb % 2
                    for hh in range(NH):
                        MM(pb[sbank][:, hh * 32:(hh + 1) * 32], ktT[:, hh, 0:128], qts[:, hh, q0:q1], True, True,
                           [("ktT", 0), "qts"], [pbk[sbank]])
                    TT("dve", sm[:, :].rearrange("p (h q) -> p h q", h=NH), pb[sbank][:, :].rearrange("p (h q) -> p h q", h=NH),
                       bias_s[:, s_ * NPB + jb, :].unsqueeze(2).to_broadcast([128, NH, 32]), ALU.add,
                       [pbk[sbank], ("bias_s", s_)], ["sm"])
                    slot = jb % 4
                    pT = hnT[:, slot, :]
                    ACT(pT[:, :], sm[:, :], AF.Exp, ["sm"], [("pT", slot)])
                    for hh in range(NH):
                        MM(pacc[0:65, hh * 32:(hh + 1) * 32], vxt[:, hh, 0, :], pT[:, hh * 32:(hh + 1) * 32], jb == 0, False,
                           ["vxt", ("pT", slot)], [pbk[4]])
                for hh in range(NH):
                    MM(pb[2][q0:q1, hh * 32:(hh + 1) * 32], ktn[:, hh, q0:q1], qts[:, hh, q0:q1], True, True, ["ktn", "qts"], [pbk[2]])
                smv = sm[q0:q1, :].rearrange("p (h q) -> p h q", h=NH)
                TT("dve", smv, pb[2][q0:q1, :].rearrange("p (h q) -> p h q", h=NH),
                   ckn[q0:q1, :].unsqueeze(2).to_broadcast([32, NH, 32]), ALU.add, [pbk[2], "bnew"], ["sm"])
                TT("dve", smv, smv, As64[q0:q1, :, :], ALU.add, ["sm", "As64"], ["sm"])
                pT = hnT[:, 4 + s_, :]
                ACT(pT[q0:q1, :], sm[q0:q1, :], AF.Exp, ["sm"], [("pT", 4 + s_)])
                for hh in range(NH):
                    MM(pacc[0:65, hh * 32:(hh + 1) * 32], vxn[q0:q1, hh, :], pT[q0:q1, hh * 32:(hh + 1) * 32], False, True,
                       ["vxn", ("pT", 4 + s_)], [pbk[4]])
                RCP(rr[64:65, :], pacc[64:65, :], [pbk[4]], ["rr"])
                MM(pb[5][0:64, :], ones_f[64:65, 0:64], rr[64:65, :], True, True, ["ones_f", "rr"], [pbk[5]])
                ACT(bcs[0:64, :], pb[5][0:64, :], AF.Copy, [pbk[5]], ["bcs"])
                TT("dve", atts[:, :, q0:q1], pacc[0:64, :].rearrange("p (h q) -> p h q", h=NH),
                   bcs[0:64, :].rearrange("p (h q) -> p h q", h=NH), ALU.mult, [pbk[4], "bcs"], [("atts", s_)])
            a4 = atts[:, :, :].rearrange("p (k two) q -> p k two q", two=2)
            CP("dve", atts_p[0:64, :, :], a4[:, :, 0, :], [("atts", 0), ("atts", 1)], ["atts_p"])
            for k in range(NH // 2):
                MM(pb[5][:, k * 64:(k + 1) * 64], Shiftm[0:64, :], atts[0:64, 2 * k + 1, :], True, True,
                   ["Shiftm", ("atts", 0), ("atts", 1)], [pbk[5]])
            CP("dve", atts_p[64:128, :, :], pb[5][64:128, :].rearrange("p (k q) -> p k q", k=NH // 2), [pbk[5], "atts_p"],
               ["atts_p"])

        def l1c_sample():
            P.dma("sp", h[:64, 0, :], h1ss, r=["h1ss"], w=[("h", 0)])
            for s_ in range(2):
                for rr_ in range(2):
                    P.dma("sp", carry[1][:, :, s_, rr_], cache_conv[1, s_, rr_].rearrange("(c p) -> p c", p=128),
                          w=[("carry", 1, c) for c in range(44)], slow=True)
            for q in range(4):
                wt, wkk = wpiece(("wo", q))
                wv_ = wt[:, :].rearrange("p (k n) -> p k n", k=8)
                for k in range(8):
                    MM(pb[q][:64, 0:256], atts_p[:, k, :], wv_[:, k, :], k == 0, k == 7, [wkk, "atts_p"], [pbk[q]])
                TT("dve", h[:64, 0, q * 256:(q + 1) * 256], h[:64, 0, q * 256:(q + 1) * 256], pb[q][:64, 0:256], ALU.add,
                   [pbk[q], ("h", 0)], [("h", 0)])
            norm_T(h[:64, 0, :], 64, 4, hnT, 0, [("h", 0)], "hnT")
            ffn(1, 64, 1, 2, False)
            P.dma("pool", y_s, h[:64, 0, :], r=[("h", 0)])
            for s_ in range(2):
                for rr_ in range(2):
                    P.dma("pool", conv_s[1, s_, rr_].rearrange("(c p) -> p c", p=128), carry[1][:, :, s_, rr_],
                          r=[("carry", 1, c) for c in range(44)], slow=True)

        for l in range(2):
            MS("dve", carry[l][:], 0.0, [("carry", l, c) for c in range(44)])
        for t in range(2 * NTH):
            l0_tile(xp[t * 512:(t + 1) * 512, :], 128, 4, 1, t, False)
        for rr_ in range(2):
            P.dma("pool", conv_p[0, rr_].rearrange("(c p) -> p c", p=128), carry[0][:, :, 0, rr_],
                  r=[("carry", 0, c) for c in range(44)], slow=True)
        for s in range(2):
            for rr_ in range(2):
                P.dma("sp", carry[0][:, :, s, rr_], cache_conv[0, s, rr_].rearrange("(c p) -> p c", p=128),
                      w=[("carry", 0, c) for c in range(44)], slow=True)
        l0_tile(xs, 64, 1, 2, 0, True)
        CP("dve", ktn[:, :, :], ktT[:, :, 0:64], [("ktT", 0)], ["ktn"])
        CP("dve", vxn[:, :, :], vxt[:64, :, 0, :], ["vxt"], ["vxn"])
        P.barrier()
        for s in range(2):
            for rr_ in range(2):
                P.dma("pool", conv_s[0, s, rr_].rearrange("(c p) -> p c", p=128), carry[0][:, :, s, rr_],
                      r=[("carry", 0, c) for c in range(44)], slow=True)
        cko_prepass()
        for i in range(NTH + 1):
            l1a_tile(i)
        if "s1" not in skip:
            l1a_sample()
        P.barrier()
        build_bias()
        for hh in range(NH):
            l1b_head(hh)
        P.barrier()
        if "s1" not in skip:
            l1b_sample()
        P.barrier()
        l1c_tile(NTH)
        for i in range(NTH):
            l1c_tile(i)
        for rr_ in range(2):
            P.dma("pool", conv_p[1, rr_].rearrange("(c p) -> p c", p=128), carry[1][:, :, 0, rr_],
                  r=[("carry", 1, c) for c in range(44)], slow=True)
        if "s1" not in skip:
            l1c_sample()

        P.emit(nc, st)
    return nc


WNAMES = ['norm_mix', 'norm_ffn', 'a_w_in', 'a_ln_g', 'a_ln_b', 'a_w_s', 'a_b_s', 'a_w_out', 'f_w_up', 'f_conv_w',
          'f_conv_b', 'f_w_down', 'kv_norm', 'w_k', 'w_v', 'k_norm_g', 'w_f', 'b_f', 'b_w_q', 'q_norm_g', 'b_w_o']


def make_in_maps(inp, NTH=8, PAST=4096):
    SEQ = 2 * NTH * 512
    f32 = lambda a: np.ascontiguousarray(np.asarray(a, dtype=np.float32))
    wts = {k: f32(inp[k]) for k in WNAMES}
    maps = []
    for c in range(8):
        b, r = c // 2, c % 2
        m = dict(wts)
        m["xp"] = f32(inp["x_prompt"][b, :SEQ])
        m["xs"] = f32(inp["x_sample"][2 * c:2 * c + 2]).reshape(64, D)
        m["cache_k"] = f32(inp["cache_k"][2 * c:2 * c + 2, :PAST]).reshape(2, PAST, D)
        m["cache_v"] = f32(inp["cache_v"][2 * c:2 * c + 2, :PAST]).reshape(2, PAST, D)
        m["cache_lf"] = f32(inp["cache_logf"][2 * c:2 * c + 2, :PAST])
        m["cache_conv"] = f32(inp["cache_ffn_conv"][:, 2 * c:2 * c + 2])
        ccv = np.zeros((128, 4), np.float32)
        ccv[:, 0] = 1.0 - r
        ccv[:, 1] = float(r)
        ccv[:, 2] = NEG * (1.0 - r)
        ccv[:, 3] = float(r)
        m["cc"] = ccv
        maps.append(m)
    return maps


_NC_CACHE = {}


def kernel(**inputs):
    if "nc" not in _NC_CACHE:
        _NC_CACHE["nc"] = build()
    nc = _NC_CACHE["nc"]
    in_maps = make_in_maps(inputs)
    res = run_bass_kernel_spmd(nc, in_maps, core_ids=list(range(8))).results
    B, S, H = 4, 8192, 4096
    y_prompt = np.zeros((B, S, D), np.float32)
    conv_p = np.zeros((2, B, 2, 2 * DFF), np.float32)
    k_p = np.zeros((B, S, D), np.float32)
    v_p = np.zeros((B, S, D), np.float32)
    lf_p = np.zeros((B, S, NH), np.float32)
    y_s = np.zeros((16, 32, D), np.float32)
    sgu = np.zeros((1, 16, 32, D), np.float32)
    conv_s = np.zeros((2, 16, 2, 2 * DFF), np.float32)
    k_s = np.zeros((16, 32, D), np.float32)
    v_s = np.zeros((16, 32, D), np.float32)
    lf_s = np.zeros((16, 32, NH), np.float32)
    for c in range(8):
        b, r = c // 2, c % 2
        o = res[c]
        y_prompt[b, r * H:(r + 1) * H] = o["y_p"]
        if r == 1:
            conv_p[0, b] = o["conv_p"][0]
            conv_p[1, b] = o["conv_p"][1]
            k_p[b] = o["k_p"]
            v_p[b] = o["v_p"]
            lf_p[b] = o["lf_p"]
        y_s[2 * c:2 * c + 2] = o["y_s"].reshape(2, 32, D)
        sgu[0, 2 * c:2 * c + 2] = o["sguv_s"].reshape(2, 32, D)
        conv_s[:, 2 * c:2 * c + 2] = o["conv_s"]
        k_s[2 * c:2 * c + 2] = o["k_s"].reshape(2, 32, D)
        v_s[2 * c:2 * c + 2] = o["v_s"].reshape(2, 32, D)
        lf_s[2 * c:2 * c + 2] = o["lf_s"].reshape(2, 32, NH)
    return (y_prompt, y_s, sgu, conv_p, conv_s,
            k_p.reshape(B, S, NH, HD), v_p.reshape(B, S, NH, HD), lf_p,
            k_s.reshape(16, 32, NH, HD), v_s.reshape(16, 32, NH, HD), lf_s)
```
